# Optimizing a Trainium2 kernel written in Bass

```python
import jax, jax.numpy as jnp
from jax import lax
import numpy as np

D_MODEL = 1024
BATCH = 8
SEQ = 8192
DEPTH = 1

N_META = 16
Q_BLOCK = 128
ATT_HEADS = 8
ATT_KV_HEADS = 2
HEAD_DIM = 64
ATT_W = ATT_HEADS * HEAD_DIM
KV_W = ATT_KV_HEADS * HEAD_DIM
IDX_HEADS = 8
IDX_DIM = 64
TOPK_MAX = 256
ROPE_THETA = 10000.0
CONV_W = D_MODEL // 2
CONV_K = 3
D_FF = 256 * ((8 * D_MODEL // 3 + 255) // 256)
LN_EPS = 1e-5
DEEPNORM_ALPHA = (2.0 * DEPTH) ** 0.25
DEEPNORM_BETA = (8.0 * DEPTH) ** -0.25
PROJ_WIDTHS = (ATT_W, KV_W, KV_W, IDX_HEADS * IDX_DIM, IDX_DIM, IDX_HEADS,
               CONV_W, CONV_W, CONV_W, D_MODEL, D_MODEL)
PROJ_W = sum(PROJ_WIDTHS)

kernel_name = "hybrid_dsa_shortconv_macaron_deepnorm"


def layer_norm(x, g, b):
    xf = x.astype(jnp.float32)
    mu = jnp.mean(xf, axis=-1, keepdims=True)
    xc = xf - mu
    var = jnp.mean(xc * xc, axis=-1, keepdims=True)
    y = xc * lax.rsqrt(var + LN_EPS) * g.astype(jnp.float32) + b.astype(jnp.float32)
    return y.astype(x.dtype)


def swiglu(x, w_gate, w_up, w_down):
    return (jax.nn.silu(x @ w_gate) * (x @ w_up)) @ w_down


def rope(x, pos):
    half = x.shape[-1] // 2
    inv_freq = ROPE_THETA ** (-jnp.arange(half, dtype=jnp.float32) / half)
    ang = pos.astype(jnp.float32)[:, None] * inv_freq[None, :]
    cos = jnp.cos(ang)[:, None, :]
    sin = jnp.sin(ang)[:, None, :]
    x1 = x[..., :half].astype(jnp.float32)
    x2 = x[..., half:].astype(jnp.float32)
    return jnp.concatenate([x1 * cos - x2 * sin, x2 * cos + x1 * sin], axis=-1).astype(x.dtype)


def dsa_attend(qi, wi, q, qpos, ki, k, v, topk):
    b_, t_ = q.shape[0], q.shape[1]
    kpos = jnp.arange(ki.shape[1], dtype=jnp.int32)
    s = jax.nn.relu(jnp.einsum('bthd,bsd->bths', qi, ki))
    score = jnp.einsum('bths,bth->bts', s, wi).astype(jnp.float32)
    causal = kpos[None, :] <= qpos[:, None]
    score = jnp.where(causal[None], score, -jnp.inf)
    _, idx = lax.top_k(score, topk)
    k_sel = jax.vmap(lambda kb, ib: kb[ib])(k, idx)
    v_sel = jax.vmap(lambda vb, ib: vb[ib])(v, idx)
    valid = idx <= qpos[None, :, None]
    qg = q.reshape(b_, t_, ATT_KV_HEADS, ATT_HEADS // ATT_KV_HEADS, HEAD_DIM)
    logits = jnp.einsum('btgrd,btngd->btgrn', qg, k_sel).astype(jnp.float32) * (HEAD_DIM ** -0.5)
    logits = jnp.where(valid[:, :, None, None, :], logits, -jnp.inf)
    p = jax.nn.softmax(logits, axis=-1).astype(v.dtype)
    o = jnp.einsum('btgrn,btngd->btgrd', p, v_sel)
    return o.reshape(b_, t_, ATT_W)


def short_conv(u, w):
    return lax.conv_general_dilated(
        u, w[:, None, :].astype(u.dtype), window_strides=(1,), padding=[(CONV_K - 1, 0)],
        dimension_numbers=('NWC', 'WIO', 'NWC'), feature_group_count=u.shape[-1])


def hybrid_mixer(h, w_in, conv_w, w_att_out, w_conv_out, w_o, pos, topk):
    b_, l_, _ = h.shape
    offs = np.cumsum(PROJ_WIDTHS)[:-1].tolist()
    proj = h @ w_in
    q, k, v, qi, ki, wi, u, gate_b, gate_c, g_att, g_conv = jnp.split(proj, offs, axis=-1)
    q = rope(q.reshape(b_, l_, ATT_HEADS, HEAD_DIM), pos)
    k = rope(k.reshape(b_, l_, ATT_KV_HEADS, HEAD_DIM), pos)
    v = v.reshape(b_, l_, ATT_KV_HEADS, HEAD_DIM)
    qi = rope(qi.reshape(b_, l_, IDX_HEADS, IDX_DIM), pos)
    ki = rope(ki[:, :, None, :], pos)[:, :, 0, :]
    wi = wi * ((IDX_HEADS * IDX_DIM) ** -0.5)

    att_meta = dsa_attend(qi[:, :N_META], wi[:, :N_META], q[:, :N_META], pos[:N_META], ki, k, v, topk)
    n_blk = (l_ - N_META) // Q_BLOCK

    def to_blocks(a):
        a = a[:, N_META:]
        return jnp.moveaxis(a.reshape((b_, n_blk, Q_BLOCK) + a.shape[2:]), 1, 0)

    qpos_blocks = pos[N_META:].reshape(n_blk, Q_BLOCK)
    att_real = lax.map(lambda args: dsa_attend(args[0], args[1], args[2], args[3], ki, k, v, topk),
                       (to_blocks(qi), to_blocks(wi), to_blocks(q), qpos_blocks))
    att_real = jnp.moveaxis(att_real, 0, 1).reshape(b_, l_ - N_META, ATT_W)
    att = jnp.concatenate([att_meta, att_real], axis=1)
    y_att = att @ w_att_out

    y_conv = (gate_b * short_conv(gate_c * u, conv_w)) @ w_conv_out

    merged = jax.nn.sigmoid(g_att) * y_att + jax.nn.sigmoid(g_conv) * y_conv
    return merged @ w_o


def setup_inputs(seed: int = 0) -> dict:
    key = jax.random.key(seed)
    ks = jax.random.split(key, 20)
    f32 = jnp.float32

    def nrm(k, shape, fan_in, scale=1.0):
        return jax.random.normal(k, shape, f32) * (scale * fan_in ** -0.5)

    def gain(k):
        return 1.0 + 0.02 * jax.random.normal(k, (DEPTH, D_MODEL), f32)

    def bias(k):
        return 0.02 * jax.random.normal(k, (DEPTH, D_MODEL), f32)

    beta = DEEPNORM_BETA
    return {
        "x": jax.random.normal(ks[0], (BATCH, SEQ, D_MODEL), f32),
        "meta_tokens": jax.random.normal(ks[1], (N_META, D_MODEL), f32),
        "ffn1_w_gate": nrm(ks[2], (DEPTH, D_MODEL, D_FF), D_MODEL),
        "ffn1_w_up": nrm(ks[3], (DEPTH, D_MODEL, D_FF), D_MODEL),
        "ffn1_w_down": nrm(ks[4], (DEPTH, D_FF, D_MODEL), D_FF, beta),
        "ln1_g": gain(ks[5]),
        "ln1_b": bias(ks[6]),
        "w_in": nrm(ks[7], (DEPTH, D_MODEL, PROJ_W), D_MODEL),
        "conv_w": nrm(ks[8], (DEPTH, CONV_K, CONV_W), CONV_K),
        "w_att_out": nrm(ks[9], (DEPTH, ATT_W, D_MODEL), ATT_W, beta),
        "w_conv_out": nrm(ks[10], (DEPTH, CONV_W, D_MODEL), CONV_W, beta),
        "w_o": nrm(ks[11], (DEPTH, D_MODEL, D_MODEL), D_MODEL, beta),
        "ln2_g": gain(ks[12]),
        "ln2_b": bias(ks[13]),
        "ffn2_w_gate": nrm(ks[14], (DEPTH, D_MODEL, D_FF), D_MODEL),
        "ffn2_w_up": nrm(ks[15], (DEPTH, D_MODEL, D_FF), D_MODEL),
        "ffn2_w_down": nrm(ks[16], (DEPTH, D_FF, D_MODEL), D_FF, beta),
        "ln3_g": gain(ks[17]),
        "ln3_b": bias(ks[18]),
    }


def reference(x, meta_tokens, ffn1_w_gate, ffn1_w_up, ffn1_w_down, ln1_g, ln1_b, w_in, conv_w,
              w_att_out, w_conv_out, w_o, ln2_g, ln2_b, ffn2_w_gate, ffn2_w_up, ffn2_w_down,
              ln3_g, ln3_b):
    b_ = x.shape[0]
    meta = jnp.broadcast_to(meta_tokens[None].astype(x.dtype), (b_, N_META, D_MODEL))
    h = jnp.concatenate([meta, x], axis=1)
    l_ = h.shape[1]
    pos = jnp.arange(l_, dtype=jnp.int32)
    topk = min(TOPK_MAX, l_ // 4)
    for l in range(DEPTH):
        h = layer_norm(DEEPNORM_ALPHA * h + 0.5 * swiglu(h, ffn1_w_gate[l], ffn1_w_up[l], ffn1_w_down[l]),
                       ln1_g[l], ln1_b[l])
        h = layer_norm(DEEPNORM_ALPHA * h + hybrid_mixer(h, w_in[l], conv_w[l], w_att_out[l], w_conv_out[l],
                                                         w_o[l], pos, topk),
                       ln2_g[l], ln2_b[l])
        h = layer_norm(DEEPNORM_ALPHA * h + 0.5 * swiglu(h, ffn2_w_gate[l], ffn2_w_up[l], ffn2_w_down[l]),
                       ln3_g[l], ln3_b[l])
    return h[:, N_META:]
```

```python
import types
import numpy as np
from contextlib import ExitStack
import concourse.bass as bass
import concourse.mybir as mybir
from concourse.bass_utils import run_bass_kernel_spmd

F32 = mybir.dt.float32
BF16 = mybir.dt.bfloat16
F8 = mybir.dt.float8e5
ALU = mybir.AluOpType
AF = mybir.ActivationFunctionType
AX = mybir.AxisListType

T = 114
NTG = 4
G = T * NTG
NG_FULL = 18
L = 8208
NTILE = 72
D = 1024
KC = 8
DFF = 2816
NFC = 22
NMETA = 16
SEQ = 8192
ALPHA = float(2.0 ** 0.25)
EPS = 1e-5
NBIS = 16
NEGM = -28672.0
QO, QIO, KKO, CVO, GAO, GCO, VWO, WPC = 0, 1024, 2048, 2560, 4096, 5120, 6144, 6280
SQ, SK, SV, SQI, SKI, SWI, SU, SGB, SGC, SGA, SGV = 0, 512, 640, 768, 1280, 1344, 1352, 1864, 2376, 2888, 3912
NRING = 3
ND = 8


def _freeze(fn):
    if fn.__closure__ is None:
        return fn
    cells = []
    for c in fn.__closure__:
        try:
            cells.append(types.CellType(c.cell_contents))
        except ValueError:
            cells.append(c)
    return types.FunctionType(fn.__code__, fn.__globals__, fn.__name__, fn.__defaults__, tuple(cells))


class Prog:
    ENGS = ("pe", "act", "dve", "pool", "sp")

    def __init__(self, nc):
        self.nc = nc
        self.ops = []
        self.last_w = {}
        self.readers = {}

    def add(self, eng, fn, R=(), W=(), dma=False):
        i = len(self.ops)
        deps = set()
        for r in R:
            lw = self.last_w.get(r)
            if lw is not None:
                deps.add(lw)
        for w in W:
            lw = self.last_w.get(w)
            if lw is not None:
                deps.add(lw)
            rs = self.readers.get(w)
            if rs:
                deps |= rs
        for r in R:
            self.readers.setdefault(r, set()).add(i)
        for w in W:
            self.last_w[w] = i
            self.readers[w] = set()
        deps.discard(i)
        self.ops.append([eng, _freeze(fn), deps, dma])
        return i

    def transfer(self, src_keys, dst_keys):
        acc = set()
        for k in src_keys:
            lw = self.last_w.get(k)
            if lw is not None:
                acc.add(lw)
            acc |= self.readers.get(k, set())
        for k in dst_keys:
            self.readers.setdefault(k, set()).update(acc)

    def emit(self, stack):
        nc = self.nc
        ops = self.ops
        engobj = {"pe": nc.tensor, "act": nc.scalar, "dve": nc.vector, "pool": nc.gpsimd, "sp": nc.sync}
        n = len(ops)
        needed = [False] * n
        red = []
        for i, (eng, fn, deps, dma) in enumerate(ops):
            best = {}
            dl = []
            for d in deps:
                pe_, _, _, pdma = ops[d]
                if pdma:
                    dl.append(d)
                else:
                    if pe_ == "pe" and eng == "pe" and not dma:
                        continue
                    if best.get(pe_, -1) < d:
                        best[pe_] = d
            dl.extend(best.values())
            red.append(dl)
            for d in dl:
                needed[d] = True
        esem = {e: stack.enter_context(nc.semaphore("s_" + e)) for e in self.ENGS}
        dsem = {e: [stack.enter_context(nc.semaphore("d_%s%d" % (e, k))) for k in range(ND)] for e in self.ENGS}
        cnt = {e: 0 for e in self.ENGS}
        dcnt = {e: 0 for e in self.ENGS}
        dhist = {e: [] for e in self.ENGS}
        sig = [None] * n
        prevdma = [None] * n
        for i, (eng, fn, deps, dma) in enumerate(ops):
            if dma:
                k = dcnt[eng]
                dcnt[eng] += 1
                sig[i] = (dsem[eng][k % ND], 16 * (k // ND + 1))
                if k >= ND:
                    prevdma[i] = dhist[eng][k - ND]
                dhist[eng].append(i)
            elif needed[i]:
                cnt[eng] += 1
                sig[i] = (esem[eng], cnt[eng])
        seen = {e: {} for e in self.ENGS}
        for i, (eng, fn, deps, dma) in enumerate(ops):
            eo = engobj[eng]
            waits = {}
            dl = list(red[i])
            if prevdma[i] is not None:
                dl.append(prevdma[i])
            for d in dl:
                s, v = sig[d]
                key = id(s)
                if key not in waits or waits[key][1] < v:
                    waits[key] = (s, v)
            for key, (s, v) in waits.items():
                if seen[eng].get(key, 0) >= v:
                    continue
                eo.wait_ge(s, v)
                seen[eng][key] = v
            ins = fn()
            if dma:
                ins.then_inc(sig[i][0], 16)
            elif needed[i]:
                ins.then_inc(sig[i][0], 1)
        for e in self.ENGS:
            for i in dhist[e][-ND:]:
                s, v = sig[i]
                if seen["sp"].get(id(s), 0) < v:
                    nc.sync.wait_ge(s, v)
                    seen["sp"][id(s)] = v
        self.stats = dict(n_ops=n, cnt=cnt, dcnt=dcnt)


class _Stop(Exception):
    pass


def build(NG=NG_FULL, debug=None, stop=None):
    nc = bass.Bass("TRN2", target_bir_lowering=False)
    stack = ExitStack()
    P = Prog(nc)

    def chk(name):
        if stop == name:
            raise _Stop()

    def dram_in(name, shape):
        return nc.dram_tensor(name, list(shape), F32, kind="ExternalInput").ap()

    x_d = dram_in("x", [SEQ, D])
    meta_d = dram_in("meta", [NMETA, D])
    wsrc = {}
    for nm, shp in [("f1g", [D, DFF]), ("f1u", [D, DFF]), ("f1d", [DFF, D]),
                    ("f2g", [D, DFF]), ("f2u", [D, DFF]), ("f2d", [DFF, D]),
                    ("win", [D, 4936]), ("wao", [512, D]), ("wco", [512, D]), ("wo", [D, D])]:
        wsrc[nm] = dram_in(nm, shp)
    lnp = {nm: dram_in(nm, [1, D]) for nm in ["ln1g", "ln1b", "ln2g", "ln2b", "ln3g", "ln3b"]}
    convw_d = dram_in("convw", [128, 12])
    cos_d = dram_in("cosT", [128, L])
    sin_d = dram_in("sinT", [128, L])
    ident_d = dram_in("ident", [128, 128])
    trin_d = dram_in("trin", [T, T])
    trip_d = dram_in("trip", [T, T])
    kcnt_d = dram_in("kcnt", [T, NTILE])
    pw2_d = dram_in("pw2", [T, NBIS])
    out_d = nc.dram_tensor("out", [SEQ, D], F32, kind="ExternalOutput").ap()
    dbg_d = {}
    if debug:
        for nm, shp in debug.items():
            dbg_d[nm] = nc.dram_tensor("dbg_" + nm, list(shp), F32, kind="ExternalOutput").ap()

    def scratch(name, shape):
        return nc.dram_tensor(name, list(shape), BF16, kind="Internal").ap()

    wsc = {"f1g": scratch("s_f1g", [D, DFF]), "f1u": scratch("s_f1u", [D, DFF]), "f1d": scratch("s_f1d", [DFF, D]),
           "f2g": scratch("s_f2g", [D, DFF]), "f2u": scratch("s_f2u", [D, DFF]), "f2d": scratch("s_f2d", [DFF, D]),
           "wp": scratch("s_wp", [D, WPC]), "wao": scratch("s_wao", [512, D]), "wco": scratch("s_wco", [512, D]),
           "wo": scratch("s_wo", [D, D])}

    def sb(name, shape, dt=F32):
        return stack.enter_context(nc.sbuf_tensor("sb_" + name, list(shape), dt))

    kT = sb("kT", [128, L], BF16)
    kiT = sb("kiT", [128, L], BF16)
    Vc = sb("Vc", [128, NTILE, 130], BF16)
    S = sb("S", [T, NTG, D], F32)
    aT = sb("aT", [128, KC, G], BF16)
    xbf = [sb("xbf0", [T, D], BF16)] * 2
    hT = sb("hT", [128, NFC, G], BF16)
    qiz = sb("qiz", [128, 8, G], BF16)
    ycv = sb("ycv", [128, 4, G], BF16)
    attT = sb("attT", [64, 8, G], BF16)
    scores = sb("scores", [128, L], F32)
    ring = [sb("ring%d" % i, [128, 4096], BF16) for i in range(NRING)]
    ropec = sb("ropec", [128, G], F32)
    ropes = sb("ropes", [128, G], F32)
    gbq = sb("gbq", [128, 2 * D], F32)
    gb = gbq[0:T, :].rearrange("p (a d) -> p a d", a=2)
    qz = gbq[:].bitcast(BF16)[:, 0:2 * 4 * G].rearrange("p (g r t) -> p g r t", g=2, r=4)
    tmpa = [sb("tmpa%d" % i, [128, G], F32) for i in range(2)]
    sg = tmpa
    tmpb = [sb("tmpb%d" % i, [128, G], F32) for i in range(2)]
    bcs = tmpb[1][0:64, :]
    zt = sb("zt", [128, G + 2], F32)
    halo = sb("halo", [128, 4, 2], F32)
    convw = sb("convw", [128, 12], F32)
    identf = sb("identf", [128, 128], F32)
    identb = sb("identb", [128, 128], BF16)
    I4 = sb("I4", [128, G], F8)
    trin = sb("trin", [T, T], F32)
    trip = sb("trip", [T, T], F32)
    kcnt = sb("kcnt", [T, NTILE], F32)
    pw2 = sb("pw2", [T, NBIS], F32)
    ones32 = sb("ones32", [128, 64], F32)
    neghalf = sb("neghalf", [128, NTG], F32)
    st = sb("st", [T, NTG, 12], F32)
    mv = sb("mv", [T, NTG, 2], F32)
    ve = sb("ve", [T, NTG], F32)
    rstd = sb("rstd", [T, NTG], F32)
    wabs = sb("wabs", [T, NTG, 8], F32)
    dg = sb("dg", [128, 8, T], BF16)
    rr = [sb("rr%d" % i, [128, 2, 512], BF16) for i in range(3)]
    ee = [sb("ee%d" % i, [128, 2, G], BF16) for i in range(2)]
    t114 = sb("t114", [T, T], F32)
    m0 = sb("m0", [T, 1], F32)
    m1 = sb("m1", [T, 1], F32)
    lo128 = sb("lo", [128, 1], F32)
    lo = lo128[0:T, :]
    rmax = sb("rmax", [T, 1], F32)
    rng = sb("rng", [T, 1], F32)
    steps = sb("steps", [T, NBIS], F32)
    mid = sb("mid", [T, 1], F32)
    cntc = sb("cntc", [T, 1], F32)
    cnta = sb("cnta", [T, 1], F32)
    negmid = sb("negmid", [T, 1], F32)
    kadj = sb("kadj", [T, 1], F32)
    ctmp = sb("ctmp", [T, 1], F32)
    gec = sb("gec", [T, 1], F32)
    rden = sb("rden", [128, G], F32)
    psall = stack.enter_context(nc.psum_tensor("psall", [128, 8, 512], F32))
    ps = [psall[:, b, :] for b in range(8)]
    ps7b = psall[:, 7, :].bitcast(BF16)
    h8 = hT[:].rearrange("p a b -> p (a b)").bitcast(F8)
    nm8 = [h8[:, 0:L], h8[:, L:2 * L]]

    PSK = [("ps", b) for b in range(8)]
    HTK = [("hT", f) for f in range(NFC)]

    sp_dma = lambda out, in_, R, W: P.add("sp", lambda: nc.sync.dma_start(out=out, in_=in_), R=R, W=W, dma=True)
    st_dma = lambda out, in_, R, W: P.add("pool", lambda: nc.gpsimd.dma_start(out=out, in_=in_), R=R, W=W, dma=True)

    wsc_barrier = []
    SCK = [("sc", c) for c in range(17)]

    def main():
        sp_dma(identf[:], ident_d, [], ["identf"])
        sp_dma(trin[:], trin_d, [], ["trin"])
        sp_dma(trip[:], trip_d, [], ["trip"])
        sp_dma(kcnt[:], kcnt_d, [], ["kcnt"])
        sp_dma(pw2[:], pw2_d, [], ["pw2"])
        sp_dma(convw[:], convw_d, [], ["convw"])
        P.add("dve", lambda: nc.vector.tensor_copy(identb[:], identf[:]), R=["identf"], W=["identb"])
        P.add("dve", lambda: nc.vector.memset(I4[:], 0.0), W=["I4"])
        for r in range(4):
            P.add("dve", lambda r=r: nc.vector.tensor_scalar(I4[0:T, r * T:(r + 1) * T], identf[0:T, 0:T], NEGM, None, ALU.mult, saturate=False), R=["identf"], W=["I4"])
        P.add("pool", lambda: nc.gpsimd.memset(qiz[:], 0.0), W=[("qiT", c) for c in range(4)])
        P.add("pool", lambda: nc.gpsimd.memset(lo128[:], 0.0), W=["lo"])
        P.add("pool", lambda: nc.gpsimd.memset(dg[:], 0.0), W=[("dg", h) for h in range(8)])
        for i in range(3):
            P.add("pool", lambda i=i: nc.gpsimd.memset(rr[i][:], 0.0), W=[("rr", i)])
        for i in range(2):
            P.add("pool", lambda i=i: nc.gpsimd.memset(ee[i][:], 0.0), W=[("ee", i)])
        P.add("pool", lambda: nc.gpsimd.memset(ones32[:], 1.0), W=["ones32"])
        P.add("pool", lambda: nc.gpsimd.memset(neghalf[:], -0.5), W=["neghalf"])
        P.add("pool", lambda: nc.gpsimd.memset(halo[:], 0.0), W=["halo"])
        P.add("pool", lambda: nc.gpsimd.memset(Vc[:], 0.0), W=[("V", t) for t in range(NTILE)])
        P.add("pool", lambda: nc.gpsimd.memset(Vc[0:T], 1.0), W=[("V", t) for t in range(NTILE)])

        chk('const')
        stage32 = [scores[:, 0:4104], scores[:, 4104:8208]]
        hflat = hT[:].rearrange("p a b -> p (a b)")
        stage16 = [hflat[:, 0:4104], hflat[:, 4104:8208]]
        ceng = ["act", "dve", "act", "dve", "pool"]
        cstate = [0]

        def cast(out, in_, R, W):
            e = ceng[cstate[0] % 5]
            cstate[0] += 1
            if e == "act":
                P.add("act", lambda: nc.scalar.copy(out, in_), R=R, W=W)
            elif e == "dve":
                P.add("dve", lambda: nc.vector.tensor_copy(out, in_), R=R, W=W)
            else:
                P.add("pool", lambda: nc.gpsimd.tensor_copy(out, in_), R=R, W=W)

        pj = [0]

        def prep_plain(src, dst, rows, cols):
            for rb in range(rows // 128):
                i = pj[0] % 2
                pj[0] += 1
                s32 = stage32[i][:, 0:cols]
                s16 = stage16[i][:, 0:cols]
                sp_dma(s32, src[rb * 128:(rb + 1) * 128, :], [], [("p32", i)])
                h = cols // 2
                cast(s16[:, 0:h], s32[:, 0:h], [("p32", i)], [("p16", i, 0)])
                cast(s16[:, h:cols], s32[:, h:cols], [("p32", i)], [("p16", i, 1)])
                st_dma(dst[rb * 128:(rb + 1) * 128, :], s16, [("p16", i, 0), ("p16", i, 1)], [])

        for nm in ["f1g", "f1u"]:
            prep_plain(wsrc[nm], wsc[nm], D, DFF)
        prep_plain(wsrc["f1d"], wsc["f1d"], DFF, D)

        s32 = scores[:, 0:4936]
        s16 = hflat[:, 0:WPC]
        for rb in range(8):
            K32 = [("p32", 0), ("p32", 1)]
            K16 = [("p16", 0, 0), ("p16", 0, 1), ("p16", 1, 0), ("p16", 1, 1)]
            sp_dma(s32, wsrc["win"][rb * 128:(rb + 1) * 128, :], [], K32)
            if rb == 0:
                P.add("pool", lambda: nc.gpsimd.memset(ve[:], 0.0), R=[], W=K16 + ["wpbar", "ve"])

            def cp(dst, src, tag):
                cast(dst, src, K32 + ["wpbar"], [("wpw", tag)])
            qsrc = s32[:, SQ:SQ + 512].rearrange("p (j a e) -> p a j e", j=2, a=4, e=64)
            for c2 in range(2):
                dA = s16[:, QO + c2 * 512: QO + c2 * 512 + 256].rearrange("p (a j e) -> p a j e", a=2, j=2, e=64)
                dB = s16[:, QO + c2 * 512 + 256: QO + c2 * 512 + 512].rearrange("p (a j e) -> p a j e", a=2, j=2, e=64)
                for a in range(2):
                    cp(dA[:, a], qsrc[:, 2 * c2 + a], ("qA", c2, a))
                    for hf in range(2):
                        cp(dB[:, a, :, hf * 32:(hf + 1) * 32], qsrc[:, 2 * c2 + a, :, (1 - hf) * 32:(2 - hf) * 32], ("qB", c2, a, hf))
            for c2 in range(2):
                cp(s16[:, QIO + c2 * 512: QIO + c2 * 512 + 256], s32[:, SQI + c2 * 256: SQI + c2 * 256 + 256], ("qiA", c2))
                dB = s16[:, QIO + c2 * 512 + 256: QIO + c2 * 512 + 512].rearrange("p (h f e) -> p h f e", h=4, f=2, e=32)
                sB = s32[:, SQI + c2 * 256: SQI + c2 * 256 + 256].rearrange("p (h f e) -> p h f e", h=4, f=2, e=32)
                for hf in range(2):
                    cp(dB[:, :, hf], sB[:, :, 1 - hf], ("qiB", c2, hf))
            cp(s16[:, KKO:KKO + 128], s32[:, SK:SK + 128], "k")
            dB = s16[:, KKO + 128:KKO + 256].rearrange("p (h f e) -> p h f e", h=2, f=2, e=32)
            sB = s32[:, SK:SK + 128].rearrange("p (h f e) -> p h f e", h=2, f=2, e=32)
            for hf in range(2):
                cp(dB[:, :, hf], sB[:, :, 1 - hf], ("kB", hf))
            for cpy in range(2):
                cp(s16[:, KKO + 256 + cpy * 64: KKO + 256 + cpy * 64 + 64], s32[:, SKI:SKI + 64], ("ki", cpy))
                for hf in range(2):
                    cp(s16[:, KKO + 384 + cpy * 64 + hf * 32: KKO + 384 + cpy * 64 + hf * 32 + 32],
                       s32[:, SKI + (1 - hf) * 32: SKI + (1 - hf) * 32 + 32], ("kiB", cpy, hf))
            dC = s16[:, CVO:CVO + 1536].rearrange("p (c m e) -> p c m e", c=4, m=3, e=128)
            for m, so in enumerate([SU, SGC, SGB]):
                cp(dC[:, :, m], s32[:, so:so + 512].rearrange("p (c e) -> p c e", c=4, e=128), ("cv", m))
            cp(s16[:, GAO:GAO + 1024], s32[:, SGA:SGA + 1024], "ga")
            cp(s16[:, GCO:GCO + 1024], s32[:, SGV:SGV + 1024], "gc")
            cp(s16[:, VWO:VWO + 128], s32[:, SV:SV + 128], "v")
            cp(s16[:, VWO + 128:VWO + 136], s32[:, SWI:SWI + 8], "wi")
            tags = [k for k in P.last_w.keys() if isinstance(k, tuple) and k and k[0] == "wpw"]
            st_dma(wsc["wp"][rb * 128:(rb + 1) * 128, :], s16, tags, K16)

        for nm, rows in [("wao", 512), ("wco", 512), ("wo", D)]:
            prep_plain(wsrc[nm], wsc[nm], rows, D)
        for nm in ["f2g", "f2u"]:
            prep_plain(wsrc[nm], wsc[nm], D, DFF)
        prep_plain(wsrc["f2d"], wsc["f2d"], DFF, D)
        allprep = [("p32", 0), ("p32", 1), ("p16", 0, 0), ("p16", 0, 1), ("p16", 1, 0), ("p16", 1, 1)] + \
            [k for k in P.last_w.keys() if isinstance(k, tuple) and k and k[0] == "wpw"]
        P.transfer(allprep, HTK + SCK + ["nm0", "nm1", "nmA0", "nmA1", "nmB0", "nmB1"])
        prep_stores = [i for i, o in enumerate(P.ops) if o[3]]
        wsc_barrier.extend(prep_stores)


    rstate = [0]
    first_loads = [True]

    def ring_load(parts):
        s = rstate[0] % NRING
        rstate[0] += 1
        for (vf, src, hk) in parts:
            hks = hk if isinstance(hk, tuple) else (hk,)
            i = sp_dma(vf(ring[s]), src, [], [("rg", s, h_) for h_ in hks])
            P.ops[i][2].update(wsc_barrier)
        return s

    def v3(lo_, n, kc):
        return lambda r: r[:, lo_:lo_ + n * kc].rearrange("p (k f) -> p k f", k=kc)

    def to_featmajor(t):
        xb = xbf[0]
        P.add("act", lambda: nc.scalar.copy(xb[:], S[:, t, :]), R=[("S", t)], W=[("xbf", 0)])
        for kc in range(KC):
            P.add("pe", lambda kc=kc: nc.tensor.transpose(ps7b[:, kc * T:(kc + 1) * T], xb[:, kc * 128:(kc + 1) * 128], identb[0:T, 0:T]),
                  R=[("xbf", 0), "identb"], W=[("ps", 7)])
        P.add("dve", lambda: nc.vector.tensor_copy(aT[:, :, t * T:(t + 1) * T], ps7b[:, 0:KC * T].rearrange("p (k t) -> p k t", k=KC)),
              R=[("ps", 7)], W=[("aT", t)])

    ATK = [("aT", t) for t in range(NTG)]

    def ffn(wg, wu, wd, resid_scale):
        for j in range(11):
            s = ring_load([(v3(0, 256, KC), wg[:, 256 * j:256 * j + 256].rearrange("(k p) f -> p k f", p=128), 0),
                           (v3(2048, 256, KC), wu[:, 256 * j:256 * j + 256].rearrange("(k p) f -> p k f", p=128), 1)])
            gv = v3(0, 256, KC)(ring[s])
            uv = v3(2048, 256, KC)(ring[s])
            for f2 in range(2):
                fc = 2 * j + f2
                bg = fc % 2
                bu = 2 + fc % 2
                for kc in range(KC):
                    P.add("pe", lambda kc=kc, f2=f2, bg=bg: nc.tensor.matmul(ps[bg][:, 0:G], gv[:, kc, f2 * 128:(f2 + 1) * 128], aT[:, kc, :], start=(kc == 0), stop=(kc == KC - 1)),
                          R=[("rg", s, 0)] + ATK, W=[("ps", bg)])
                for kc in range(KC):
                    P.add("pe", lambda kc=kc, f2=f2, bu=bu: nc.tensor.matmul(ps[bu][:, 0:G], uv[:, kc, f2 * 128:(f2 + 1) * 128], aT[:, kc, :], start=(kc == 0), stop=(kc == KC - 1)),
                          R=[("rg", s, 1)] + ATK, W=[("ps", bu)])
                sgt = sg[fc % 2]
                P.add("act", lambda bg=bg, sgt=sgt: nc.scalar.activation(sgt[:], ps[bg][:, 0:G], AF.Silu), R=[("ps", bg)], W=[("tmpa", fc % 2)])
                P.add("dve", lambda bu=bu, sgt=sgt, fc=fc: nc.vector.scalar_tensor_tensor(hT[:, fc, :], sgt[:], resid_scale, ps[bu][:, 0:G], ALU.mult, ALU.mult),
                      R=[("tmpa", fc % 2), ("ps", bu)], W=[("hT", fc)])
        for j in range(6):
            nf = 4 if j < 5 else 2
            s = ring_load([(lambda r, nf=nf: r[:, 0:nf * 1024].rearrange("p (c d) -> p c d", c=nf),
                            wd[512 * j:512 * j + 128 * nf, :].rearrange("(c p) d -> p c d", p=128), (0, 1))])
            wv = ring[s][:, 0:nf * 1024].rearrange("p (c d) -> p c d", c=nf)
            for c in range(nf):
                fc = 4 * j + c
                for t in range(NTG):
                    for dh in range(2):
                        b = t * 2 + dh
                        P.add("pe", lambda c=c, fc=fc, t=t, dh=dh, b=b, wv=wv: nc.tensor.matmul(ps[b][0:T, :], hT[:, fc, t * T:(t + 1) * T], wv[:, c, dh * 512:(dh + 1) * 512], start=(fc == 0), stop=(fc == NFC - 1)),
                              R=[("rg", s, 0), ("rg", s, 1), ("hT", fc)], W=[("ps", b)])
        for t in range(NTG):
            for dh in range(2):
                b = t * 2 + dh
                P.add("dve", lambda t=t, dh=dh, b=b: nc.vector.scalar_tensor_tensor(S[:, t, dh * 512:(dh + 1) * 512], S[:, t, dh * 512:(dh + 1) * 512], ALPHA, ps[b][0:T, :], ALU.mult, ALU.add),
                      R=[("ps", b), ("S", t)], W=[("S", t)])

    def load_gb(gname, bname):
        sp_dma(gb[:, 0, :], lnp[gname].partition_broadcast(T), [], ["gb"])
        sp_dma(gb[:, 1, :], lnp[bname].partition_broadcast(T), [], ["gb2"])

    def layernorm_all(after=None):
        for t in range(NTG):
            for hh in range(2):
                P.add("dve", lambda t=t, hh=hh: nc.vector.bn_stats(st[:, t, hh * 6:(hh + 1) * 6], S[:, t, hh * 512:(hh + 1) * 512]), R=[("S", t)], W=[("st", t, hh)])
            P.add("dve", lambda t=t: nc.vector.bn_aggr(mv[:, t, :], st[:, t, :]), R=[("st", t, 0), ("st", t, 1)], W=[("mv", t)])
        MVK = [("mv", t) for t in range(NTG)]
        P.add("pool", lambda: nc.gpsimd.tensor_scalar(ve[:], mv[:, :, 1], EPS, None, ALU.add), R=MVK, W=["ve"])
        P.add("pool", lambda: nc.gpsimd.tensor_tensor(rstd[:], ve[:], neghalf[0:T, :], ALU.pow), R=["ve", "neghalf"], W=["rstd"])
        for t in range(NTG):
            St = S[:, t, :]
            P.add("dve", lambda t=t, St=St: nc.vector.scalar_tensor_tensor(St, St, mv[:, t, 0:1], gb[:, 0, :], ALU.subtract, ALU.mult), R=[("S", t), ("mv", t), "gb"], W=[("S", t)])
            P.add("dve", lambda t=t, St=St: nc.vector.scalar_tensor_tensor(St, St, rstd[:, t:t + 1], gb[:, 1, :], ALU.mult, ALU.add), R=[("S", t), "rstd", "gb2"], W=[("S", t)])
            if after is not None:
                after(t)

    def wp_piece(col0, ncols):
        return ring_load([(v3(0, ncols, KC), wsc["wp"][:, col0:col0 + ncols].rearrange("(k p) f -> p k f", p=128), (0, 1))])

    def proj_fm(s, off, bank):
        wv = v3(0, 512, KC)(ring[s])
        for kc in range(KC):
            P.add("pe", lambda kc=kc: nc.tensor.matmul(ps[bank][:, 0:G], wv[:, kc, off:off + 128], aT[:, kc, :], start=(kc == 0), stop=(kc == KC - 1)),
                  R=[("rg", s, 0), ("rg", s, 1)] + ATK, W=[("ps", bank)])

    pairctr = [0]

    def rope_chunk(s, offA, offB, dst, dstR, dstW):
        i = pairctr[0] % 2
        pairctr[0] += 1
        bA, bB = 2 * i, 2 * i + 1
        proj_fm(s, offA, bA)
        proj_fm(s, offB, bB)
        ta, tb = tmpa[i], tmpb[i]
        P.add("dve", lambda: nc.vector.tensor_tensor(ta[:], ps[bA][:, 0:G], ropec[:], ALU.mult), R=[("ps", bA), "ropec"], W=[("tmpa", i)])
        P.add("dve", lambda: nc.vector.tensor_tensor(tb[:], ps[bB][:, 0:G], ropes[:], ALU.mult), R=[("ps", bB), "ropes"], W=[("tmpb", i)])
        if isinstance(dst, tuple):
            P.add("dve", lambda: nc.vector.tensor_tensor(dst[0], ta[0:64, :], tb[0:64, :], ALU.add), R=[("tmpa", i), ("tmpb", i)] + dstR, W=dstW)
            P.add("dve", lambda: nc.vector.tensor_tensor(dst[1], ta[64:128, :], tb[64:128, :], ALU.add), R=[("tmpa", i), ("tmpb", i)] + dstR, W=dstW)
        else:
            P.add("dve", lambda: nc.vector.tensor_tensor(dst, ta[:], tb[:], ALU.add), R=[("tmpa", i), ("tmpb", i)] + dstR, W=dstW)

    def group(g):
        p0 = g * G
        for t in range(NTG):
            pos0 = p0 + t * T
            if pos0 == 0:
                sp_dma(S[0:NMETA, t, :], meta_d, [], [("S", t)])
                sp_dma(S[NMETA:T, t, :], x_d[0:T - NMETA, :], [], [("S", t)])
            else:
                sp_dma(S[:, t, :], x_d[pos0 - NMETA:pos0 - NMETA + T, :], [], [("S", t)])
        for t in range(NTG):
            to_featmajor(t)
        chk('load')
        load_gb("ln1g", "ln1b")
        ffn(wsc["f1g"], wsc["f1u"], wsc["f1d"], 0.5)
        chk('ffn1')
        layernorm_all()
        if debug and "h1" in debug and g == 0:
            for t in range(NTG):
                sp_dma(dbg_d["h1"][t * T:(t + 1) * T, :], S[:, t, :], [("S", t)], [])
        for t in range(NTG):
            to_featmajor(t)
        chk('ln1')
        QTK = [("qT", r) for r in range(4)]
        P.transfer(["gb", "gb2"], QTK)
        P.add("pool", lambda: nc.gpsimd.memset(qz[64:128, 0], 0.0), W=QTK)
        P.add("pool", lambda: nc.gpsimd.memset(qz[0:64, 1], 0.0), W=QTK)
        sp_dma(ropec[:], cos_d[:, p0:p0 + G], [], ["ropec"])
        sp_dma(ropes[:], sin_d[:, p0:p0 + G], [], ["ropes"])
        for c2 in range(2):
            s = wp_piece(QIO + c2 * 512, 512)
            for a in range(2):
                c = 2 * c2 + a
                rope_chunk(s, a * 128, 256 + a * 128, (qiz[0:64, 2 * c, :], qiz[64:128, 2 * c + 1, :]), [], [("qiT", c)])
        chk('projq')
        s = wp_piece(KKO, 512)
        rope_chunk(s, 0, 128, kT[:, p0:p0 + G], [], [("kT", g)])
        rope_chunk(s, 256, 384, kiT[:, p0:p0 + G], [], [("kiT", g)])
        chk('projc')
        s = ring_load([(v3(0, 136, KC), wsc["wp"][:, VWO:VWO + 136].rearrange("(k p) f -> p k f", p=128), 0)])
        wv = v3(0, 136, KC)(ring[s])
        for t in range(NTG):
            qt = g * NTG + t
            for kc in range(KC):
                P.add("pe", lambda kc=kc, t=t: nc.tensor.matmul(ps[7][0:T, 0:136], aT[:, kc, t * T:(t + 1) * T], wv[:, kc, :], start=(kc == 0), stop=(kc == KC - 1)),
                      R=[("rg", s, 0), ("aT", t)], W=[("ps", 7)])
            chk('pv1')
            P.add("act", lambda t=t, qt=qt: nc.scalar.copy(Vc[0:T, qt, :].rearrange("p (g e) -> p g e", g=2)[:, :, 0:64], ps[7][0:T, 0:128].rearrange("p (g e) -> p g e", g=2)),
                  R=[("ps", 7)], W=[("V", qt)])
            chk('pv2')
            P.add("act", lambda t=t: nc.scalar.activation(wabs[:, t, :], ps[7][0:T, 128:136], AF.Copy, scale=float(512 ** -0.5)), R=[("ps", 7)], W=[("wabs", t)])
        def late_q(c2):
            s = wp_piece(QO + c2 * 512, 512)
            for a in range(2):
                r = 2 * c2 + a
                rope_chunk(s, a * 128, 256 + a * 128, (qz[0:64, 0, r, :], qz[64:128, 1, r, :]), [], [("qT", r)])

        def late_conv(sl, c):
            offs = [(3 * c + m) * 128 for m in range(3)]
            for m in range(3):
                proj_fm(sl[offs[m] // 512], offs[m] % 512, 4 + m)
            P.add("act", lambda: nc.scalar.copy(tmpa[0][:], ps[4][:, 0:G]), R=[("ps", 4)], W=[("tmpa", 0)])
            P.add("pool", lambda c=c: nc.gpsimd.tensor_copy(zt[:, 0:2], halo[:, c, :]), R=["halo"], W=["zt0"])
            P.add("dve", lambda: nc.vector.tensor_tensor(zt[:, 2:G + 2], tmpa[0][:], ps[5][:, 0:G], ALU.mult), R=[("tmpa", 0), ("ps", 5)], W=["zt"])
            P.add("pool", lambda c=c: nc.gpsimd.tensor_copy(halo[:, c, :], zt[:, G:G + 2]), R=["zt", "zt0"], W=["halo"])
            P.add("dve", lambda c=c: nc.vector.tensor_scalar(tmpb[0][:], zt[:, 2:G + 2], convw[:, c * 3 + 2:c * 3 + 3], None, ALU.mult), R=["zt", "convw"], W=[("tmpb", 0)])
            P.add("dve", lambda c=c: nc.vector.scalar_tensor_tensor(tmpb[0][:], zt[:, 1:G + 1], convw[:, c * 3 + 1:c * 3 + 2], tmpb[0][:], ALU.mult, ALU.add), R=["zt", "zt0", "convw", ("tmpb", 0)], W=[("tmpb", 0)])
            P.add("dve", lambda c=c: nc.vector.scalar_tensor_tensor(tmpb[0][:], zt[:, 0:G], convw[:, c * 3:c * 3 + 1], tmpb[0][:], ALU.mult, ALU.add), R=["zt", "zt0", "convw", ("tmpb", 0)], W=[("tmpb", 0)])
            P.add("dve", lambda c=c: nc.vector.tensor_tensor(ycv[:, c, :], tmpb[0][:], ps[6][:, 0:G], ALU.mult), R=[("tmpb", 0), ("ps", 6)], W=[("ycv", c)])

        chk('proj')
        P.transfer(HTK, ["nm0", "nm1", "nmA0", "nmA1", "nmB0", "nmB1"])
        ctx = [att_idx(g, 0)]
        ga = att_bis(ctx[0], 0.5)
        slh = []

        def conv_job(c):
            if not slh:
                slh.extend([wp_piece(CVO + i * 512, 512) for i in range(3)])
            late_conv(slh, c)
        work = [lambda: late_q(0), lambda: late_q(1)] + [lambda c=c: conv_job(c) for c in range(4)]
        for i in range(NBIS):
            next(ga)
            if i % 2 == 1 and work:
                work.pop(0)()
        for _ in ga:
            pass
        while work:
            work.pop(0)()
        for t in range(1, NTG):
            ctx.append(att_idx(g, t))
            ga = att_bis(ctx[t], 0.38)
            gb_ = att_att(ctx[t - 1])
            nb = ctx[t - 1]["qt"] + 1
            done = 0
            head = min(nb, max(1, nb // 8))
            while done < head:
                next(gb_)
                done += 1
            for i in range(NBIS):
                next(ga)
                tgt = head + ((i + 1) * (nb - head)) // NBIS
                while done < tgt:
                    next(gb_)
                    done += 1
            for _ in ga:
                pass
            for _ in gb_:
                pass
        for _ in att_att(ctx[NTG - 1]):
            pass
        P.transfer(["nm0", "nm1", "nmA0", "nmA1", "nmB0", "nmB1"], HTK)
        P.transfer([("qT", r) for r in range(4)], ["gb", "gb2"])
        chk('att')
        load_gb("ln2g", "ln2b")
        ATT = [("attT", h) for h in range(8)]
        for qd in range(4):
            c0q = qd * 256
            sX = ring_load([(lambda r: r[0:64, 0:2048].rearrange("p (h d) -> p h d", h=8), wsc["wao"][:, c0q:c0q + 256].rearrange("(h p) d -> p h d", p=64), 0),
                            (lambda r: r[:, 2048:3072].rearrange("p (c d) -> p c d", c=4), wsc["wco"][:, c0q:c0q + 256].rearrange("(c p) d -> p c d", p=128), 1)])
            sY = ring_load([(v3(0, 256, KC), wsc["wp"][:, GAO + c0q:GAO + c0q + 256].rearrange("(k p) f -> p k f", p=128), 0),
                            (v3(2048, 256, KC), wsc["wp"][:, GCO + c0q:GCO + c0q + 256].rearrange("(k p) f -> p k f", p=128), 1)])
            wa = ring[sX][0:64, 0:2048].rearrange("p (h d) -> p h d", h=8)
            wc = ring[sX][:, 2048:3072].rearrange("p (c d) -> p c d", c=4)
            wga = v3(0, 256, KC)(ring[sY])
            wgc = v3(2048, 256, KC)(ring[sY])
            for d2 in range(2):
                dc = qd * 2 + d2
                bo = 4 * (dc % 2)
                cs = slice(d2 * 128, (d2 + 1) * 128)
                for h in range(8):
                    P.add("pe", lambda: nc.tensor.matmul(ps[bo][:, 0:G], wa[:, h, cs], attT[:, h, :], start=(h == 0), stop=(h == 7)),
                          R=[("rg", sX, 0)] + ATT, W=[("ps", bo)])
                for c in range(4):
                    P.add("pe", lambda: nc.tensor.matmul(ps[bo + 1][:, 0:G], wc[:, c, cs], ycv[:, c, :], start=(c == 0), stop=(c == 3)),
                          R=[("rg", sX, 1)] + [("ycv", cc) for cc in range(4)], W=[("ps", bo + 1)])
                for kc in range(KC):
                    P.add("pe", lambda: nc.tensor.matmul(ps[bo + 2][:, 0:G], wga[:, kc, cs], aT[:, kc, :], start=(kc == 0), stop=(kc == KC - 1)),
                          R=[("rg", sY, 0)] + ATK, W=[("ps", bo + 2)])
                for kc in range(KC):
                    P.add("pe", lambda: nc.tensor.matmul(ps[bo + 3][:, 0:G], wgc[:, kc, cs], aT[:, kc, :], start=(kc == 0), stop=(kc == KC - 1)),
                          R=[("rg", sY, 1)] + ATK, W=[("ps", bo + 3)])
                i = dc % 2
                P.add("act", lambda: nc.scalar.activation(tmpa[i][:], ps[bo + 2][:, 0:G], AF.Sigmoid), R=[("ps", bo + 2)], W=[("tmpa", i)])
                P.add("act", lambda: nc.scalar.activation(tmpb[i][:], ps[bo + 3][:, 0:G], AF.Sigmoid), R=[("ps", bo + 3)], W=[("tmpb", i)])
                P.add("dve", lambda: nc.vector.tensor_tensor(tmpa[i][:], tmpa[i][:], ps[bo][:, 0:G], ALU.mult), R=[("tmpa", i), ("ps", bo)], W=[("tmpa", i)])
                P.add("dve", lambda: nc.vector.tensor_tensor(tmpb[i][:], tmpb[i][:], ps[bo + 1][:, 0:G], ALU.mult), R=[("tmpb", i), ("ps", bo + 1)], W=[("tmpb", i)])
                P.add("dve", lambda: nc.vector.tensor_tensor(hT[:, dc, :], tmpa[i][:], tmpb[i][:], ALU.add), R=[("tmpa", i), ("tmpb", i)], W=[("hT", dc)])
        for dh in range(2):
            s = ring_load([(v3(0, 512, KC), wsc["wo"][:, dh * 512:(dh + 1) * 512].rearrange("(k p) f -> p k f", p=128), (0, 1))])
            wv = v3(0, 512, KC)(ring[s])
            for t in range(NTG):
                b = (dh * NTG + t) % 8
                for kc in range(KC):
                    P.add("pe", lambda kc=kc, t=t, b=b, wv=wv: nc.tensor.matmul(ps[b][0:T, :], hT[:, kc, t * T:(t + 1) * T], wv[:, kc, :], start=(kc == 0), stop=(kc == KC - 1)),
                          R=[("rg", s, 0), ("rg", s, 1)] + [("hT", k) for k in range(8)], W=[("ps", b)])
                P.add("dve", lambda t=t, dh=dh, b=b: nc.vector.scalar_tensor_tensor(S[:, t, dh * 512:(dh + 1) * 512], S[:, t, dh * 512:(dh + 1) * 512], ALPHA, ps[b][0:T, :], ALU.mult, ALU.add),
                      R=[("ps", b), ("S", t)], W=[("S", t)])
        chk('merge')
        layernorm_all()
        if debug and "h2" in debug and g == 0:
            for t in range(NTG):
                sp_dma(dbg_d["h2"][t * T:(t + 1) * T, :], S[:, t, :], [("S", t)], [])
        for t in range(NTG):
            to_featmajor(t)
        chk('ln2')
        load_gb("ln3g", "ln3b")
        ffn(wsc["f2g"], wsc["f2u"], wsc["f2d"], 0.5)

        def store(t):
            pos0 = p0 + t * T
            if pos0 == 0:
                st_dma(out_d[0:T - NMETA, :], S[NMETA:T, t, :], [("S", t)], [])
            else:
                st_dma(out_d[pos0 - NMETA:pos0 - NMETA + T, :], S[:, t, :], [("S", t)], [])
        layernorm_all(after=store)

    def att_idx(g, t):
        qt = g * NTG + t
        q0 = qt * T
        nk = q0 + T
        tc = slice(t * T, (t + 1) * T)
        for h in range(8):
            P.add("pool", lambda h=h: nc.gpsimd.tensor_scalar(dg[0:T, h, :], identb[0:T, 0:T], wabs[:, t, h:h + 1], None, ALU.mult),
                  R=["identb", ("wabs", t)], W=[("dg", h)])
        nch = (nk + 511) // 512
        items = [(ci, hp) for ci in range(nch) for hp in range(4)]

        def geom(ci):
            c0 = ci * 512
            c1 = min(nk, c0 + 512)
            return c0, c1, c1 - c0

        def X(j):
            ci, hp = items[j]
            c0, c1, w = geom(ci)
            b0 = [0, 4, 6][j % 3]
            ri = j % 3
            KIK = [("kiT", gg) for gg in range(c0 // G, (c1 - 1) // G + 1)]
            for hh in range(2):
                h = 2 * hp + hh
                P.add("pe", lambda: nc.tensor.matmul(ps[b0 + hh][0:T, 0:w], qiz[:, h, tc], kiT[:, c0:c1], start=True, stop=True),
                      R=[("qiT", hp)] + KIK, W=[("ps", b0 + hh)])
            if j % 8 in (1, 4, 6):
                P.add("dve", lambda: nc.vector.tensor_scalar(rr[ri][0:T, :, 0:w], psall[0:T, b0:b0 + 2, 0:w], 0.0, None, ALU.max),
                      R=[("ps", b0), ("ps", b0 + 1)], W=[("rr", ri)])
            else:
                P.add("act", lambda: nc.scalar.activation(rr[ri][0:T, :, 0:w], psall[0:T, b0:b0 + 2, 0:w], AF.Relu),
                      R=[("ps", b0), ("ps", b0 + 1)], W=[("rr", ri)])

        def A(j):
            ci, hp = items[j]
            c0, c1, w = geom(ci)
            pacc = 2 + ci % 2
            ri = j % 3
            for hh in range(2):
                h = 2 * hp + hh
                P.add("pe", lambda: nc.tensor.matmul(ps[pacc][0:T, 0:w], dg[:, h, :], rr[ri][:, hh, 0:w], start=(h == 0), stop=(h == 7)),
                      R=[("dg", h), ("rr", ri)], W=[("ps", pacc)])
            if hp == 3:
                if ci % 2 == 0:
                    P.add("act", lambda: nc.scalar.copy(scores[0:T, c0:c1], ps[pacc][0:T, 0:w]), R=[("ps", pacc)], W=[("sc", ci)])
                else:
                    P.add("dve", lambda: nc.vector.tensor_copy(scores[0:T, c0:c1], ps[pacc][0:T, 0:w]), R=[("ps", pacc)], W=[("sc", ci)])
        LA = 2
        for j in range(min(LA, len(items))):
            X(j)
        for j in range(len(items)):
            if j + LA < len(items):
                X(j + LA)
            A(j)
        SCR = [("sc", ci) for ci in range(nch)]
        dci = sorted(set([q0 // 512, (nk - 1) // 512]))
        DCK = [("sc", ci) for ci in dci]
        P.add("dve", lambda: nc.vector.tensor_tensor(t114[:], scores[0:T, q0:nk], trip[:], ALU.add), R=DCK + ["trip"], W=["t114"])
        P.add("dve", lambda: nc.vector.tensor_reduce(m1[:], t114[:], AX.X, ALU.min), R=["t114"], W=["m1"])
        if q0 > 0:
            P.add("dve", lambda: nc.vector.tensor_reduce(m0[:], scores[0:T, 0:q0], AX.X, ALU.min), R=SCR, W=["m0"])
            P.add("dve", lambda: nc.vector.tensor_tensor(lo[:], m0[:], m1[:], ALU.min), R=["m0", "m1"], W=["lo"])
        else:
            P.add("dve", lambda: nc.vector.tensor_copy(lo[:], m1[:]), R=["m1"], W=["lo"])
        P.add("dve", lambda: nc.vector.tensor_tensor(scores[0:T, q0:nk], scores[0:T, q0:nk], trin[:], ALU.add), R=DCK + ["trin", "t114"], W=DCK)
        P.add("dve", lambda: nc.vector.tensor_reduce(rmax[:], scores[0:T, 0:nk], AX.X, ALU.max), R=SCR, W=["rmax"])
        P.add("dve", lambda: nc.vector.tensor_tensor(rng[:], rmax[:], lo[:], ALU.subtract), R=["rmax", "lo"], W=["rng"])
        P.add("dve", lambda: nc.vector.tensor_scalar(steps[:], pw2[:], rng[:], None, ALU.mult), R=["pw2", "rng"], W=["steps"])
        return dict(g=g, t=t, qt=qt, q0=q0, nk=nk, tc=tc, SCR=SCR)

    def att_bis(c, act_share):
        qt, nk, SCR = c["qt"], c["nk"], c["SCR"]
        bi = qt % 2
        nmb = nm8[bi]
        kA, kB, kM = "nmA%d" % bi, "nmB%d" % bi, "nm%d" % bi
        split = nk >= 1400
        ca = (int(nk * (1.0 - act_share)) // 2) * 2 if split else nk
        if split:
            nact = nk - ca
            P.add("dve", lambda: nc.vector.tensor_scalar(kadj[:], kcnt[:, qt:qt + 1], float(-0.5 - 0.5 * nact), None, ALU.add), R=["kcnt"], W=["kadj"])
        for it in range(NBIS):
            P.add("dve", lambda it=it: nc.vector.tensor_tensor(mid[:], lo[:], steps[:, it:it + 1], ALU.add), R=["lo", "steps"], W=["mid"])
            if split:
                P.add("dve", lambda it=it: nc.vector.tensor_scalar(negmid[:], lo[:], steps[:, it:it + 1], -1.0, ALU.add, ALU.mult), R=["lo", "steps"], W=["negmid"])
                P.add("act", lambda: nc.scalar.activation(nmb[0:T, ca:nk], scores[0:T, ca:nk], AF.Sign, bias=negmid[:], scale=1.0, accum_out=cnta[:], saturate=False),
                      R=SCR + ["negmid"], W=[kB, "cnta"])
            P.add("dve", lambda: nc.vector.tensor_scalar(nmb[0:T, 0:ca], scores[0:T, 0:ca], mid[:], None, ALU.is_ge, ALU.add, accum_out=cntc[:], saturate=False),
                  R=SCR + ["mid"], W=[kA, "cntc"])
            if split:
                P.add("dve", lambda: nc.vector.scalar_tensor_tensor(ctmp[:], cnta[:], 0.5, cntc[:], ALU.mult, ALU.add), R=["cnta", "cntc"], W=["ctmp"])
                P.add("dve", lambda: nc.vector.tensor_tensor(gec[:], ctmp[:], kadj[:], ALU.is_ge), R=["ctmp", "kadj"], W=["gec"])
            else:
                P.add("dve", lambda: nc.vector.tensor_tensor(gec[:], cntc[:], kcnt[:, qt:qt + 1], ALU.is_ge), R=["cntc", "kcnt"], W=["gec"])
            P.add("dve", lambda it=it: nc.vector.scalar_tensor_tensor(lo[:], gec[:], steps[:, it:it + 1], lo[:], ALU.mult, ALU.add), R=["gec", "steps", "lo"], W=["lo"])
            yield
        P.add("dve", lambda: nc.vector.tensor_scalar(nmb[:, 0:nk], scores[:, 0:nk], lo128[:], None, ALU.is_lt, saturate=False), R=SCR + ["lo"], W=[kM, kA, kB])
        if debug and "sc" in debug and qt == debug.get("_qt", 0):
            sp_dma(dbg_d["sc"][:, 0:nk], scores[0:T, 0:nk], SCR, [])
            sp_dma(dbg_d["lo"][:, :], lo[:], ["lo"], [])

    def att_att(c):
        qt, t, tc = c["qt"], c["t"], c["tc"]
        bi = qt % 2
        nmb = nm8[bi]
        kM = "nm%d" % bi
        QK = [("qT", r) for r in range(4)]

        def QKm(kt):
            k0 = kt * T
            sset = kt % 2
            b0 = [0, 4][sset]
            for gq in range(2):
                b = b0 + gq
                P.add("pe", lambda: nc.tensor.matmul(ps[b][0:T, 0:G], kT[:, k0:k0 + T], qz[:, gq, :, tc], start=True, stop=False),
                      R=[("kT", k0 // G)] + QK, W=[("ps", b)])
                P.add("pe", lambda: nc.tensor.matmul(ps[b][0:T, 0:G], nmb[:, k0:k0 + T], I4[:], start=False, stop=True),
                      R=[kM, "I4"], W=[("ps", b)])
            P.add("act", lambda: nc.scalar.activation(ee[sset][0:T, :, :], psall[0:T, b0:b0 + 2, 0:G], AF.Exp, scale=0.125),
                  R=[("ps", b0), ("ps", b0 + 1)], W=[("ee", sset)])

        def PVm(kt):
            sset = kt % 2
            for gq in range(2):
                P.add("pe", lambda: nc.tensor.matmul(ps[6 + gq][0:65, 0:G], Vc[:, kt, gq * 65:(gq + 1) * 65], ee[sset][:, gq, :], start=(kt == 0), stop=(kt == qt)),
                      R=[("V", kt), ("ee", sset)], W=[("ps", 6 + gq)])
        QKm(0)
        for kt in range(qt + 1):
            if kt + 1 <= qt:
                QKm(kt + 1)
            PVm(kt)
            yield
        for gq in range(2):
            P.add("dve", lambda: nc.vector.reciprocal(rden[64:65, :], ps[6 + gq][64:65, 0:G]), R=[("ps", 6 + gq)], W=["rden"])
            P.add("pe", lambda: nc.tensor.matmul(ps[2][0:64, 0:G], ones32[64:65, 0:64], rden[64:65, :], start=True, stop=True),
                  R=["rden", "ones32"], W=[("ps", 2)])
            P.add("act", lambda: nc.scalar.copy(bcs[:], ps[2][0:64, 0:G]), R=[("ps", 2)], W=[("tmpb", 1)])
            P.add("dve", lambda: nc.vector.tensor_tensor(attT[:, gq * 4:(gq + 1) * 4, tc], ps[6 + gq][0:64, 0:G].rearrange("p (r t) -> p r t", r=4), bcs[:].rearrange("p (r t) -> p r t", r=4), ALU.mult),
                  R=[("ps", 6 + gq), ("tmpb", 1)], W=[("attT", gq * 4 + r) for r in range(4)])

    try:
        main()
        for g in range(NG):
            group(g)
    except _Stop:
        pass
    P.emit(stack)
    return nc, stack, P


def host_consts():
    half = 32
    inv_freq = (np.float32(10000.0) ** (-np.arange(half, dtype=np.float32) / np.float32(half))).astype(np.float32)
    pos = np.arange(L, dtype=np.float32)
    ang = (pos[:, None] * inv_freq[None, :]).astype(np.float32).astype(np.float64)
    cos = np.cos(ang).astype(np.float32)
    sin = np.sin(ang).astype(np.float32)
    cosT = np.zeros((128, L), np.float32)
    sinT = np.zeros((128, L), np.float32)
    for p in range(128):
        d = p % 64
        cosT[p] = cos[:, d % 32]
        sinT[p] = (-sin[:, d % 32]) if d < 32 else sin[:, d % 32]
    r = np.arange(T)
    trin = np.where(r[None, :] <= r[:, None], 0.0, -1e30).astype(np.float32)
    trip = np.where(r[None, :] <= r[:, None], 0.0, 1e30).astype(np.float32)
    posq = (np.arange(NTILE)[None, :] * T + r[:, None])
    kcnt = np.minimum(256, posq + 1).astype(np.float32)
    pw2 = np.tile((2.0 ** -(np.arange(NBIS) + 1.0)).astype(np.float32)[None, :], (T, 1))
    return dict(cosT=cosT, sinT=sinT, ident=np.eye(128, dtype=np.float32), trin=trin, trip=trip,
                kcnt=np.ascontiguousarray(kcnt), pw2=np.ascontiguousarray(pw2))


def make_in_maps(inputs, cores):
    c = host_consts()
    f = lambda a: np.ascontiguousarray(np.asarray(a, dtype=np.float32))
    shared = dict(
        meta=f(inputs["meta_tokens"]),
        f1g=f(inputs["ffn1_w_gate"][0]), f1u=f(inputs["ffn1_w_up"][0]), f1d=f(inputs["ffn1_w_down"][0]),
        f2g=f(inputs["ffn2_w_gate"][0]), f2u=f(inputs["ffn2_w_up"][0]), f2d=f(inputs["ffn2_w_down"][0]),
        win=f(inputs["w_in"][0]), wao=f(inputs["w_att_out"][0]), wco=f(inputs["w_conv_out"][0]), wo=f(inputs["w_o"][0]),
        ln1g=f(inputs["ln1_g"]), ln1b=f(inputs["ln1_b"]), ln2g=f(inputs["ln2_g"]), ln2b=f(inputs["ln2_b"]),
        ln3g=f(inputs["ln3_g"]), ln3b=f(inputs["ln3_b"]),
        convw=f(np.asarray(inputs["conv_w"])[0].reshape(3, 4, 128).transpose(2, 1, 0).reshape(128, 12)),
        **c)
    maps = []
    for b in cores:
        m = dict(shared)
        m["x"] = f(inputs["x"][b])
        maps.append(m)
    return maps


def kernel(**inputs):
    nc, stack, P = build(NG_FULL)
    cores = list(range(8))
    in_maps = make_in_maps(inputs, cores)
    with stack:
        res = run_bass_kernel_spmd(nc, in_maps, core_ids=cores)
    out = np.stack([np.asarray(r["out"], dtype=np.float32) for r in res.results], axis=0)
    return out
```

```python
import types
import numpy as np
from contextlib import ExitStack
import concourse.bass as bass
import concourse.mybir as mybir
from concourse.bass_utils import run_bass_kernel_spmd

F32 = mybir.dt.float32
BF16 = mybir.dt.bfloat16
F8 = mybir.dt.float8e5
ALU = mybir.AluOpType
AF = mybir.ActivationFunctionType
AX = mybir.AxisListType

T = 114
NTG = 4
G = T * NTG
NG_FULL = 18
L = 8208
NTILE = 72
D = 1024
KC = 8
DFF = 2816
NFC = 22
NMETA = 16
SEQ = 8192
ALPHA = float(2.0 ** 0.25)
EPS = 1e-5
NBIS = 16
NBIS_BLK = 14
NEGM = -28672.0
QO, QIO, KKO, CVO, GAO, GCO, VWO, WPC = 0, 1024, 2048, 2560, 4096, 5120, 6144, 6280
SQ, SK, SV, SQI, SKI, SWI, SU, SGB, SGC, SGA, SGV = 0, 512, 640, 768, 1280, 1344, 1352, 1864, 2376, 2888, 3912
NRING = 3
ND = 8


def _freeze(fn):
    if fn.__closure__ is None:
        return fn
    cells = []
    for c in fn.__closure__:
        try:
            cells.append(types.CellType(c.cell_contents))
        except ValueError:
            cells.append(c)
    return types.FunctionType(fn.__code__, fn.__globals__, fn.__name__, fn.__defaults__, tuple(cells))


class Prog:
    ENGS = ("pe", "act", "dve", "pool", "sp")

    def __init__(self, nc):
        self.nc = nc
        self.ops = []
        self.last_w = {}
        self.readers = {}

    def add(self, eng, fn, R=(), W=(), dma=False):
        i = len(self.ops)
        deps = set()
        for r in R:
            lw = self.last_w.get(r)
            if lw is not None:
                deps.add(lw)
        for w in W:
            lw = self.last_w.get(w)
            if lw is not None:
                deps.add(lw)
            rs = self.readers.get(w)
            if rs:
                deps |= rs
        for r in R:
            self.readers.setdefault(r, set()).add(i)
        for w in W:
            self.last_w[w] = i
            self.readers[w] = set()
        deps.discard(i)
        self.ops.append([eng, _freeze(fn), deps, dma])
        return i

    def transfer(self, src_keys, dst_keys):
        acc = set()
        for k in src_keys:
            lw = self.last_w.get(k)
            if lw is not None:
                acc.add(lw)
            acc |= self.readers.get(k, set())
        for k in dst_keys:
            self.readers.setdefault(k, set()).update(acc)

    def emit(self, stack):
        nc = self.nc
        ops = self.ops
        engobj = {"pe": nc.tensor, "act": nc.scalar, "dve": nc.vector, "pool": nc.gpsimd, "sp": nc.sync}
        n = len(ops)
        needed = [False] * n
        red = []
        for i, (eng, fn, deps, dma) in enumerate(ops):
            best = {}
            dl = []
            for d in deps:
                pe_, _, _, pdma = ops[d]
                if pdma:
                    dl.append(d)
                else:
                    if pe_ == "pe" and eng == "pe" and not dma:
                        continue
                    if best.get(pe_, -1) < d:
                        best[pe_] = d
            dl.extend(best.values())
            red.append(dl)
            for d in dl:
                needed[d] = True
        esem = {e: stack.enter_context(nc.semaphore("s_" + e)) for e in self.ENGS}
        dsem = {e: [stack.enter_context(nc.semaphore("d_%s%d" % (e, k))) for k in range(ND)] for e in self.ENGS}
        cnt = {e: 0 for e in self.ENGS}
        dcnt = {e: 0 for e in self.ENGS}
        dhist = {e: [] for e in self.ENGS}
        sig = [None] * n
        prevdma = [None] * n
        for i, (eng, fn, deps, dma) in enumerate(ops):
            if dma:
                k = dcnt[eng]
                dcnt[eng] += 1
                sig[i] = (dsem[eng][k % ND], 16 * (k // ND + 1))
                if k >= ND:
                    prevdma[i] = dhist[eng][k - ND]
                dhist[eng].append(i)
            elif needed[i]:
                cnt[eng] += 1
                sig[i] = (esem[eng], cnt[eng])
        seen = {e: {} for e in self.ENGS}
        for i, (eng, fn, deps, dma) in enumerate(ops):
            eo = engobj[eng]
            waits = {}
            dl = list(red[i])
            if prevdma[i] is not None:
                dl.append(prevdma[i])
            for d in dl:
                s, v = sig[d]
                key = id(s)
                if key not in waits or waits[key][1] < v:
                    waits[key] = (s, v)
            for key, (s, v) in waits.items():
                if seen[eng].get(key, 0) >= v:
                    continue
                eo.wait_ge(s, v)
                seen[eng][key] = v
            ins = fn()
            if dma:
                ins.then_inc(sig[i][0], 16)
            elif needed[i]:
                ins.then_inc(sig[i][0], 1)
        for e in self.ENGS:
            for i in dhist[e][-ND:]:
                s, v = sig[i]
                if seen["sp"].get(id(s), 0) < v:
                    nc.sync.wait_ge(s, v)
                    seen["sp"][id(s)] = v
        self.stats = dict(n_ops=n, cnt=cnt, dcnt=dcnt)


class _Stop(Exception):
    pass


def build(NG=NG_FULL, debug=None, stop=None):
    nc = bass.Bass("TRN2", target_bir_lowering=False)
    stack = ExitStack()
    P = Prog(nc)

    def chk(name):
        if stop == name:
            raise _Stop()

    def dram_in(name, shape):
        return nc.dram_tensor(name, list(shape), F32, kind="ExternalInput").ap()

    x_d = dram_in("x", [SEQ, D])
    meta_d = dram_in("meta", [NMETA, D])
    wsrc = {}
    for nm, shp in [("f1g", [D, DFF]), ("f1u", [D, DFF]), ("f1d", [DFF, D]),
                    ("f2g", [D, DFF]), ("f2u", [D, DFF]), ("f2d", [DFF, D]),
                    ("win", [D, 4936]), ("wao", [512, D]), ("wco", [512, D]), ("wo", [D, D])]:
        wsrc[nm] = dram_in(nm, shp)
    lnp = {nm: dram_in(nm, [1, D]) for nm in ["ln1g", "ln1b", "ln2g", "ln2b", "ln3g", "ln3b"]}
    convw_d = dram_in("convw", [128, 12])
    cos_d = dram_in("cosT", [128, L])
    sin_d = dram_in("sinT", [128, L])
    ident_d = dram_in("ident", [128, 128])
    trin_d = dram_in("trin", [T, T])
    trip_d = dram_in("trip", [T, T])
    kcnt_d = dram_in("kcnt", [T, NTILE])
    pw2_d = dram_in("pw2", [T, NBIS])
    out_d = nc.dram_tensor("out", [SEQ, D], F32, kind="ExternalOutput").ap()
    dbg_d = {}
    if debug:
        for nm, shp in debug.items():
            dbg_d[nm] = nc.dram_tensor("dbg_" + nm, list(shp), F32, kind="ExternalOutput").ap()

    def scratch(name, shape):
        return nc.dram_tensor(name, list(shape), BF16, kind="Internal").ap()

    wsc = {"f1g": scratch("s_f1g", [D, DFF]), "f1u": scratch("s_f1u", [D, DFF]), "f1d": scratch("s_f1d", [DFF, D]),
           "f2g": scratch("s_f2g", [D, DFF]), "f2u": scratch("s_f2u", [D, DFF]), "f2d": scratch("s_f2d", [DFF, D]),
           "wp": scratch("s_wp", [D, WPC]), "wao": scratch("s_wao", [512, D]), "wco": scratch("s_wco", [512, D]),
           "wo": scratch("s_wo", [D, D])}

    def sb(name, shape, dt=F32):
        return stack.enter_context(nc.sbuf_tensor("sb_" + name, list(shape), dt))

    kT = sb("kT", [128, L], BF16)
    kiT = sb("kiT", [128, L], BF16)
    Vc = sb("Vc", [128, NTILE, 130], BF16)
    S = sb("S", [T, NTG, D], F32)
    aT = sb("aT", [128, KC, G], BF16)
    xbf = [sb("xbf0", [T, D], BF16)] * 2
    hT = sb("hT", [128, NFC, G], BF16)
    qiz = sb("qiz", [128, 8, G], BF16)
    ycv = sb("ycv", [128, 4, G], BF16)
    attT = sb("attT", [64, 8, G], BF16)
    scores = sb("scores", [128, L], F32)
    ring = [sb("ring%d" % i, [128, 4096], BF16) for i in range(NRING)]
    ropec = sb("ropec", [128, G], F32)
    ropes = sb("ropes", [128, G], F32)
    gbq = sb("gbq", [128, 2 * D], F32)
    gb = gbq[0:T, :].rearrange("p (a d) -> p a d", a=2)
    qz = gbq[:].bitcast(BF16)[:, 0:2 * 4 * G].rearrange("p (g r t) -> p g r t", g=2, r=4)
    tmpa = [sb("tmpa%d" % i, [128, G], F32) for i in range(2)]
    sg = tmpa
    tmpb = [sb("tmpb%d" % i, [128, G], F32) for i in range(2)]
    bcs = tmpb[1][0:64, :]
    zt = sb("zt", [128, G + 2], F32)
    halo = sb("halo", [128, 4, 2], F32)
    convw = sb("convw", [128, 12], F32)
    identf = sb("identf", [128, 128], F32)
    identb = sb("identb", [128, 128], BF16)
    I4 = sb("I4", [128, G], F8)
    trin = sb("trin", [T, T], F32)
    trip = sb("trip", [T, T], F32)
    kcnt = sb("kcnt", [T, NTILE], F32)
    pw2 = sb("pw2", [T, NBIS], F32)
    ones32 = sb("ones32", [128, 64], F32)
    neghalf = sb("neghalf", [128, NTG], F32)
    st = sb("st", [T, NTG, 12], F32)
    mv = sb("mv", [T, NTG, 2], F32)
    ve = sb("ve", [T, NTG], F32)
    rstd = sb("rstd", [T, NTG], F32)
    wabs = sb("wabs", [T, NTG, 8], F32)
    dg = sb("dg", [128, 8, T], BF16)
    rr = [sb("rr%d" % i, [128, 2, 512], BF16) for i in range(3)]
    ee = [sb("ee%d" % i, [128, 2, G], BF16) for i in range(2)]
    t114 = sb("t114", [T, T], F32)
    mx = zt[0:T, 0:256]
    m8t = zt[0:T, 256:288]
    m0 = sb("m0", [T, 1], F32)
    m1 = sb("m1", [T, 1], F32)
    lo128 = sb("lo", [128, 1], F32)
    lo = lo128[0:T, :]
    rmax = sb("rmax", [T, 1], F32)
    rng = sb("rng", [T, 1], F32)
    steps = sb("steps", [T, NBIS], F32)
    mid = sb("mid", [T, 1], F32)
    cntc = sb("cntc", [T, 1], F32)
    cnta = sb("cnta", [T, 1], F32)
    negmid = sb("negmid", [T, 1], F32)
    kadj = sb("kadj", [T, 1], F32)
    ctmp = sb("ctmp", [T, 1], F32)
    gec = sb("gec", [T, 1], F32)
    rden = sb("rden", [128, G], F32)
    psall = stack.enter_context(nc.psum_tensor("psall", [128, 8, 512], F32))
    ps = [psall[:, b, :] for b in range(8)]
    ps7b = psall[:, 7, :].bitcast(BF16)
    h8 = hT[:].rearrange("p a b -> p (a b)").bitcast(F8)
    nm8 = [h8[:, 0:L], h8[:, L:2 * L]]

    PSK = [("ps", b) for b in range(8)]
    HTK = [("hT", f) for f in range(NFC)]

    sp_dma = lambda out, in_, R, W: P.add("sp", lambda: nc.sync.dma_start(out=out, in_=in_), R=R, W=W, dma=True)
    st_dma = lambda out, in_, R, W: P.add("pool", lambda: nc.gpsimd.dma_start(out=out, in_=in_), R=R, W=W, dma=True)

    wsc_barrier = []
    SCK = [("sc", c) for c in range(17)]

    def main():
        sp_dma(identf[:], ident_d, [], ["identf"])
        sp_dma(trin[:], trin_d, [], ["trin"])
        sp_dma(trip[:], trip_d, [], ["trip"])
        sp_dma(kcnt[:], kcnt_d, [], ["kcnt"])
        sp_dma(pw2[:], pw2_d, [], ["pw2"])
        sp_dma(convw[:], convw_d, [], ["convw"])
        P.add("dve", lambda: nc.vector.tensor_copy(identb[:], identf[:]), R=["identf"], W=["identb"])
        P.add("dve", lambda: nc.vector.memset(I4[:], 0.0), W=["I4"])
        for r in range(4):
            P.add("dve", lambda r=r: nc.vector.tensor_scalar(I4[0:T, r * T:(r + 1) * T], identf[0:T, 0:T], NEGM, None, ALU.mult, saturate=False), R=["identf"], W=["I4"])
        P.add("pool", lambda: nc.gpsimd.memset(qiz[:], 0.0), W=[("qiT", c) for c in range(4)])
        P.add("pool", lambda: nc.gpsimd.memset(lo128[:], 0.0), W=["lo"])
        P.add("pool", lambda: nc.gpsimd.memset(dg[:], 0.0), W=[("dg", h) for h in range(8)])
        for i in range(3):
            P.add("pool", lambda i=i: nc.gpsimd.memset(rr[i][:], 0.0), W=[("rr", i)])
        for i in range(2):
            P.add("pool", lambda i=i: nc.gpsimd.memset(ee[i][:], 0.0), W=[("ee", i)])
        P.add("pool", lambda: nc.gpsimd.memset(ones32[:], 1.0), W=["ones32"])
        P.add("pool", lambda: nc.gpsimd.memset(neghalf[:], -0.5), W=["neghalf"])
        P.add("pool", lambda: nc.gpsimd.memset(halo[:], 0.0), W=["halo"])
        P.add("pool", lambda: nc.gpsimd.memset(Vc[:], 0.0), W=[("V", t) for t in range(NTILE)])
        P.add("pool", lambda: nc.gpsimd.memset(Vc[0:T], 1.0), W=[("V", t) for t in range(NTILE)])

        chk('const')
        stage32 = [scores[:, 0:4104], scores[:, 4104:8208]]
        hflat = hT[:].rearrange("p a b -> p (a b)")
        stage16 = [hflat[:, 0:4104], hflat[:, 4104:8208]]
        ceng = ["act", "dve", "act", "dve", "pool"]
        cstate = [0]

        def cast(out, in_, R, W):
            e = ceng[cstate[0] % 5]
            cstate[0] += 1
            if e == "act":
                P.add("act", lambda: nc.scalar.copy(out, in_), R=R, W=W)
            elif e == "dve":
                P.add("dve", lambda: nc.vector.tensor_copy(out, in_), R=R, W=W)
            else:
                P.add("pool", lambda: nc.gpsimd.tensor_copy(out, in_), R=R, W=W)

        pj = [0]

        def prep_plain(src, dst, rows, cols):
            for rb in range(rows // 128):
                i = pj[0] % 2
                pj[0] += 1
                s32 = stage32[i][:, 0:cols]
                s16 = stage16[i][:, 0:cols]
                sp_dma(s32, src[rb * 128:(rb + 1) * 128, :], [], [("p32", i)])
                h = cols // 2
                cast(s16[:, 0:h], s32[:, 0:h], [("p32", i)], [("p16", i, 0)])
                cast(s16[:, h:cols], s32[:, h:cols], [("p32", i)], [("p16", i, 1)])
                st_dma(dst[rb * 128:(rb + 1) * 128, :], s16, [("p16", i, 0), ("p16", i, 1)], [])

        for nm in ["f1g", "f1u"]:
            prep_plain(wsrc[nm], wsc[nm], D, DFF)
        prep_plain(wsrc["f1d"], wsc["f1d"], DFF, D)

        s32 = scores[:, 0:4936]
        s16 = hflat[:, 0:WPC]
        for rb in range(8):
            K32 = [("p32", 0), ("p32", 1)]
            K16 = [("p16", 0, 0), ("p16", 0, 1), ("p16", 1, 0), ("p16", 1, 1)]
            sp_dma(s32, wsrc["win"][rb * 128:(rb + 1) * 128, :], [], K32)
            if rb == 0:
                P.add("pool", lambda: nc.gpsimd.memset(ve[:], 0.0), R=[], W=K16 + ["wpbar", "ve"])

            def cp(dst, src, tag):
                cast(dst, src, K32 + ["wpbar"], [("wpw", tag)])
            qsrc = s32[:, SQ:SQ + 512].rearrange("p (j a e) -> p a j e", j=2, a=4, e=64)
            for c2 in range(2):
                dA = s16[:, QO + c2 * 512: QO + c2 * 512 + 256].rearrange("p (a j e) -> p a j e", a=2, j=2, e=64)
                dB = s16[:, QO + c2 * 512 + 256: QO + c2 * 512 + 512].rearrange("p (a j e) -> p a j e", a=2, j=2, e=64)
                for a in range(2):
                    cp(dA[:, a], qsrc[:, 2 * c2 + a], ("qA", c2, a))
                    for hf in range(2):
                        cp(dB[:, a, :, hf * 32:(hf + 1) * 32], qsrc[:, 2 * c2 + a, :, (1 - hf) * 32:(2 - hf) * 32], ("qB", c2, a, hf))
            for c2 in range(2):
                cp(s16[:, QIO + c2 * 512: QIO + c2 * 512 + 256], s32[:, SQI + c2 * 256: SQI + c2 * 256 + 256], ("qiA", c2))
                dB = s16[:, QIO + c2 * 512 + 256: QIO + c2 * 512 + 512].rearrange("p (h f e) -> p h f e", h=4, f=2, e=32)
                sB = s32[:, SQI + c2 * 256: SQI + c2 * 256 + 256].rearrange("p (h f e) -> p h f e", h=4, f=2, e=32)
                for hf in range(2):
                    cp(dB[:, :, hf], sB[:, :, 1 - hf], ("qiB", c2, hf))
            cp(s16[:, KKO:KKO + 128], s32[:, SK:SK + 128], "k")
            dB = s16[:, KKO + 128:KKO + 256].rearrange("p (h f e) -> p h f e", h=2, f=2, e=32)
            sB = s32[:, SK:SK + 128].rearrange("p (h f e) -> p h f e", h=2, f=2, e=32)
            for hf in range(2):
                cp(dB[:, :, hf], sB[:, :, 1 - hf], ("kB", hf))
            for cpy in range(2):
                cp(s16[:, KKO + 256 + cpy * 64: KKO + 256 + cpy * 64 + 64], s32[:, SKI:SKI + 64], ("ki", cpy))
                for hf in range(2):
                    cp(s16[:, KKO + 384 + cpy * 64 + hf * 32: KKO + 384 + cpy * 64 + hf * 32 + 32],
                       s32[:, SKI + (1 - hf) * 32: SKI + (1 - hf) * 32 + 32], ("kiB", cpy, hf))
            dC = s16[:, CVO:CVO + 1536].rearrange("p (c m e) -> p c m e", c=4, m=3, e=128)
            for m, so in enumerate([SU, SGC, SGB]):
                cp(dC[:, :, m], s32[:, so:so + 512].rearrange("p (c e) -> p c e", c=4, e=128), ("cv", m))
            cp(s16[:, GAO:GAO + 1024], s32[:, SGA:SGA + 1024], "ga")
            cp(s16[:, GCO:GCO + 1024], s32[:, SGV:SGV + 1024], "gc")
            cp(s16[:, VWO:VWO + 128], s32[:, SV:SV + 128], "v")
            cp(s16[:, VWO + 128:VWO + 136], s32[:, SWI:SWI + 8], "wi")
            tags = [k for k in P.last_w.keys() if isinstance(k, tuple) and k and k[0] == "wpw"]
            st_dma(wsc["wp"][rb * 128:(rb + 1) * 128, :], s16, tags, K16)

        for nm, rows in [("wao", 512), ("wco", 512), ("wo", D)]:
            prep_plain(wsrc[nm], wsc[nm], rows, D)
        for nm in ["f2g", "f2u"]:
            prep_plain(wsrc[nm], wsc[nm], D, DFF)
        prep_plain(wsrc["f2d"], wsc["f2d"], DFF, D)
        allprep = [("p32", 0), ("p32", 1), ("p16", 0, 0), ("p16", 0, 1), ("p16", 1, 0), ("p16", 1, 1)] + \
            [k for k in P.last_w.keys() if isinstance(k, tuple) and k and k[0] == "wpw"]
        P.transfer(allprep, HTK + SCK + ["nm0", "nm1", "nmA0", "nmA1", "nmB0", "nmB1"])
        prep_stores = [i for i, o in enumerate(P.ops) if o[3]]
        wsc_barrier.extend(prep_stores)


    rstate = [0]
    first_loads = [True]

    def ring_load(parts):
        s = rstate[0] % NRING
        rstate[0] += 1
        for (vf, src, hk) in parts:
            hks = hk if isinstance(hk, tuple) else (hk,)
            i = sp_dma(vf(ring[s]), src, [], [("rg", s, h_) for h_ in hks])
            P.ops[i][2].update(wsc_barrier)
        return s

    def v3(lo_, n, kc):
        return lambda r: r[:, lo_:lo_ + n * kc].rearrange("p (k f) -> p k f", k=kc)

    def to_featmajor(t):
        xb = xbf[0]
        P.add("act", lambda: nc.scalar.copy(xb[:], S[:, t, :]), R=[("S", t)], W=[("xbf", 0)])
        for kc in range(KC):
            P.add("pe", lambda kc=kc: nc.tensor.transpose(ps7b[:, kc * T:(kc + 1) * T], xb[:, kc * 128:(kc + 1) * 128], identb[0:T, 0:T]),
                  R=[("xbf", 0), "identb"], W=[("ps", 7)])
        P.add("dve", lambda: nc.vector.tensor_copy(aT[:, :, t * T:(t + 1) * T], ps7b[:, 0:KC * T].rearrange("p (k t) -> p k t", k=KC)),
              R=[("ps", 7)], W=[("aT", t)])

    ATK = [("aT", t) for t in range(NTG)]

    def ffn(wg, wu, wd, resid_scale):
        for j in range(11):
            s = ring_load([(v3(0, 256, KC), wg[:, 256 * j:256 * j + 256].rearrange("(k p) f -> p k f", p=128), 0),
                           (v3(2048, 256, KC), wu[:, 256 * j:256 * j + 256].rearrange("(k p) f -> p k f", p=128), 1)])
            gv = v3(0, 256, KC)(ring[s])
            uv = v3(2048, 256, KC)(ring[s])
            for f2 in range(2):
                fc = 2 * j + f2
                bg = fc % 2
                bu = 2 + fc % 2
                for kc in range(KC):
                    P.add("pe", lambda kc=kc, f2=f2, bg=bg: nc.tensor.matmul(ps[bg][:, 0:G], gv[:, kc, f2 * 128:(f2 + 1) * 128], aT[:, kc, :], start=(kc == 0), stop=(kc == KC - 1)),
                          R=[("rg", s, 0)] + ATK, W=[("ps", bg)])
                for kc in range(KC):
                    P.add("pe", lambda kc=kc, f2=f2, bu=bu: nc.tensor.matmul(ps[bu][:, 0:G], uv[:, kc, f2 * 128:(f2 + 1) * 128], aT[:, kc, :], start=(kc == 0), stop=(kc == KC - 1)),
                          R=[("rg", s, 1)] + ATK, W=[("ps", bu)])
                sgt = sg[fc % 2]
                P.add("act", lambda bg=bg, sgt=sgt: nc.scalar.activation(sgt[:], ps[bg][:, 0:G], AF.Silu), R=[("ps", bg)], W=[("tmpa", fc % 2)])
                P.add("dve", lambda bu=bu, sgt=sgt, fc=fc: nc.vector.scalar_tensor_tensor(hT[:, fc, :], sgt[:], resid_scale, ps[bu][:, 0:G], ALU.mult, ALU.mult),
                      R=[("tmpa", fc % 2), ("ps", bu)], W=[("hT", fc)])
        for j in range(6):
            nf = 4 if j < 5 else 2
            s = ring_load([(lambda r, nf=nf: r[:, 0:nf * 1024].rearrange("p (c d) -> p c d", c=nf),
                            wd[512 * j:512 * j + 128 * nf, :].rearrange("(c p) d -> p c d", p=128), (0, 1))])
            wv = ring[s][:, 0:nf * 1024].rearrange("p (c d) -> p c d", c=nf)
            for c in range(nf):
                fc = 4 * j + c
                for t in range(NTG):
                    for dh in range(2):
                        b = t * 2 + dh
                        P.add("pe", lambda c=c, fc=fc, t=t, dh=dh, b=b, wv=wv: nc.tensor.matmul(ps[b][0:T, :], hT[:, fc, t * T:(t + 1) * T], wv[:, c, dh * 512:(dh + 1) * 512], start=(fc == 0), stop=(fc == NFC - 1)),
                              R=[("rg", s, 0), ("rg", s, 1), ("hT", fc)], W=[("ps", b)])
        for t in range(NTG):
            for dh in range(2):
                b = t * 2 + dh
                P.add("dve", lambda t=t, dh=dh, b=b: nc.vector.scalar_tensor_tensor(S[:, t, dh * 512:(dh + 1) * 512], S[:, t, dh * 512:(dh + 1) * 512], ALPHA, ps[b][0:T, :], ALU.mult, ALU.add),
                      R=[("ps", b), ("S", t)], W=[("S", t)])

    def load_gb(gname, bname):
        sp_dma(gb[:, 0, :], lnp[gname].partition_broadcast(T), [], ["gb"])
        sp_dma(gb[:, 1, :], lnp[bname].partition_broadcast(T), [], ["gb2"])

    def layernorm_all(after=None):
        for t in range(NTG):
            for hh in range(2):
                P.add("dve", lambda t=t, hh=hh: nc.vector.bn_stats(st[:, t, hh * 6:(hh + 1) * 6], S[:, t, hh * 512:(hh + 1) * 512]), R=[("S", t)], W=[("st", t, hh)])
            P.add("dve", lambda t=t: nc.vector.bn_aggr(mv[:, t, :], st[:, t, :]), R=[("st", t, 0), ("st", t, 1)], W=[("mv", t)])
        MVK = [("mv", t) for t in range(NTG)]
        P.add("pool", lambda: nc.gpsimd.tensor_scalar(ve[:], mv[:, :, 1], EPS, None, ALU.add), R=MVK, W=["ve"])
        P.add("pool", lambda: nc.gpsimd.tensor_tensor(rstd[:], ve[:], neghalf[0:T, :], ALU.pow), R=["ve", "neghalf"], W=["rstd"])
        for t in range(NTG):
            St = S[:, t, :]
            P.add("dve", lambda t=t, St=St: nc.vector.scalar_tensor_tensor(St, St, mv[:, t, 0:1], gb[:, 0, :], ALU.subtract, ALU.mult), R=[("S", t), ("mv", t), "gb"], W=[("S", t)])
            P.add("dve", lambda t=t, St=St: nc.vector.scalar_tensor_tensor(St, St, rstd[:, t:t + 1], gb[:, 1, :], ALU.mult, ALU.add), R=[("S", t), "rstd", "gb2"], W=[("S", t)])
            if after is not None:
                after(t)

    def wp_piece(col0, ncols):
        return ring_load([(v3(0, ncols, KC), wsc["wp"][:, col0:col0 + ncols].rearrange("(k p) f -> p k f", p=128), (0, 1))])

    def proj_fm(s, off, bank):
        wv = v3(0, 512, KC)(ring[s])
        for kc in range(KC):
            P.add("pe", lambda kc=kc: nc.tensor.matmul(ps[bank][:, 0:G], wv[:, kc, off:off + 128], aT[:, kc, :], start=(kc == 0), stop=(kc == KC - 1)),
                  R=[("rg", s, 0), ("rg", s, 1)] + ATK, W=[("ps", bank)])

    pairctr = [0]

    def rope_chunk(s, offA, offB, dst, dstR, dstW):
        i = pairctr[0] % 2
        pairctr[0] += 1
        bA, bB = 2 * i, 2 * i + 1
        proj_fm(s, offA, bA)
        proj_fm(s, offB, bB)
        ta, tb = tmpa[i], tmpb[i]
        P.add("dve", lambda: nc.vector.tensor_tensor(ta[:], ps[bA][:, 0:G], ropec[:], ALU.mult), R=[("ps", bA), "ropec"], W=[("tmpa", i)])
        P.add("dve", lambda: nc.vector.tensor_tensor(tb[:], ps[bB][:, 0:G], ropes[:], ALU.mult), R=[("ps", bB), "ropes"], W=[("tmpb", i)])
        if isinstance(dst, tuple):
            P.add("dve", lambda: nc.vector.tensor_tensor(dst[0], ta[0:64, :], tb[0:64, :], ALU.add), R=[("tmpa", i), ("tmpb", i)] + dstR, W=dstW)
            P.add("dve", lambda: nc.vector.tensor_tensor(dst[1], ta[64:128, :], tb[64:128, :], ALU.add), R=[("tmpa", i), ("tmpb", i)] + dstR, W=dstW)
        else:
            P.add("dve", lambda: nc.vector.tensor_tensor(dst, ta[:], tb[:], ALU.add), R=[("tmpa", i), ("tmpb", i)] + dstR, W=dstW)

    def group(g):
        p0 = g * G
        for t in range(NTG):
            pos0 = p0 + t * T
            if pos0 == 0:
                sp_dma(S[0:NMETA, t, :], meta_d, [], [("S", t)])
                sp_dma(S[NMETA:T, t, :], x_d[0:T - NMETA, :], [], [("S", t)])
            else:
                sp_dma(S[:, t, :], x_d[pos0 - NMETA:pos0 - NMETA + T, :], [], [("S", t)])
        for t in range(NTG):
            to_featmajor(t)
        chk('load')
        load_gb("ln1g", "ln1b")
        ffn(wsc["f1g"], wsc["f1u"], wsc["f1d"], 0.5)
        chk('ffn1')
        layernorm_all()
        if debug and "h1" in debug and g == 0:
            for t in range(NTG):
                sp_dma(dbg_d["h1"][t * T:(t + 1) * T, :], S[:, t, :], [("S", t)], [])
        for t in range(NTG):
            to_featmajor(t)
        chk('ln1')
        QTK = [("qT", r) for r in range(4)]
        P.transfer(["gb", "gb2"], QTK)
        P.add("pool", lambda: nc.gpsimd.memset(qz[64:128, 0], 0.0), W=QTK)
        P.add("pool", lambda: nc.gpsimd.memset(qz[0:64, 1], 0.0), W=QTK)
        sp_dma(ropec[:], cos_d[:, p0:p0 + G], [], ["ropec"])
        sp_dma(ropes[:], sin_d[:, p0:p0 + G], [], ["ropes"])
        for c2 in range(2):
            s = wp_piece(QIO + c2 * 512, 512)
            for a in range(2):
                c = 2 * c2 + a
                rope_chunk(s, a * 128, 256 + a * 128, (qiz[0:64, 2 * c, :], qiz[64:128, 2 * c + 1, :]), [], [("qiT", c)])
        chk('projq')
        s = wp_piece(KKO, 512)
        rope_chunk(s, 0, 128, kT[:, p0:p0 + G], [], [("kT", g)])
        rope_chunk(s, 256, 384, kiT[:, p0:p0 + G], [], [("kiT", g)])
        chk('projc')
        s = ring_load([(v3(0, 136, KC), wsc["wp"][:, VWO:VWO + 136].rearrange("(k p) f -> p k f", p=128), 0)])
        wv = v3(0, 136, KC)(ring[s])
        for t in range(NTG):
            qt = g * NTG + t
            for kc in range(KC):
                P.add("pe", lambda kc=kc, t=t: nc.tensor.matmul(ps[7][0:T, 0:136], aT[:, kc, t * T:(t + 1) * T], wv[:, kc, :], start=(kc == 0), stop=(kc == KC - 1)),
                      R=[("rg", s, 0), ("aT", t)], W=[("ps", 7)])
            chk('pv1')
            P.add("act", lambda t=t, qt=qt: nc.scalar.copy(Vc[0:T, qt, :].rearrange("p (g e) -> p g e", g=2)[:, :, 0:64], ps[7][0:T, 0:128].rearrange("p (g e) -> p g e", g=2)),
                  R=[("ps", 7)], W=[("V", qt)])
            chk('pv2')
            P.add("act", lambda t=t: nc.scalar.activation(wabs[:, t, :], ps[7][0:T, 128:136], AF.Copy, scale=float(512 ** -0.5)), R=[("ps", 7)], W=[("wabs", t)])
        def late_q(c2):
            s = wp_piece(QO + c2 * 512, 512)
            for a in range(2):
                r = 2 * c2 + a
                rope_chunk(s, a * 128, 256 + a * 128, (qz[0:64, 0, r, :], qz[64:128, 1, r, :]), [], [("qT", r)])

        def late_conv(sl, c):
            offs = [(3 * c + m) * 128 for m in range(3)]
            for m in range(3):
                proj_fm(sl[offs[m] // 512], offs[m] % 512, 4 + m)
            P.add("act", lambda: nc.scalar.copy(tmpa[0][:], ps[4][:, 0:G]), R=[("ps", 4)], W=[("tmpa", 0)])
            P.add("pool", lambda c=c: nc.gpsimd.tensor_copy(zt[:, 0:2], halo[:, c, :]), R=["halo"], W=["zt0"])
            P.add("dve", lambda: nc.vector.tensor_tensor(zt[:, 2:G + 2], tmpa[0][:], ps[5][:, 0:G], ALU.mult), R=[("tmpa", 0), ("ps", 5)], W=["zt"])
            P.add("pool", lambda c=c: nc.gpsimd.tensor_copy(halo[:, c, :], zt[:, G:G + 2]), R=["zt", "zt0"], W=["halo"])
            P.add("dve", lambda c=c: nc.vector.tensor_scalar(tmpb[0][:], zt[:, 2:G + 2], convw[:, c * 3 + 2:c * 3 + 3], None, ALU.mult), R=["zt", "convw"], W=[("tmpb", 0)])
            P.add("dve", lambda c=c: nc.vector.scalar_tensor_tensor(tmpb[0][:], zt[:, 1:G + 1], convw[:, c * 3 + 1:c * 3 + 2], tmpb[0][:], ALU.mult, ALU.add), R=["zt", "zt0", "convw", ("tmpb", 0)], W=[("tmpb", 0)])
            P.add("dve", lambda c=c: nc.vector.scalar_tensor_tensor(tmpb[0][:], zt[:, 0:G], convw[:, c * 3:c * 3 + 1], tmpb[0][:], ALU.mult, ALU.add), R=["zt", "zt0", "convw", ("tmpb", 0)], W=[("tmpb", 0)])
            P.add("dve", lambda c=c: nc.vector.tensor_tensor(ycv[:, c, :], tmpb[0][:], ps[6][:, 0:G], ALU.mult), R=[("tmpb", 0), ("ps", 6)], W=[("ycv", c)])

        chk('proj')
        P.transfer(HTK, ["nm0", "nm1", "nmA0", "nmA1", "nmB0", "nmB1"])
        ctx = [att_idx(g, 0)]
        ga = att_bis(ctx[0], 0.5)
        slh = []

        def conv_job(c):
            if not slh:
                slh.extend([wp_piece(CVO + i * 512, 512) for i in range(3)])
            late_conv(slh, c)
        work = [lambda: late_q(0), lambda: late_q(1)] + [lambda c=c: conv_job(c) for c in range(4)]
        for i in range(ctx[0]["nbis"]):
            next(ga)
            if i % 2 == 1 and work:
                work.pop(0)()
        for _ in ga:
            pass
        while work:
            work.pop(0)()
        for t in range(1, NTG):
            ctx.append(att_idx(g, t))
            ga = att_bis(ctx[t], 0.38)
            gb_ = att_att(ctx[t - 1])
            nb = ctx[t - 1]["qt"] + 1
            done = 0
            head = min(nb, max(1, nb // 8))
            while done < head:
                next(gb_)
                done += 1
            nbi = ctx[t]["nbis"]
            for i in range(nbi):
                next(ga)
                tgt = head + ((i + 1) * (nb - head)) // nbi
                while done < tgt:
                    next(gb_)
                    done += 1
            for _ in ga:
                pass
            for _ in gb_:
                pass
        for _ in att_att(ctx[NTG - 1]):
            pass
        P.transfer(["nm0", "nm1", "nmA0", "nmA1", "nmB0", "nmB1"], HTK)
        P.transfer([("qT", r) for r in range(4)], ["gb", "gb2"])
        chk('att')
        load_gb("ln2g", "ln2b")
        ATT = [("attT", h) for h in range(8)]
        for qd in range(4):
            c0q = qd * 256
            sX = ring_load([(lambda r: r[0:64, 0:2048].rearrange("p (h d) -> p h d", h=8), wsc["wao"][:, c0q:c0q + 256].rearrange("(h p) d -> p h d", p=64), 0),
                            (lambda r: r[:, 2048:3072].rearrange("p (c d) -> p c d", c=4), wsc["wco"][:, c0q:c0q + 256].rearrange("(c p) d -> p c d", p=128), 1)])
            sY = ring_load([(v3(0, 256, KC), wsc["wp"][:, GAO + c0q:GAO + c0q + 256].rearrange("(k p) f -> p k f", p=128), 0),
                            (v3(2048, 256, KC), wsc["wp"][:, GCO + c0q:GCO + c0q + 256].rearrange("(k p) f -> p k f", p=128), 1)])
            wa = ring[sX][0:64, 0:2048].rearrange("p (h d) -> p h d", h=8)
            wc = ring[sX][:, 2048:3072].rearrange("p (c d) -> p c d", c=4)
            wga = v3(0, 256, KC)(ring[sY])
            wgc = v3(2048, 256, KC)(ring[sY])
            for d2 in range(2):
                dc = qd * 2 + d2
                bo = 4 * (dc % 2)
                cs = slice(d2 * 128, (d2 + 1) * 128)
                for h in range(8):
                    P.add("pe", lambda: nc.tensor.matmul(ps[bo][:, 0:G], wa[:, h, cs], attT[:, h, :], start=(h == 0), stop=(h == 7)),
                          R=[("rg", sX, 0)] + ATT, W=[("ps", bo)])
                for c in range(4):
                    P.add("pe", lambda: nc.tensor.matmul(ps[bo + 1][:, 0:G], wc[:, c, cs], ycv[:, c, :], start=(c == 0), stop=(c == 3)),
                          R=[("rg", sX, 1)] + [("ycv", cc) for cc in range(4)], W=[("ps", bo + 1)])
                for kc in range(KC):
                    P.add("pe", lambda: nc.tensor.matmul(ps[bo + 2][:, 0:G], wga[:, kc, cs], aT[:, kc, :], start=(kc == 0), stop=(kc == KC - 1)),
                          R=[("rg", sY, 0)] + ATK, W=[("ps", bo + 2)])
                for kc in range(KC):
                    P.add("pe", lambda: nc.tensor.matmul(ps[bo + 3][:, 0:G], wgc[:, kc, cs], aT[:, kc, :], start=(kc == 0), stop=(kc == KC - 1)),
                          R=[("rg", sY, 1)] + ATK, W=[("ps", bo + 3)])
                i = dc % 2
                P.add("act", lambda: nc.scalar.activation(tmpa[i][:], ps[bo + 2][:, 0:G], AF.Sigmoid), R=[("ps", bo + 2)], W=[("tmpa", i)])
                P.add("act", lambda: nc.scalar.activation(tmpb[i][:], ps[bo + 3][:, 0:G], AF.Sigmoid), R=[("ps", bo + 3)], W=[("tmpb", i)])
                P.add("dve", lambda: nc.vector.tensor_tensor(tmpa[i][:], tmpa[i][:], ps[bo][:, 0:G], ALU.mult), R=[("tmpa", i), ("ps", bo)], W=[("tmpa", i)])
                P.add("dve", lambda: nc.vector.tensor_tensor(tmpb[i][:], tmpb[i][:], ps[bo + 1][:, 0:G], ALU.mult), R=[("tmpb", i), ("ps", bo + 1)], W=[("tmpb", i)])
                P.add("dve", lambda: nc.vector.tensor_tensor(hT[:, dc, :], tmpa[i][:], tmpb[i][:], ALU.add), R=[("tmpa", i), ("tmpb", i)], W=[("hT", dc)])
        for dh in range(2):
            s = ring_load([(v3(0, 512, KC), wsc["wo"][:, dh * 512:(dh + 1) * 512].rearrange("(k p) f -> p k f", p=128), (0, 1))])
            wv = v3(0, 512, KC)(ring[s])
            for t in range(NTG):
                b = (dh * NTG + t) % 8
                for kc in range(KC):
                    P.add("pe", lambda kc=kc, t=t, b=b, wv=wv: nc.tensor.matmul(ps[b][0:T, :], hT[:, kc, t * T:(t + 1) * T], wv[:, kc, :], start=(kc == 0), stop=(kc == KC - 1)),
                          R=[("rg", s, 0), ("rg", s, 1)] + [("hT", k) for k in range(8)], W=[("ps", b)])
                P.add("dve", lambda t=t, dh=dh, b=b: nc.vector.scalar_tensor_tensor(S[:, t, dh * 512:(dh + 1) * 512], S[:, t, dh * 512:(dh + 1) * 512], ALPHA, ps[b][0:T, :], ALU.mult, ALU.add),
                      R=[("ps", b), ("S", t)], W=[("S", t)])
        chk('merge')
        layernorm_all()
        if debug and "h2" in debug and g == 0:
            for t in range(NTG):
                sp_dma(dbg_d["h2"][t * T:(t + 1) * T, :], S[:, t, :], [("S", t)], [])
        for t in range(NTG):
            to_featmajor(t)
        chk('ln2')
        load_gb("ln3g", "ln3b")
        ffn(wsc["f2g"], wsc["f2u"], wsc["f2d"], 0.5)

        def store(t):
            pos0 = p0 + t * T
            if pos0 == 0:
                st_dma(out_d[0:T - NMETA, :], S[NMETA:T, t, :], [("S", t)], [])
            else:
                st_dma(out_d[pos0 - NMETA:pos0 - NMETA + T, :], S[:, t, :], [("S", t)], [])
        layernorm_all(after=store)

    def att_idx(g, t):
        qt = g * NTG + t
        q0 = qt * T
        nk = q0 + T
        tc = slice(t * T, (t + 1) * T)
        for h in range(8):
            P.add("pool", lambda h=h: nc.gpsimd.tensor_scalar(dg[0:T, h, :], identb[0:T, 0:T], wabs[:, t, h:h + 1], None, ALU.mult),
                  R=["identb", ("wabs", t)], W=[("dg", h)])
        nch = (nk + 511) // 512
        items = [(ci, hp) for ci in range(nch) for hp in range(4)]

        def geom(ci):
            c0 = ci * 512
            c1 = min(nk, c0 + 512)
            return c0, c1, c1 - c0

        def X(j):
            ci, hp = items[j]
            c0, c1, w = geom(ci)
            b0 = [0, 4, 6][j % 3]
            ri = j % 3
            KIK = [("kiT", gg) for gg in range(c0 // G, (c1 - 1) // G + 1)]
            for hh in range(2):
                h = 2 * hp + hh
                P.add("pe", lambda: nc.tensor.matmul(ps[b0 + hh][0:T, 0:w], qiz[:, h, tc], kiT[:, c0:c1], start=True, stop=True),
                      R=[("qiT", hp)] + KIK, W=[("ps", b0 + hh)])
            if j % 8 in (1, 4, 6):
                P.add("dve", lambda: nc.vector.tensor_scalar(rr[ri][0:T, :, 0:w], psall[0:T, b0:b0 + 2, 0:w], 0.0, None, ALU.max),
                      R=[("ps", b0), ("ps", b0 + 1)], W=[("rr", ri)])
            else:
                P.add("act", lambda: nc.scalar.activation(rr[ri][0:T, :, 0:w], psall[0:T, b0:b0 + 2, 0:w], AF.Relu),
                      R=[("ps", b0), ("ps", b0 + 1)], W=[("rr", ri)])

        def A(j):
            ci, hp = items[j]
            c0, c1, w = geom(ci)
            pacc = 2 + ci % 2
            ri = j % 3
            for hh in range(2):
                h = 2 * hp + hh
                P.add("pe", lambda: nc.tensor.matmul(ps[pacc][0:T, 0:w], dg[:, h, :], rr[ri][:, hh, 0:w], start=(h == 0), stop=(h == 7)),
                      R=[("dg", h), ("rr", ri)], W=[("ps", pacc)])
            if hp == 3:
                if ci % 2 == 0:
                    P.add("act", lambda: nc.scalar.copy(scores[0:T, c0:c1], ps[pacc][0:T, 0:w]), R=[("ps", pacc)], W=[("sc", ci)])
                else:
                    P.add("dve", lambda: nc.vector.tensor_copy(scores[0:T, c0:c1], ps[pacc][0:T, 0:w]), R=[("ps", pacc)], W=[("sc", ci)])
        LA = 2
        for j in range(min(LA, len(items))):
            X(j)
        for j in range(len(items)):
            if j + LA < len(items):
                X(j + LA)
            A(j)
        SCR = [("sc", ci) for ci in range(nch)]
        dci = sorted(set([q0 // 512, (nk - 1) // 512]))
        DCK = [("sc", ci) for ci in dci]
        blk = q0 >= 1024
        if not blk:
            P.add("dve", lambda: nc.vector.tensor_tensor(t114[:], scores[0:T, q0:nk], trip[:], ALU.add), R=DCK + ["trip"], W=["t114"])
            P.add("dve", lambda: nc.vector.tensor_reduce(m1[:], t114[:], AX.X, ALU.min), R=["t114"], W=["m1"])
            if q0 > 0:
                P.add("dve", lambda: nc.vector.tensor_reduce(m0[:], scores[0:T, 0:q0], AX.X, ALU.min), R=SCR, W=["m0"])
                P.add("dve", lambda: nc.vector.tensor_tensor(lo[:], m0[:], m1[:], ALU.min), R=["m0", "m1"], W=["lo"])
            else:
                P.add("dve", lambda: nc.vector.tensor_copy(lo[:], m1[:]), R=["m1"], W=["lo"])
            P.add("dve", lambda: nc.vector.tensor_tensor(scores[0:T, q0:nk], scores[0:T, q0:nk], trin[:], ALU.add), R=DCK + ["trin", "t114"], W=DCK)
            P.add("dve", lambda: nc.vector.tensor_reduce(rmax[:], scores[0:T, 0:nk], AX.X, ALU.max), R=SCR, W=["rmax"])
        else:
            bs = q0 // 32
            P.add("dve", lambda: nc.vector.tensor_tensor(scores[0:T, q0:nk], scores[0:T, q0:nk], trin[:], ALU.add), R=DCK + ["trin"], W=DCK)
            for bk in range(32):
                P.add("dve", lambda bk=bk: nc.vector.max(out=mx[:, bk * 8:(bk + 1) * 8], in_=scores[0:T, bk * bs:(bk + 1) * bs]), R=SCR, W=[("mx", bk), "zt", "zt0"] if bk == 0 else [("mx", bk)])
            MXK = [("mx", bk) for bk in range(32)]
            P.add("dve", lambda: nc.vector.tensor_reduce(m8t, mx.rearrange("p (b e) -> p b e", e=8), AX.X, ALU.min), R=MXK + ["zt", "zt0"], W=["m8t"])
            P.add("dve", lambda: nc.vector.tensor_reduce(lo[:], m8t, AX.X, ALU.min), R=["m8t", "zt", "zt0"], W=["lo"])
            P.add("dve", lambda: nc.vector.tensor_reduce(m0[:], m8t, AX.X, ALU.max), R=["m8t", "zt", "zt0"], W=["m0"])
            P.add("dve", lambda: nc.vector.tensor_reduce(m1[:], scores[0:T, 32 * bs:nk], AX.X, ALU.max), R=SCR, W=["m1"])
            P.add("dve", lambda: nc.vector.tensor_tensor(rmax[:], m0[:], m1[:], ALU.max), R=["m0", "m1"], W=["rmax"])
        P.add("dve", lambda: nc.vector.tensor_tensor(rng[:], rmax[:], lo[:], ALU.subtract), R=["rmax", "lo"], W=["rng"])
        P.add("dve", lambda: nc.vector.tensor_scalar(steps[:], pw2[:], rng[:], None, ALU.mult), R=["pw2", "rng"], W=["steps"])
        return dict(g=g, t=t, qt=qt, q0=q0, nk=nk, tc=tc, SCR=SCR, nbis=(NBIS_BLK if blk else NBIS), MXK=(MXK if blk else []))


    def att_bis(c, act_share):
        qt, nk, SCR = c["qt"], c["nk"], c["SCR"]
        bi = qt % 2
        nmb = nm8[bi]
        kA, kB, kM = "nmA%d" % bi, "nmB%d" % bi, "nm%d" % bi
        split = nk >= 1400
        ca = (int(nk * (1.0 - act_share)) // 2) * 2 if split else nk
        if split:
            nact = nk - ca
            P.add("dve", lambda: nc.vector.tensor_scalar(kadj[:], kcnt[:, qt:qt + 1], float(-0.5 - 0.5 * nact), None, ALU.add), R=["kcnt"], W=["kadj"])
        for it in range(c["nbis"]):
            P.add("dve", lambda it=it: nc.vector.tensor_tensor(mid[:], lo[:], steps[:, it:it + 1], ALU.add), R=["lo", "steps"], W=["mid"])
            if split:
                P.add("dve", lambda it=it: nc.vector.tensor_scalar(negmid[:], lo[:], steps[:, it:it + 1], -1.0, ALU.add, ALU.mult), R=["lo", "steps"], W=["negmid"])
                P.add("act", lambda: nc.scalar.activation(nmb[0:T, ca:nk], scores[0:T, ca:nk], AF.Sign, bias=negmid[:], scale=1.0, accum_out=cnta[:], saturate=False),
                      R=SCR + ["negmid"], W=[kB, "cnta"])
            P.add("dve", lambda: nc.vector.tensor_scalar(nmb[0:T, 0:ca], scores[0:T, 0:ca], mid[:], None, ALU.is_ge, ALU.add, accum_out=cntc[:], saturate=False),
                  R=SCR + ["mid"], W=[kA, "cntc"])
            if split:
                P.add("dve", lambda: nc.vector.scalar_tensor_tensor(ctmp[:], cnta[:], 0.5, cntc[:], ALU.mult, ALU.add), R=["cnta", "cntc"], W=["ctmp"])
                P.add("dve", lambda: nc.vector.tensor_tensor(gec[:], ctmp[:], kadj[:], ALU.is_ge), R=["ctmp", "kadj"], W=["gec"])
            else:
                P.add("dve", lambda: nc.vector.tensor_tensor(gec[:], cntc[:], kcnt[:, qt:qt + 1], ALU.is_ge), R=["cntc", "kcnt"], W=["gec"])
            P.add("dve", lambda it=it: nc.vector.scalar_tensor_tensor(lo[:], gec[:], steps[:, it:it + 1], lo[:], ALU.mult, ALU.add), R=["gec", "steps", "lo"], W=["lo"])
            yield
        P.add("dve", lambda: nc.vector.tensor_scalar(nmb[:, 0:nk], scores[:, 0:nk], lo128[:], None, ALU.is_lt, saturate=False), R=SCR + ["lo"], W=[kM, kA, kB])
        if debug and "sc" in debug and qt == debug.get("_qt", 0):
            sp_dma(dbg_d["sc"][:, 0:nk], scores[0:T, 0:nk], SCR, [])
            sp_dma(dbg_d["lo"][:, :], lo[:], ["lo"], [])

    def att_att(c):
        qt, t, tc = c["qt"], c["t"], c["tc"]
        bi = qt % 2
        nmb = nm8[bi]
        kM = "nm%d" % bi
        QK = [("qT", r) for r in range(4)]

        def QKm(kt):
            k0 = kt * T
            sset = kt % 2
            b0 = [0, 4][sset]
            for gq in range(2):
                b = b0 + gq
                P.add("pe", lambda: nc.tensor.matmul(ps[b][0:T, 0:G], kT[:, k0:k0 + T], qz[:, gq, :, tc], start=True, stop=False),
                      R=[("kT", k0 // G)] + QK, W=[("ps", b)])
                P.add("pe", lambda: nc.tensor.matmul(ps[b][0:T, 0:G], nmb[:, k0:k0 + T], I4[:], start=False, stop=True),
                      R=[kM, "I4"], W=[("ps", b)])
            P.add("act", lambda: nc.scalar.activation(ee[sset][0:T, :, :], psall[0:T, b0:b0 + 2, 0:G], AF.Exp, scale=0.125),
                  R=[("ps", b0), ("ps", b0 + 1)], W=[("ee", sset)])

        def PVm(kt):
            sset = kt % 2
            for gq in range(2):
                P.add("pe", lambda: nc.tensor.matmul(ps[6 + gq][0:65, 0:G], Vc[:, kt, gq * 65:(gq + 1) * 65], ee[sset][:, gq, :], start=(kt == 0), stop=(kt == qt)),
                      R=[("V", kt), ("ee", sset)], W=[("ps", 6 + gq)])
        QKm(0)
        for kt in range(qt + 1):
            if kt + 1 <= qt:
                QKm(kt + 1)
            PVm(kt)
            yield
        for gq in range(2):
            P.add("dve", lambda: nc.vector.reciprocal(rden[64:65, :], ps[6 + gq][64:65, 0:G]), R=[("ps", 6 + gq)], W=["rden"])
            P.add("pe", lambda: nc.tensor.matmul(ps[2][0:64, 0:G], ones32[64:65, 0:64], rden[64:65, :], start=True, stop=True),
                  R=["rden", "ones32"], W=[("ps", 2)])
            P.add("act", lambda: nc.scalar.copy(bcs[:], ps[2][0:64, 0:G]), R=[("ps", 2)], W=[("tmpb", 1)])
            P.add("dve", lambda: nc.vector.tensor_tensor(attT[:, gq * 4:(gq + 1) * 4, tc], ps[6 + gq][0:64, 0:G].rearrange("p (r t) -> p r t", r=4), bcs[:].rearrange("p (r t) -> p r t", r=4), ALU.mult),
                  R=[("ps", 6 + gq), ("tmpb", 1)], W=[("attT", gq * 4 + r) for r in range(4)])

    try:
        main()
        for g in range(NG):
            group(g)
    except _Stop:
        pass
    P.emit(stack)
    return nc, stack, P


def host_consts():
    half = 32
    inv_freq = (np.float32(10000.0) ** (-np.arange(half, dtype=np.float32) / np.float32(half))).astype(np.float32)
    pos = np.arange(L, dtype=np.float32)
    ang = (pos[:, None] * inv_freq[None, :]).astype(np.float32).astype(np.float64)
    cos = np.cos(ang).astype(np.float32)
    sin = np.sin(ang).astype(np.float32)
    cosT = np.zeros((128, L), np.float32)
    sinT = np.zeros((128, L), np.float32)
    for p in range(128):
        d = p % 64
        cosT[p] = cos[:, d % 32]
        sinT[p] = (-sin[:, d % 32]) if d < 32 else sin[:, d % 32]
    r = np.arange(T)
    trin = np.where(r[None, :] <= r[:, None], 0.0, -1e30).astype(np.float32)
    trip = np.where(r[None, :] <= r[:, None], 0.0, 1e30).astype(np.float32)
    posq = (np.arange(NTILE)[None, :] * T + r[:, None])
    kcnt = np.minimum(256, posq + 1).astype(np.float32)
    pw2 = np.tile((2.0 ** -(np.arange(NBIS) + 1.0)).astype(np.float32)[None, :], (T, 1))
    return dict(cosT=cosT, sinT=sinT, ident=np.eye(128, dtype=np.float32), trin=trin, trip=trip,
                kcnt=np.ascontiguousarray(kcnt), pw2=np.ascontiguousarray(pw2))


def make_in_maps(inputs, cores):
    c = host_consts()
    f = lambda a: np.ascontiguousarray(np.asarray(a, dtype=np.float32))
    shared = dict(
        meta=f(inputs["meta_tokens"]),
        f1g=f(inputs["ffn1_w_gate"][0]), f1u=f(inputs["ffn1_w_up"][0]), f1d=f(inputs["ffn1_w_down"][0]),
        f2g=f(inputs["ffn2_w_gate"][0]), f2u=f(inputs["ffn2_w_up"][0]), f2d=f(inputs["ffn2_w_down"][0]),
        win=f(inputs["w_in"][0]), wao=f(inputs["w_att_out"][0]), wco=f(inputs["w_conv_out"][0]), wo=f(inputs["w_o"][0]),
        ln1g=f(inputs["ln1_g"]), ln1b=f(inputs["ln1_b"]), ln2g=f(inputs["ln2_g"]), ln2b=f(inputs["ln2_b"]),
        ln3g=f(inputs["ln3_g"]), ln3b=f(inputs["ln3_b"]),
        convw=f(np.asarray(inputs["conv_w"])[0].reshape(3, 4, 128).transpose(2, 1, 0).reshape(128, 12)),
        **c)
    maps = []
    for b in cores:
        m = dict(shared)
        m["x"] = f(inputs["x"][b])
        maps.append(m)
    return maps


def kernel(**inputs):
    nc, stack, P = build(NG_FULL)
    cores = list(range(8))
    in_maps = make_in_maps(inputs, cores)
    with stack:
        res = run_bass_kernel_spmd(nc, in_maps, core_ids=cores)
    out = np.stack([np.asarray(r["out"], dtype=np.float32) for r in res.results], axis=0)
    return out
```

```python
import types
import numpy as np
from contextlib import ExitStack
import concourse.bass as bass
import concourse.mybir as mybir
from concourse.bass_utils import run_bass_kernel_spmd

F32 = mybir.dt.float32
BF16 = mybir.dt.bfloat16
F8 = mybir.dt.float8e5
ALU = mybir.AluOpType
AF = mybir.ActivationFunctionType
AX = mybir.AxisListType

T = 114
NTG = 4
G = T * NTG
NG_FULL = 18
L = 8208
NTILE = 72
D = 1024
KC = 8
DFF = 2816
NFC = 22
NMETA = 16
SEQ = 8192
ALPHA = float(2.0 ** 0.25)
EPS = 1e-5
NBIS = 16
NBIS_BLK = 12
NEGM = -28672.0
QO, QIO, KKO, CVO, GAO, GCO, VWO, WPC = 0, 1024, 2048, 2560, 4096, 5120, 6144, 6280
SQ, SK, SV, SQI, SKI, SWI, SU, SGB, SGC, SGA, SGV = 0, 512, 640, 768, 1280, 1344, 1352, 1864, 2376, 2888, 3912
NRING = 3
ND = 8


def _freeze(fn):
    if fn.__closure__ is None:
        return fn
    cells = []
    for c in fn.__closure__:
        try:
            cells.append(types.CellType(c.cell_contents))
        except ValueError:
            cells.append(c)
    return types.FunctionType(fn.__code__, fn.__globals__, fn.__name__, fn.__defaults__, tuple(cells))


class Prog:
    ENGS = ("pe", "act", "dve", "pool", "sp")

    def __init__(self, nc):
        self.nc = nc
        self.ops = []
        self.last_w = {}
        self.readers = {}

    def add(self, eng, fn, R=(), W=(), dma=False):
        i = len(self.ops)
        deps = set()
        for r in R:
            lw = self.last_w.get(r)
            if lw is not None:
                deps.add(lw)
        for w in W:
            lw = self.last_w.get(w)
            if lw is not None:
                deps.add(lw)
            rs = self.readers.get(w)
            if rs:
                deps |= rs
        for r in R:
            self.readers.setdefault(r, set()).add(i)
        for w in W:
            self.last_w[w] = i
            self.readers[w] = set()
        deps.discard(i)
        self.ops.append([eng, _freeze(fn), deps, dma])
        return i

    def transfer(self, src_keys, dst_keys):
        acc = set()
        for k in src_keys:
            lw = self.last_w.get(k)
            if lw is not None:
                acc.add(lw)
            acc |= self.readers.get(k, set())
        for k in dst_keys:
            self.readers.setdefault(k, set()).update(acc)

    def emit(self, stack):
        nc = self.nc
        ops = self.ops
        engobj = {"pe": nc.tensor, "act": nc.scalar, "dve": nc.vector, "pool": nc.gpsimd, "sp": nc.sync}
        n = len(ops)
        needed = [False] * n
        red = []
        for i, (eng, fn, deps, dma) in enumerate(ops):
            best = {}
            dl = []
            for d in deps:
                pe_, _, _, pdma = ops[d]
                if pdma:
                    dl.append(d)
                else:
                    if pe_ == "pe" and eng == "pe" and not dma:
                        continue
                    if best.get(pe_, -1) < d:
                        best[pe_] = d
            dl.extend(best.values())
            red.append(dl)
            for d in dl:
                needed[d] = True
        esem = {e: stack.enter_context(nc.semaphore("s_" + e)) for e in self.ENGS}
        dsem = {e: [stack.enter_context(nc.semaphore("d_%s%d" % (e, k))) for k in range(ND)] for e in self.ENGS}
        cnt = {e: 0 for e in self.ENGS}
        dcnt = {e: 0 for e in self.ENGS}
        dhist = {e: [] for e in self.ENGS}
        sig = [None] * n
        prevdma = [None] * n
        for i, (eng, fn, deps, dma) in enumerate(ops):
            if dma:
                k = dcnt[eng]
                dcnt[eng] += 1
                sig[i] = (dsem[eng][k % ND], 16 * (k // ND + 1))
                if k >= ND:
                    prevdma[i] = dhist[eng][k - ND]
                dhist[eng].append(i)
            elif needed[i]:
                cnt[eng] += 1
                sig[i] = (esem[eng], cnt[eng])
        seen = {e: {} for e in self.ENGS}
        for i, (eng, fn, deps, dma) in enumerate(ops):
            eo = engobj[eng]
            waits = {}
            dl = list(red[i])
            if prevdma[i] is not None:
                dl.append(prevdma[i])
            for d in dl:
                s, v = sig[d]
                key = id(s)
                if key not in waits or waits[key][1] < v:
                    waits[key] = (s, v)
            for key, (s, v) in waits.items():
                if seen[eng].get(key, 0) >= v:
                    continue
                eo.wait_ge(s, v)
                seen[eng][key] = v
            ins = fn()
            if dma:
                ins.then_inc(sig[i][0], 16)
            elif needed[i]:
                ins.then_inc(sig[i][0], 1)
        for e in self.ENGS:
            for i in dhist[e][-ND:]:
                s, v = sig[i]
                if seen["sp"].get(id(s), 0) < v:
                    nc.sync.wait_ge(s, v)
                    seen["sp"][id(s)] = v
        self.stats = dict(n_ops=n, cnt=cnt, dcnt=dcnt)


class _Stop(Exception):
    pass


def build(NG=NG_FULL, debug=None, stop=None):
    nc = bass.Bass("TRN2", target_bir_lowering=False)
    stack = ExitStack()
    P = Prog(nc)

    def chk(name):
        if stop == name:
            raise _Stop()

    def dram_in(name, shape):
        return nc.dram_tensor(name, list(shape), F32, kind="ExternalInput").ap()

    x_d = dram_in("x", [SEQ, D])
    meta_d = dram_in("meta", [NMETA, D])
    wsrc = {}
    for nm, shp in [("f1g", [D, DFF]), ("f1u", [D, DFF]), ("f1d", [DFF, D]),
                    ("f2g", [D, DFF]), ("f2u", [D, DFF]), ("f2d", [DFF, D]),
                    ("win", [D, 4936]), ("wao", [512, D]), ("wco", [512, D]), ("wo", [D, D])]:
        wsrc[nm] = dram_in(nm, shp)
    lnp = {nm: dram_in(nm, [1, D]) for nm in ["ln1g", "ln1b", "ln2g", "ln2b", "ln3g", "ln3b"]}
    convw_d = dram_in("convw", [128, 12])
    cos_d = dram_in("cosT", [128, L])
    sin_d = dram_in("sinT", [128, L])
    ident_d = dram_in("ident", [128, 128])
    trin_d = dram_in("trin", [T, T])
    trip_d = dram_in("trip", [T, T])
    kcnt_d = dram_in("kcnt", [T, NTILE])
    pw2_d = dram_in("pw2", [T, NBIS])
    out_d = nc.dram_tensor("out", [SEQ, D], F32, kind="ExternalOutput").ap()
    dbg_d = {}
    if debug:
        for nm, shp in debug.items():
            dbg_d[nm] = nc.dram_tensor("dbg_" + nm, list(shp), F32, kind="ExternalOutput").ap()

    def scratch(name, shape):
        return nc.dram_tensor(name, list(shape), BF16, kind="Internal").ap()

    wsc = {"f1g": scratch("s_f1g", [D, DFF]), "f1u": scratch("s_f1u", [D, DFF]), "f1d": scratch("s_f1d", [DFF, D]),
           "f2g": scratch("s_f2g", [D, DFF]), "f2u": scratch("s_f2u", [D, DFF]), "f2d": scratch("s_f2d", [DFF, D]),
           "wp": scratch("s_wp", [D, WPC]), "wao": scratch("s_wao", [512, D]), "wco": scratch("s_wco", [512, D]),
           "wo": scratch("s_wo", [D, D])}

    def sb(name, shape, dt=F32):
        return stack.enter_context(nc.sbuf_tensor("sb_" + name, list(shape), dt))

    kT = sb("kT", [128, L], BF16)
    kiT = sb("kiT", [128, L], BF16)
    Vc = sb("Vc", [128, NTILE, 130], BF16)
    S = sb("S", [T, NTG, D], F32)
    aT = sb("aT", [128, KC, G], BF16)
    xbf = [sb("xbf0", [T, D], BF16)] * 2
    hT = sb("hT", [128, NFC, G], BF16)
    qiz = sb("qiz", [128, 8, G], BF16)
    ycv = sb("ycv", [128, 4, G], BF16)
    attT = sb("attT", [64, 8, G], BF16)
    scores = sb("scores", [128, L], F32)
    ring = [sb("ring%d" % i, [128, 4096], BF16) for i in range(NRING)]
    ropec = sb("ropec", [128, G], F32)
    ropes = sb("ropes", [128, G], F32)
    gbq = sb("gbq", [128, 2 * D], F32)
    gb = gbq[0:T, :].rearrange("p (a d) -> p a d", a=2)
    qz = gbq[:].bitcast(BF16)[:, 0:2 * 4 * G].rearrange("p (g r t) -> p g r t", g=2, r=4)
    tmpa = [sb("tmpa%d" % i, [128, G], F32) for i in range(2)]
    sg = tmpa
    tmpb = [sb("tmpb%d" % i, [128, G], F32) for i in range(2)]
    bcs = tmpb[1][0:64, :]
    zt = sb("zt", [128, G + 2], F32)
    halo = sb("halo", [128, 4, 2], F32)
    convw = sb("convw", [128, 12], F32)
    identf = sb("identf", [128, 128], F32)
    identb = sb("identb", [128, 128], BF16)
    I4 = sb("I4", [128, G], F8)
    trin = sb("trin", [T, T], F32)
    trip = sb("trip", [T, T], F32)
    kcnt = sb("kcnt", [T, NTILE], F32)
    pw2 = sb("pw2", [T, NBIS], F32)
    ones32 = sb("ones32", [128, 64], F32)
    neghalf = sb("neghalf", [128, NTG], F32)
    st = sb("st", [T, NTG, 12], F32)
    mv = sb("mv", [T, NTG, 2], F32)
    ve = sb("ve", [T, NTG], F32)
    rstd = sb("rstd", [T, NTG], F32)
    wabs = sb("wabs", [T, NTG, 8], F32)
    dg = sb("dg", [128, 8, T], BF16)
    rr = [sb("rr%d" % i, [128, 2, 512], BF16) for i in range(3)]
    ee = [sb("ee%d" % i, [128, 2, G], BF16) for i in range(2)]
    t114 = sb("t114", [T, T], F32)
    mx = zt[0:T, 0:256]
    m8t = zt[0:T, 256:288]
    m0 = sb("m0", [T, 1], F32)
    m1 = sb("m1", [T, 1], F32)
    lo128 = sb("lo", [128, 1], F32)
    lo = lo128[0:T, :]
    rmax = sb("rmax", [T, 1], F32)
    rng = sb("rng", [T, 1], F32)
    steps = sb("steps", [T, NBIS], F32)
    mid = sb("mid", [T, 1], F32)
    cntc = sb("cntc", [T, 1], F32)
    cnta = sb("cnta", [T, 1], F32)
    negmid = sb("negmid", [T, 1], F32)
    kadj = sb("kadj", [T, 1], F32)
    ctmp = sb("ctmp", [T, 1], F32)
    gec = sb("gec", [T, 1], F32)
    rden = sb("rden", [128, G], F32)
    psall = stack.enter_context(nc.psum_tensor("psall", [128, 8, 512], F32))
    ps = [psall[:, b, :] for b in range(8)]
    ps7b = psall[:, 7, :].bitcast(BF16)
    h8 = hT[:].rearrange("p a b -> p (a b)").bitcast(F8)
    nm8 = [h8[:, 0:L], h8[:, L:2 * L]]

    PSK = [("ps", b) for b in range(8)]
    HTK = [("hT", f) for f in range(NFC)]

    sp_dma = lambda out, in_, R, W: P.add("sp", lambda: nc.sync.dma_start(out=out, in_=in_), R=R, W=W, dma=True)
    st_dma = lambda out, in_, R, W: P.add("pool", lambda: nc.gpsimd.dma_start(out=out, in_=in_), R=R, W=W, dma=True)

    wsc_barrier = []
    SCK = [("sc", c) for c in range(17)]

    def main():
        sp_dma(identf[:], ident_d, [], ["identf"])
        sp_dma(trin[:], trin_d, [], ["trin"])
        sp_dma(trip[:], trip_d, [], ["trip"])
        sp_dma(kcnt[:], kcnt_d, [], ["kcnt"])
        sp_dma(pw2[:], pw2_d, [], ["pw2"])
        sp_dma(convw[:], convw_d, [], ["convw"])
        P.add("dve", lambda: nc.vector.tensor_copy(identb[:], identf[:]), R=["identf"], W=["identb"])
        P.add("dve", lambda: nc.vector.memset(I4[:], 0.0), W=["I4"])
        for r in range(4):
            P.add("dve", lambda r=r: nc.vector.tensor_scalar(I4[0:T, r * T:(r + 1) * T], identf[0:T, 0:T], NEGM, None, ALU.mult, saturate=False), R=["identf"], W=["I4"])
        P.add("pool", lambda: nc.gpsimd.memset(qiz[:], 0.0), W=[("qiT", c) for c in range(4)])
        P.add("pool", lambda: nc.gpsimd.memset(lo128[:], 0.0), W=["lo"])
        P.add("pool", lambda: nc.gpsimd.memset(dg[:], 0.0), W=[("dg", h) for h in range(8)])
        for i in range(3):
            P.add("pool", lambda i=i: nc.gpsimd.memset(rr[i][:], 0.0), W=[("rr", i)])
        for i in range(2):
            P.add("pool", lambda i=i: nc.gpsimd.memset(ee[i][:], 0.0), W=[("ee", i)])
        P.add("pool", lambda: nc.gpsimd.memset(ones32[:], 1.0), W=["ones32"])
        P.add("pool", lambda: nc.gpsimd.memset(neghalf[:], -0.5), W=["neghalf"])
        P.add("pool", lambda: nc.gpsimd.memset(halo[:], 0.0), W=["halo"])
        P.add("pool", lambda: nc.gpsimd.memset(Vc[:], 0.0), W=[("V", t) for t in range(NTILE)])
        P.add("pool", lambda: nc.gpsimd.memset(Vc[0:T], 1.0), W=[("V", t) for t in range(NTILE)])

        chk('const')
        stage32 = [scores[:, 0:4104], scores[:, 4104:8208]]
        hflat = hT[:].rearrange("p a b -> p (a b)")
        stage16 = [hflat[:, 0:4104], hflat[:, 4104:8208]]
        ceng = ["act", "dve", "act", "dve", "pool"]
        cstate = [0]

        def cast(out, in_, R, W):
            e = ceng[cstate[0] % 5]
            cstate[0] += 1
            if e == "act":
                P.add("act", lambda: nc.scalar.copy(out, in_), R=R, W=W)
            elif e == "dve":
                P.add("dve", lambda: nc.vector.tensor_copy(out, in_), R=R, W=W)
            else:
                P.add("pool", lambda: nc.gpsimd.tensor_copy(out, in_), R=R, W=W)

        pj = [0]

        def prep_plain(src, dst, rows, cols):
            for rb in range(rows // 128):
                i = pj[0] % 2
                pj[0] += 1
                s32 = stage32[i][:, 0:cols]
                s16 = stage16[i][:, 0:cols]
                sp_dma(s32, src[rb * 128:(rb + 1) * 128, :], [], [("p32", i)])
                h = cols // 2
                cast(s16[:, 0:h], s32[:, 0:h], [("p32", i)], [("p16", i, 0)])
                cast(s16[:, h:cols], s32[:, h:cols], [("p32", i)], [("p16", i, 1)])
                st_dma(dst[rb * 128:(rb + 1) * 128, :], s16, [("p16", i, 0), ("p16", i, 1)], [])

        for nm in ["f1g", "f1u"]:
            prep_plain(wsrc[nm], wsc[nm], D, DFF)
        prep_plain(wsrc["f1d"], wsc["f1d"], DFF, D)

        s32 = scores[:, 0:4936]
        s16 = hflat[:, 0:WPC]
        for rb in range(8):
            K32 = [("p32", 0), ("p32", 1)]
            K16 = [("p16", 0, 0), ("p16", 0, 1), ("p16", 1, 0), ("p16", 1, 1)]
            sp_dma(s32, wsrc["win"][rb * 128:(rb + 1) * 128, :], [], K32)
            if rb == 0:
                P.add("pool", lambda: nc.gpsimd.memset(ve[:], 0.0), R=[], W=K16 + ["wpbar", "ve"])

            def cp(dst, src, tag):
                cast(dst, src, K32 + ["wpbar"], [("wpw", tag)])
            qsrc = s32[:, SQ:SQ + 512].rearrange("p (j a e) -> p a j e", j=2, a=4, e=64)
            for c2 in range(2):
                dA = s16[:, QO + c2 * 512: QO + c2 * 512 + 256].rearrange("p (a j e) -> p a j e", a=2, j=2, e=64)
                dB = s16[:, QO + c2 * 512 + 256: QO + c2 * 512 + 512].rearrange("p (a j e) -> p a j e", a=2, j=2, e=64)
                for a in range(2):
                    cp(dA[:, a], qsrc[:, 2 * c2 + a], ("qA", c2, a))
                    for hf in range(2):
                        cp(dB[:, a, :, hf * 32:(hf + 1) * 32], qsrc[:, 2 * c2 + a, :, (1 - hf) * 32:(2 - hf) * 32], ("qB", c2, a, hf))
            for c2 in range(2):
                cp(s16[:, QIO + c2 * 512: QIO + c2 * 512 + 256], s32[:, SQI + c2 * 256: SQI + c2 * 256 + 256], ("qiA", c2))
                dB = s16[:, QIO + c2 * 512 + 256: QIO + c2 * 512 + 512].rearrange("p (h f e) -> p h f e", h=4, f=2, e=32)
                sB = s32[:, SQI + c2 * 256: SQI + c2 * 256 + 256].rearrange("p (h f e) -> p h f e", h=4, f=2, e=32)
                for hf in range(2):
                    cp(dB[:, :, hf], sB[:, :, 1 - hf], ("qiB", c2, hf))
            cp(s16[:, KKO:KKO + 128], s32[:, SK:SK + 128], "k")
            dB = s16[:, KKO + 128:KKO + 256].rearrange("p (h f e) -> p h f e", h=2, f=2, e=32)
            sB = s32[:, SK:SK + 128].rearrange("p (h f e) -> p h f e", h=2, f=2, e=32)
            for hf in range(2):
                cp(dB[:, :, hf], sB[:, :, 1 - hf], ("kB", hf))
            for cpy in range(2):
                cp(s16[:, KKO + 256 + cpy * 64: KKO + 256 + cpy * 64 + 64], s32[:, SKI:SKI + 64], ("ki", cpy))
                for hf in range(2):
                    cp(s16[:, KKO + 384 + cpy * 64 + hf * 32: KKO + 384 + cpy * 64 + hf * 32 + 32],
                       s32[:, SKI + (1 - hf) * 32: SKI + (1 - hf) * 32 + 32], ("kiB", cpy, hf))
            dC = s16[:, CVO:CVO + 1536].rearrange("p (c m e) -> p c m e", c=4, m=3, e=128)
            for m, so in enumerate([SU, SGC, SGB]):
                cp(dC[:, :, m], s32[:, so:so + 512].rearrange("p (c e) -> p c e", c=4, e=128), ("cv", m))
            cp(s16[:, GAO:GAO + 1024], s32[:, SGA:SGA + 1024], "ga")
            cp(s16[:, GCO:GCO + 1024], s32[:, SGV:SGV + 1024], "gc")
            cp(s16[:, VWO:VWO + 128], s32[:, SV:SV + 128], "v")
            cp(s16[:, VWO + 128:VWO + 136], s32[:, SWI:SWI + 8], "wi")
            tags = [k for k in P.last_w.keys() if isinstance(k, tuple) and k and k[0] == "wpw"]
            st_dma(wsc["wp"][rb * 128:(rb + 1) * 128, :], s16, tags, K16)

        for nm, rows in [("wao", 512), ("wco", 512), ("wo", D)]:
            prep_plain(wsrc[nm], wsc[nm], rows, D)
        for nm in ["f2g", "f2u"]:
            prep_plain(wsrc[nm], wsc[nm], D, DFF)
        prep_plain(wsrc["f2d"], wsc["f2d"], DFF, D)
        allprep = [("p32", 0), ("p32", 1), ("p16", 0, 0), ("p16", 0, 1), ("p16", 1, 0), ("p16", 1, 1)] + \
            [k for k in P.last_w.keys() if isinstance(k, tuple) and k and k[0] == "wpw"]
        P.transfer(allprep, HTK + SCK + ["nm0", "nm1", "nmA0", "nmA1", "nmB0", "nmB1"])
        prep_stores = [i for i, o in enumerate(P.ops) if o[3]]
        wsc_barrier.extend(prep_stores)


    rstate = [0]
    first_loads = [True]

    def ring_load(parts):
        s = rstate[0] % NRING
        rstate[0] += 1
        for (vf, src, hk) in parts:
            hks = hk if isinstance(hk, tuple) else (hk,)
            i = sp_dma(vf(ring[s]), src, [], [("rg", s, h_) for h_ in hks])
            P.ops[i][2].update(wsc_barrier)
        return s

    def v3(lo_, n, kc):
        return lambda r: r[:, lo_:lo_ + n * kc].rearrange("p (k f) -> p k f", k=kc)

    def to_featmajor(t):
        xb = xbf[0]
        P.add("act", lambda: nc.scalar.copy(xb[:], S[:, t, :]), R=[("S", t)], W=[("xbf", 0)])
        for kc in range(KC):
            P.add("pe", lambda kc=kc: nc.tensor.transpose(ps7b[:, kc * T:(kc + 1) * T], xb[:, kc * 128:(kc + 1) * 128], identb[0:T, 0:T]),
                  R=[("xbf", 0), "identb"], W=[("ps", 7)])
        P.add("dve", lambda: nc.vector.tensor_copy(aT[:, :, t * T:(t + 1) * T], ps7b[:, 0:KC * T].rearrange("p (k t) -> p k t", k=KC)),
              R=[("ps", 7)], W=[("aT", t)])

    ATK = [("aT", t) for t in range(NTG)]

    def ffn(wg, wu, wd, resid_scale):
        for j in range(11):
            s = ring_load([(v3(0, 256, KC), wg[:, 256 * j:256 * j + 256].rearrange("(k p) f -> p k f", p=128), 0),
                           (v3(2048, 256, KC), wu[:, 256 * j:256 * j + 256].rearrange("(k p) f -> p k f", p=128), 1)])
            gv = v3(0, 256, KC)(ring[s])
            uv = v3(2048, 256, KC)(ring[s])
            for f2 in range(2):
                fc = 2 * j + f2
                bg = fc % 2
                bu = 2 + fc % 2
                for kc in range(KC):
                    P.add("pe", lambda kc=kc, f2=f2, bg=bg: nc.tensor.matmul(ps[bg][:, 0:G], gv[:, kc, f2 * 128:(f2 + 1) * 128], aT[:, kc, :], start=(kc == 0), stop=(kc == KC - 1)),
                          R=[("rg", s, 0)] + ATK, W=[("ps", bg)])
                for kc in range(KC):
                    P.add("pe", lambda kc=kc, f2=f2, bu=bu: nc.tensor.matmul(ps[bu][:, 0:G], uv[:, kc, f2 * 128:(f2 + 1) * 128], aT[:, kc, :], start=(kc == 0), stop=(kc == KC - 1)),
                          R=[("rg", s, 1)] + ATK, W=[("ps", bu)])
                sgt = sg[fc % 2]
                P.add("act", lambda bg=bg, sgt=sgt: nc.scalar.activation(sgt[:], ps[bg][:, 0:G], AF.Silu), R=[("ps", bg)], W=[("tmpa", fc % 2)])
                P.add("dve", lambda bu=bu, sgt=sgt, fc=fc: nc.vector.scalar_tensor_tensor(hT[:, fc, :], sgt[:], resid_scale, ps[bu][:, 0:G], ALU.mult, ALU.mult),
                      R=[("tmpa", fc % 2), ("ps", bu)], W=[("hT", fc)])
        for j in range(6):
            nf = 4 if j < 5 else 2
            s = ring_load([(lambda r, nf=nf: r[:, 0:nf * 1024].rearrange("p (c d) -> p c d", c=nf),
                            wd[512 * j:512 * j + 128 * nf, :].rearrange("(c p) d -> p c d", p=128), (0, 1))])
            wv = ring[s][:, 0:nf * 1024].rearrange("p (c d) -> p c d", c=nf)
            for c in range(nf):
                fc = 4 * j + c
                for t in range(NTG):
                    for dh in range(2):
                        b = t * 2 + dh
                        P.add("pe", lambda c=c, fc=fc, t=t, dh=dh, b=b, wv=wv: nc.tensor.matmul(ps[b][0:T, :], hT[:, fc, t * T:(t + 1) * T], wv[:, c, dh * 512:(dh + 1) * 512], start=(fc == 0), stop=(fc == NFC - 1)),
                              R=[("rg", s, 0), ("rg", s, 1), ("hT", fc)], W=[("ps", b)])
        for t in range(NTG):
            for dh in range(2):
                b = t * 2 + dh
                P.add("dve", lambda t=t, dh=dh, b=b: nc.vector.scalar_tensor_tensor(S[:, t, dh * 512:(dh + 1) * 512], S[:, t, dh * 512:(dh + 1) * 512], ALPHA, ps[b][0:T, :], ALU.mult, ALU.add),
                      R=[("ps", b), ("S", t)], W=[("S", t)])

    def load_gb(gname, bname):
        sp_dma(gb[:, 0, :], lnp[gname].partition_broadcast(T), [], ["gb"])
        sp_dma(gb[:, 1, :], lnp[bname].partition_broadcast(T), [], ["gb2"])

    def layernorm_all(after=None):
        for t in range(NTG):
            for hh in range(2):
                P.add("dve", lambda t=t, hh=hh: nc.vector.bn_stats(st[:, t, hh * 6:(hh + 1) * 6], S[:, t, hh * 512:(hh + 1) * 512]), R=[("S", t)], W=[("st", t, hh)])
            P.add("dve", lambda t=t: nc.vector.bn_aggr(mv[:, t, :], st[:, t, :]), R=[("st", t, 0), ("st", t, 1)], W=[("mv", t)])
        MVK = [("mv", t) for t in range(NTG)]
        P.add("pool", lambda: nc.gpsimd.tensor_scalar(ve[:], mv[:, :, 1], EPS, None, ALU.add), R=MVK, W=["ve"])
        P.add("pool", lambda: nc.gpsimd.tensor_tensor(rstd[:], ve[:], neghalf[0:T, :], ALU.pow), R=["ve", "neghalf"], W=["rstd"])
        for t in range(NTG):
            St = S[:, t, :]
            P.add("dve", lambda t=t, St=St: nc.vector.scalar_tensor_tensor(St, St, mv[:, t, 0:1], gb[:, 0, :], ALU.subtract, ALU.mult), R=[("S", t), ("mv", t), "gb"], W=[("S", t)])
            P.add("dve", lambda t=t, St=St: nc.vector.scalar_tensor_tensor(St, St, rstd[:, t:t + 1], gb[:, 1, :], ALU.mult, ALU.add), R=[("S", t), "rstd", "gb2"], W=[("S", t)])
            if after is not None:
                after(t)

    def wp_piece(col0, ncols):
        return ring_load([(v3(0, ncols, KC), wsc["wp"][:, col0:col0 + ncols].rearrange("(k p) f -> p k f", p=128), (0, 1))])

    def proj_fm(s, off, bank):
        wv = v3(0, 512, KC)(ring[s])
        for kc in range(KC):
            P.add("pe", lambda kc=kc: nc.tensor.matmul(ps[bank][:, 0:G], wv[:, kc, off:off + 128], aT[:, kc, :], start=(kc == 0), stop=(kc == KC - 1)),
                  R=[("rg", s, 0), ("rg", s, 1)] + ATK, W=[("ps", bank)])

    pairctr = [0]

    def rope_chunk(s, offA, offB, dst, dstR, dstW):
        i = pairctr[0] % 2
        pairctr[0] += 1
        bA, bB = 2 * i, 2 * i + 1
        proj_fm(s, offA, bA)
        proj_fm(s, offB, bB)
        ta, tb = tmpa[i], tmpb[i]
        P.add("dve", lambda: nc.vector.tensor_tensor(ta[:], ps[bA][:, 0:G], ropec[:], ALU.mult), R=[("ps", bA), "ropec"], W=[("tmpa", i)])
        P.add("dve", lambda: nc.vector.tensor_tensor(tb[:], ps[bB][:, 0:G], ropes[:], ALU.mult), R=[("ps", bB), "ropes"], W=[("tmpb", i)])
        if isinstance(dst, tuple):
            P.add("dve", lambda: nc.vector.tensor_tensor(dst[0], ta[0:64, :], tb[0:64, :], ALU.add), R=[("tmpa", i), ("tmpb", i)] + dstR, W=dstW)
            P.add("dve", lambda: nc.vector.tensor_tensor(dst[1], ta[64:128, :], tb[64:128, :], ALU.add), R=[("tmpa", i), ("tmpb", i)] + dstR, W=dstW)
        else:
            P.add("dve", lambda: nc.vector.tensor_tensor(dst, ta[:], tb[:], ALU.add), R=[("tmpa", i), ("tmpb", i)] + dstR, W=dstW)

    def group(g):
        p0 = g * G
        for t in range(NTG):
            pos0 = p0 + t * T
            if pos0 == 0:
                sp_dma(S[0:NMETA, t, :], meta_d, [], [("S", t)])
                sp_dma(S[NMETA:T, t, :], x_d[0:T - NMETA, :], [], [("S", t)])
            else:
                sp_dma(S[:, t, :], x_d[pos0 - NMETA:pos0 - NMETA + T, :], [], [("S", t)])
        for t in range(NTG):
            to_featmajor(t)
        chk('load')
        load_gb("ln1g", "ln1b")
        ffn(wsc["f1g"], wsc["f1u"], wsc["f1d"], 0.5)
        chk('ffn1')
        layernorm_all()
        if debug and "h1" in debug and g == 0:
            for t in range(NTG):
                sp_dma(dbg_d["h1"][t * T:(t + 1) * T, :], S[:, t, :], [("S", t)], [])
        for t in range(NTG):
            to_featmajor(t)
        chk('ln1')
        QTK = [("qT", r) for r in range(4)]
        P.transfer(["gb", "gb2"], QTK)
        P.add("pool", lambda: nc.gpsimd.memset(qz[64:128, 0], 0.0), W=QTK)
        P.add("pool", lambda: nc.gpsimd.memset(qz[0:64, 1], 0.0), W=QTK)
        sp_dma(ropec[:], cos_d[:, p0:p0 + G], [], ["ropec"])
        sp_dma(ropes[:], sin_d[:, p0:p0 + G], [], ["ropes"])
        for c2 in range(2):
            s = wp_piece(QIO + c2 * 512, 512)
            for a in range(2):
                c = 2 * c2 + a
                rope_chunk(s, a * 128, 256 + a * 128, (qiz[0:64, 2 * c, :], qiz[64:128, 2 * c + 1, :]), [], [("qiT", c)])
        chk('projq')
        s = wp_piece(KKO, 512)
        rope_chunk(s, 0, 128, kT[:, p0:p0 + G], [], [("kT", g)])
        rope_chunk(s, 256, 384, kiT[:, p0:p0 + G], [], [("kiT", g)])
        chk('projc')
        s = ring_load([(v3(0, 136, KC), wsc["wp"][:, VWO:VWO + 136].rearrange("(k p) f -> p k f", p=128), 0)])
        wv = v3(0, 136, KC)(ring[s])
        for t in range(NTG):
            qt = g * NTG + t
            for kc in range(KC):
                P.add("pe", lambda kc=kc, t=t: nc.tensor.matmul(ps[7][0:T, 0:136], aT[:, kc, t * T:(t + 1) * T], wv[:, kc, :], start=(kc == 0), stop=(kc == KC - 1)),
                      R=[("rg", s, 0), ("aT", t)], W=[("ps", 7)])
            chk('pv1')
            P.add("act", lambda t=t, qt=qt: nc.scalar.copy(Vc[0:T, qt, :].rearrange("p (g e) -> p g e", g=2)[:, :, 0:64], ps[7][0:T, 0:128].rearrange("p (g e) -> p g e", g=2)),
                  R=[("ps", 7)], W=[("V", qt)])
            chk('pv2')
            P.add("act", lambda t=t: nc.scalar.activation(wabs[:, t, :], ps[7][0:T, 128:136], AF.Copy, scale=float(512 ** -0.5)), R=[("ps", 7)], W=[("wabs", t)])
        def late_q(c2):
            s = wp_piece(QO + c2 * 512, 512)
            for a in range(2):
                r = 2 * c2 + a
                rope_chunk(s, a * 128, 256 + a * 128, (qz[0:64, 0, r, :], qz[64:128, 1, r, :]), [], [("qT", r)])

        def late_conv(sl, c):
            offs = [(3 * c + m) * 128 for m in range(3)]
            for m in range(3):
                proj_fm(sl[offs[m] // 512], offs[m] % 512, 4 + m)
            P.add("act", lambda: nc.scalar.copy(tmpa[0][:], ps[4][:, 0:G]), R=[("ps", 4)], W=[("tmpa", 0)])
            P.add("pool", lambda c=c: nc.gpsimd.tensor_copy(zt[:, 0:2], halo[:, c, :]), R=["halo"], W=["zt0"])
            P.add("dve", lambda: nc.vector.tensor_tensor(zt[:, 2:G + 2], tmpa[0][:], ps[5][:, 0:G], ALU.mult), R=[("tmpa", 0), ("ps", 5)], W=["zt"])
            P.add("pool", lambda c=c: nc.gpsimd.tensor_copy(halo[:, c, :], zt[:, G:G + 2]), R=["zt", "zt0"], W=["halo"])
            P.add("dve", lambda c=c: nc.vector.tensor_scalar(tmpb[0][:], zt[:, 2:G + 2], convw[:, c * 3 + 2:c * 3 + 3], None, ALU.mult), R=["zt", "convw"], W=[("tmpb", 0)])
            P.add("dve", lambda c=c: nc.vector.scalar_tensor_tensor(tmpb[0][:], zt[:, 1:G + 1], convw[:, c * 3 + 1:c * 3 + 2], tmpb[0][:], ALU.mult, ALU.add), R=["zt", "zt0", "convw", ("tmpb", 0)], W=[("tmpb", 0)])
            P.add("dve", lambda c=c: nc.vector.scalar_tensor_tensor(tmpb[0][:], zt[:, 0:G], convw[:, c * 3:c * 3 + 1], tmpb[0][:], ALU.mult, ALU.add), R=["zt", "zt0", "convw", ("tmpb", 0)], W=[("tmpb", 0)])
            P.add("dve", lambda c=c: nc.vector.tensor_tensor(ycv[:, c, :], tmpb[0][:], ps[6][:, 0:G], ALU.mult), R=[("tmpb", 0), ("ps", 6)], W=[("ycv", c)])

        chk('proj')
        P.transfer(HTK, ["nm0", "nm1", "nmA0", "nmA1", "nmB0", "nmB1"])
        ctx = [att_idx(g, 0)]
        ga = att_bis(ctx[0], 0.5)
        slh = []

        def conv_job(c):
            if not slh:
                slh.extend([wp_piece(CVO + i * 512, 512) for i in range(3)])
            late_conv(slh, c)
        work = [lambda: late_q(0), lambda: late_q(1)] + [lambda c=c: conv_job(c) for c in range(4)]
        for i in range(ctx[0]["nbis"]):
            next(ga)
            if i % 2 == 1 and work:
                work.pop(0)()
        for _ in ga:
            pass
        while work:
            work.pop(0)()
        for t in range(1, NTG):
            ctx.append(att_idx(g, t))
            ga = att_bis(ctx[t], 0.38)
            gb_ = att_att(ctx[t - 1])
            nb = ctx[t - 1]["qt"] + 1
            done = 0
            head = min(nb, max(1, nb // 8))
            while done < head:
                next(gb_)
                done += 1
            nbi = ctx[t]["nbis"]
            for i in range(nbi):
                next(ga)
                tgt = head + ((i + 1) * (nb - head)) // nbi
                while done < tgt:
                    next(gb_)
                    done += 1
            for _ in ga:
                pass
            for _ in gb_:
                pass
        for _ in att_att(ctx[NTG - 1]):
            pass
        P.transfer(["nm0", "nm1", "nmA0", "nmA1", "nmB0", "nmB1"], HTK)
        P.transfer([("qT", r) for r in range(4)], ["gb", "gb2"])
        chk('att')
        load_gb("ln2g", "ln2b")
        ATT = [("attT", h) for h in range(8)]
        for qd in range(4):
            c0q = qd * 256
            sX = ring_load([(lambda r: r[0:64, 0:2048].rearrange("p (h d) -> p h d", h=8), wsc["wao"][:, c0q:c0q + 256].rearrange("(h p) d -> p h d", p=64), 0),
                            (lambda r: r[:, 2048:3072].rearrange("p (c d) -> p c d", c=4), wsc["wco"][:, c0q:c0q + 256].rearrange("(c p) d -> p c d", p=128), 1)])
            sY = ring_load([(v3(0, 256, KC), wsc["wp"][:, GAO + c0q:GAO + c0q + 256].rearrange("(k p) f -> p k f", p=128), 0),
                            (v3(2048, 256, KC), wsc["wp"][:, GCO + c0q:GCO + c0q + 256].rearrange("(k p) f -> p k f", p=128), 1)])
            wa = ring[sX][0:64, 0:2048].rearrange("p (h d) -> p h d", h=8)
            wc = ring[sX][:, 2048:3072].rearrange("p (c d) -> p c d", c=4)
            wga = v3(0, 256, KC)(ring[sY])
            wgc = v3(2048, 256, KC)(ring[sY])
            for d2 in range(2):
                dc = qd * 2 + d2
                bo = 4 * (dc % 2)
                cs = slice(d2 * 128, (d2 + 1) * 128)
                for h in range(8):
                    P.add("pe", lambda: nc.tensor.matmul(ps[bo][:, 0:G], wa[:, h, cs], attT[:, h, :], start=(h == 0), stop=(h == 7)),
                          R=[("rg", sX, 0)] + ATT, W=[("ps", bo)])
                for c in range(4):
                    P.add("pe", lambda: nc.tensor.matmul(ps[bo + 1][:, 0:G], wc[:, c, cs], ycv[:, c, :], start=(c == 0), stop=(c == 3)),
                          R=[("rg", sX, 1)] + [("ycv", cc) for cc in range(4)], W=[("ps", bo + 1)])
                for kc in range(KC):
                    P.add("pe", lambda: nc.tensor.matmul(ps[bo + 2][:, 0:G], wga[:, kc, cs], aT[:, kc, :], start=(kc == 0), stop=(kc == KC - 1)),
                          R=[("rg", sY, 0)] + ATK, W=[("ps", bo + 2)])
                for kc in range(KC):
                    P.add("pe", lambda: nc.tensor.matmul(ps[bo + 3][:, 0:G], wgc[:, kc, cs], aT[:, kc, :], start=(kc == 0), stop=(kc == KC - 1)),
                          R=[("rg", sY, 1)] + ATK, W=[("ps", bo + 3)])
                i = dc % 2
                P.add("act", lambda: nc.scalar.activation(tmpa[i][:], ps[bo + 2][:, 0:G], AF.Sigmoid), R=[("ps", bo + 2)], W=[("tmpa", i)])
                P.add("act", lambda: nc.scalar.activation(tmpb[i][:], ps[bo + 3][:, 0:G], AF.Sigmoid), R=[("ps", bo + 3)], W=[("tmpb", i)])
                P.add("dve", lambda: nc.vector.tensor_tensor(tmpa[i][:], tmpa[i][:], ps[bo][:, 0:G], ALU.mult), R=[("tmpa", i), ("ps", bo)], W=[("tmpa", i)])
                P.add("dve", lambda: nc.vector.tensor_tensor(tmpb[i][:], tmpb[i][:], ps[bo + 1][:, 0:G], ALU.mult), R=[("tmpb", i), ("ps", bo + 1)], W=[("tmpb", i)])
                P.add("dve", lambda: nc.vector.tensor_tensor(hT[:, dc, :], tmpa[i][:], tmpb[i][:], ALU.add), R=[("tmpa", i), ("tmpb", i)], W=[("hT", dc)])
        for dh in range(2):
            s = ring_load([(v3(0, 512, KC), wsc["wo"][:, dh * 512:(dh + 1) * 512].rearrange("(k p) f -> p k f", p=128), (0, 1))])
            wv = v3(0, 512, KC)(ring[s])
            for t in range(NTG):
                b = (dh * NTG + t) % 8
                for kc in range(KC):
                    P.add("pe", lambda kc=kc, t=t, b=b, wv=wv: nc.tensor.matmul(ps[b][0:T, :], hT[:, kc, t * T:(t + 1) * T], wv[:, kc, :], start=(kc == 0), stop=(kc == KC - 1)),
                          R=[("rg", s, 0), ("rg", s, 1)] + [("hT", k) for k in range(8)], W=[("ps", b)])
                P.add("dve", lambda t=t, dh=dh, b=b: nc.vector.scalar_tensor_tensor(S[:, t, dh * 512:(dh + 1) * 512], S[:, t, dh * 512:(dh + 1) * 512], ALPHA, ps[b][0:T, :], ALU.mult, ALU.add),
                      R=[("ps", b), ("S", t)], W=[("S", t)])
        chk('merge')
        layernorm_all()
        if debug and "h2" in debug and g == 0:
            for t in range(NTG):
                sp_dma(dbg_d["h2"][t * T:(t + 1) * T, :], S[:, t, :], [("S", t)], [])
        for t in range(NTG):
            to_featmajor(t)
        chk('ln2')
        load_gb("ln3g", "ln3b")
        ffn(wsc["f2g"], wsc["f2u"], wsc["f2d"], 0.5)

        def store(t):
            pos0 = p0 + t * T
            if pos0 == 0:
                st_dma(out_d[0:T - NMETA, :], S[NMETA:T, t, :], [("S", t)], [])
            else:
                st_dma(out_d[pos0 - NMETA:pos0 - NMETA + T, :], S[:, t, :], [("S", t)], [])
        layernorm_all(after=store)

    def att_idx(g, t):
        qt = g * NTG + t
        q0 = qt * T
        nk = q0 + T
        tc = slice(t * T, (t + 1) * T)
        for h in range(8):
            P.add("pool", lambda h=h: nc.gpsimd.tensor_scalar(dg[0:T, h, :], identb[0:T, 0:T], wabs[:, t, h:h + 1], None, ALU.mult),
                  R=["identb", ("wabs", t)], W=[("dg", h)])
        nch = (nk + 511) // 512
        items = [(ci, hp) for ci in range(nch) for hp in range(4)]

        def geom(ci):
            c0 = ci * 512
            c1 = min(nk, c0 + 512)
            return c0, c1, c1 - c0

        def X(j):
            ci, hp = items[j]
            c0, c1, w = geom(ci)
            b0 = [0, 4, 6][j % 3]
            ri = j % 3
            KIK = [("kiT", gg) for gg in range(c0 // G, (c1 - 1) // G + 1)]
            for hh in range(2):
                h = 2 * hp + hh
                P.add("pe", lambda: nc.tensor.matmul(ps[b0 + hh][0:T, 0:w], qiz[:, h, tc], kiT[:, c0:c1], start=True, stop=True),
                      R=[("qiT", hp)] + KIK, W=[("ps", b0 + hh)])
            if j % 8 in (1, 3, 4, 6):
                P.add("dve", lambda: nc.vector.tensor_scalar(rr[ri][0:T, :, 0:w], psall[0:T, b0:b0 + 2, 0:w], 0.0, None, ALU.max),
                      R=[("ps", b0), ("ps", b0 + 1)], W=[("rr", ri)])
            else:
                P.add("act", lambda: nc.scalar.activation(rr[ri][0:T, :, 0:w], psall[0:T, b0:b0 + 2, 0:w], AF.Relu),
                      R=[("ps", b0), ("ps", b0 + 1)], W=[("rr", ri)])

        def A(j):
            ci, hp = items[j]
            c0, c1, w = geom(ci)
            pacc = 2 + ci % 2
            ri = j % 3
            for hh in range(2):
                h = 2 * hp + hh
                P.add("pe", lambda: nc.tensor.matmul(ps[pacc][0:T, 0:w], dg[:, h, :], rr[ri][:, hh, 0:w], start=(h == 0), stop=(h == 7)),
                      R=[("dg", h), ("rr", ri)], W=[("ps", pacc)])
            if hp == 3:
                if ci % 2 == 0:
                    P.add("act", lambda: nc.scalar.copy(scores[0:T, c0:c1], ps[pacc][0:T, 0:w]), R=[("ps", pacc)], W=[("sc", ci)])
                else:
                    P.add("dve", lambda: nc.vector.tensor_copy(scores[0:T, c0:c1], ps[pacc][0:T, 0:w]), R=[("ps", pacc)], W=[("sc", ci)])
        LA = 2
        for j in range(min(LA, len(items))):
            X(j)
        for j in range(len(items)):
            if j + LA < len(items):
                X(j + LA)
            A(j)
        SCR = [("sc", ci) for ci in range(nch)]
        dci = sorted(set([q0 // 512, (nk - 1) // 512]))
        DCK = [("sc", ci) for ci in dci]
        blk = q0 >= 1024
        if not blk:
            P.add("dve", lambda: nc.vector.tensor_tensor(t114[:], scores[0:T, q0:nk], trip[:], ALU.add), R=DCK + ["trip"], W=["t114"])
            P.add("dve", lambda: nc.vector.tensor_reduce(m1[:], t114[:], AX.X, ALU.min), R=["t114"], W=["m1"])
            if q0 > 0:
                P.add("dve", lambda: nc.vector.tensor_reduce(m0[:], scores[0:T, 0:q0], AX.X, ALU.min), R=SCR, W=["m0"])
                P.add("dve", lambda: nc.vector.tensor_tensor(lo[:], m0[:], m1[:], ALU.min), R=["m0", "m1"], W=["lo"])
            else:
                P.add("dve", lambda: nc.vector.tensor_copy(lo[:], m1[:]), R=["m1"], W=["lo"])
            P.add("dve", lambda: nc.vector.tensor_tensor(scores[0:T, q0:nk], scores[0:T, q0:nk], trin[:], ALU.add), R=DCK + ["trin", "t114"], W=DCK)
            P.add("dve", lambda: nc.vector.tensor_reduce(rmax[:], scores[0:T, 0:nk], AX.X, ALU.max), R=SCR, W=["rmax"])
        else:
            bs = q0 // 32
            P.add("dve", lambda: nc.vector.tensor_tensor(scores[0:T, q0:nk], scores[0:T, q0:nk], trin[:], ALU.add), R=DCK + ["trin"], W=DCK)
            for bk in range(32):
                P.add("dve", lambda bk=bk: nc.vector.max(out=mx[:, bk * 8:(bk + 1) * 8], in_=scores[0:T, bk * bs:(bk + 1) * bs]), R=SCR, W=[("mx", bk), "zt", "zt0"] if bk == 0 else [("mx", bk)])
            MXK = [("mx", bk) for bk in range(32)]
            P.add("dve", lambda: nc.vector.tensor_reduce(m8t, mx.rearrange("p (b e) -> p b e", e=8), AX.X, ALU.min), R=MXK + ["zt", "zt0"], W=["m8t"])
            P.add("dve", lambda: nc.vector.tensor_reduce(lo[:], m8t, AX.X, ALU.min), R=["m8t", "zt", "zt0"], W=["lo"])
            P.add("dve", lambda: nc.vector.tensor_reduce(m0[:], m8t, AX.X, ALU.max), R=["m8t", "zt", "zt0"], W=["m0"])
            P.add("dve", lambda: nc.vector.tensor_reduce(m1[:], scores[0:T, 32 * bs:nk], AX.X, ALU.max), R=SCR, W=["m1"])
            P.add("dve", lambda: nc.vector.tensor_tensor(rmax[:], m0[:], m1[:], ALU.max), R=["m0", "m1"], W=["rmax"])
        P.add("dve", lambda: nc.vector.tensor_tensor(rng[:], rmax[:], lo[:], ALU.subtract), R=["rmax", "lo"], W=["rng"])
        P.add("dve", lambda: nc.vector.tensor_scalar(steps[:], pw2[:], rng[:], None, ALU.mult), R=["pw2", "rng"], W=["steps"])
        return dict(g=g, t=t, qt=qt, q0=q0, nk=nk, tc=tc, SCR=SCR, nbis=(NBIS_BLK if blk else NBIS), MXK=(MXK if blk else []))


    def att_bis(c, act_share):
        qt, nk, SCR = c["qt"], c["nk"], c["SCR"]
        bi = qt % 2
        nmb = nm8[bi]
        kA, kB, kM = "nmA%d" % bi, "nmB%d" % bi, "nm%d" % bi
        split = nk >= 1400
        ca = (int(nk * (1.0 - act_share)) // 2) * 2 if split else nk
        if split:
            nact = nk - ca
            P.add("dve", lambda: nc.vector.tensor_scalar(kadj[:], kcnt[:, qt:qt + 1], float(-0.5 - 0.5 * nact), None, ALU.add), R=["kcnt"], W=["kadj"])
        for it in range(c["nbis"]):
            P.add("dve", lambda it=it: nc.vector.tensor_tensor(mid[:], lo[:], steps[:, it:it + 1], ALU.add), R=["lo", "steps"], W=["mid"])
            if split:
                P.add("dve", lambda it=it: nc.vector.tensor_scalar(negmid[:], lo[:], steps[:, it:it + 1], -1.0, ALU.add, ALU.mult), R=["lo", "steps"], W=["negmid"])
                P.add("act", lambda: nc.scalar.activation(nmb[0:T, ca:nk], scores[0:T, ca:nk], AF.Sign, bias=negmid[:], scale=1.0, accum_out=cnta[:], saturate=False),
                      R=SCR + ["negmid"], W=[kB, "cnta"])
            P.add("dve", lambda: nc.vector.tensor_scalar(nmb[0:T, 0:ca], scores[0:T, 0:ca], mid[:], None, ALU.is_ge, ALU.add, accum_out=cntc[:], saturate=False),
                  R=SCR + ["mid"], W=[kA, "cntc"])
            if split:
                P.add("dve", lambda: nc.vector.scalar_tensor_tensor(ctmp[:], cnta[:], 0.5, cntc[:], ALU.mult, ALU.add), R=["cnta", "cntc"], W=["ctmp"])
                P.add("dve", lambda: nc.vector.tensor_tensor(gec[:], ctmp[:], kadj[:], ALU.is_ge), R=["ctmp", "kadj"], W=["gec"])
            else:
                P.add("dve", lambda: nc.vector.tensor_tensor(gec[:], cntc[:], kcnt[:, qt:qt + 1], ALU.is_ge), R=["cntc", "kcnt"], W=["gec"])
            P.add("dve", lambda it=it: nc.vector.scalar_tensor_tensor(lo[:], gec[:], steps[:, it:it + 1], lo[:], ALU.mult, ALU.add), R=["gec", "steps", "lo"], W=["lo"])
            yield
        P.add("dve", lambda: nc.vector.tensor_scalar(nmb[:, 0:nk], scores[:, 0:nk], lo128[:], None, ALU.is_lt, saturate=False), R=SCR + ["lo"], W=[kM, kA, kB])
        if debug and "sc" in debug and qt == debug.get("_qt", 0):
            sp_dma(dbg_d["sc"][:, 0:nk], scores[0:T, 0:nk], SCR, [])
            sp_dma(dbg_d["lo"][:, :], lo[:], ["lo"], [])

    def att_att(c):
        qt, t, tc = c["qt"], c["t"], c["tc"]
        bi = qt % 2
        nmb = nm8[bi]
        kM = "nm%d" % bi
        QK = [("qT", r) for r in range(4)]

        def QKm(kt):
            k0 = kt * T
            sset = kt % 2
            b0 = [0, 4][sset]
            for gq in range(2):
                b = b0 + gq
                P.add("pe", lambda: nc.tensor.matmul(ps[b][0:T, 0:G], kT[:, k0:k0 + T], qz[:, gq, :, tc], start=True, stop=False),
                      R=[("kT", k0 // G)] + QK, W=[("ps", b)])
                P.add("pe", lambda: nc.tensor.matmul(ps[b][0:T, 0:G], nmb[:, k0:k0 + T], I4[:], start=False, stop=True),
                      R=[kM, "I4"], W=[("ps", b)])
            P.add("act", lambda: nc.scalar.activation(ee[sset][0:T, :, :], psall[0:T, b0:b0 + 2, 0:G], AF.Exp, scale=0.125),
                  R=[("ps", b0), ("ps", b0 + 1)], W=[("ee", sset)])

        def PVm(kt):
            sset = kt % 2
            for gq in range(2):
                P.add("pe", lambda: nc.tensor.matmul(ps[6 + gq][0:65, 0:G], Vc[:, kt, gq * 65:(gq + 1) * 65], ee[sset][:, gq, :], start=(kt == 0), stop=(kt == qt)),
                      R=[("V", kt), ("ee", sset)], W=[("ps", 6 + gq)])
        QKm(0)
        for kt in range(qt + 1):
            if kt + 1 <= qt:
                QKm(kt + 1)
            PVm(kt)
            yield
        for gq in range(2):
            P.add("dve", lambda: nc.vector.reciprocal(rden[64:65, :], ps[6 + gq][64:65, 0:G]), R=[("ps", 6 + gq)], W=["rden"])
            P.add("pe", lambda: nc.tensor.matmul(ps[2][0:64, 0:G], ones32[64:65, 0:64], rden[64:65, :], start=True, stop=True),
                  R=["rden", "ones32"], W=[("ps", 2)])
            P.add("act", lambda: nc.scalar.copy(bcs[:], ps[2][0:64, 0:G]), R=[("ps", 2)], W=[("tmpb", 1)])
            P.add("dve", lambda: nc.vector.tensor_tensor(attT[:, gq * 4:(gq + 1) * 4, tc], ps[6 + gq][0:64, 0:G].rearrange("p (r t) -> p r t", r=4), bcs[:].rearrange("p (r t) -> p r t", r=4), ALU.mult),
                  R=[("ps", 6 + gq), ("tmpb", 1)], W=[("attT", gq * 4 + r) for r in range(4)])

    try:
        main()
        for g in range(NG):
            group(g)
    except _Stop:
        pass
    P.emit(stack)
    return nc, stack, P


def host_consts():
    half = 32
    inv_freq = (np.float32(10000.0) ** (-np.arange(half, dtype=np.float32) / np.float32(half))).astype(np.float32)
    pos = np.arange(L, dtype=np.float32)
    ang = (pos[:, None] * inv_freq[None, :]).astype(np.float32).astype(np.float64)
    cos = np.cos(ang).astype(np.float32)
    sin = np.sin(ang).astype(np.float32)
    cosT = np.zeros((128, L), np.float32)
    sinT = np.zeros((128, L), np.float32)
    for p in range(128):
        d = p % 64
        cosT[p] = cos[:, d % 32]
        sinT[p] = (-sin[:, d % 32]) if d < 32 else sin[:, d % 32]
    r = np.arange(T)
    trin = np.where(r[None, :] <= r[:, None], 0.0, -1e30).astype(np.float32)
    trip = np.where(r[None, :] <= r[:, None], 0.0, 1e30).astype(np.float32)
    posq = (np.arange(NTILE)[None, :] * T + r[:, None])
    kcnt = np.minimum(256, posq + 1).astype(np.float32)
    pw2 = np.tile((2.0 ** -(np.arange(NBIS) + 1.0)).astype(np.float32)[None, :], (T, 1))
    return dict(cosT=cosT, sinT=sinT, ident=np.eye(128, dtype=np.float32), trin=trin, trip=trip,
                kcnt=np.ascontiguousarray(kcnt), pw2=np.ascontiguousarray(pw2))


def make_in_maps(inputs, cores):
    c = host_consts()
    f = lambda a: np.ascontiguousarray(np.asarray(a, dtype=np.float32))
    shared = dict(
        meta=f(inputs["meta_tokens"]),
        f1g=f(inputs["ffn1_w_gate"][0]), f1u=f(inputs["ffn1_w_up"][0]), f1d=f(inputs["ffn1_w_down"][0]),
        f2g=f(inputs["ffn2_w_gate"][0]), f2u=f(inputs["ffn2_w_up"][0]), f2d=f(inputs["ffn2_w_down"][0]),
        win=f(inputs["w_in"][0]), wao=f(inputs["w_att_out"][0]), wco=f(inputs["w_conv_out"][0]), wo=f(inputs["w_o"][0]),
        ln1g=f(inputs["ln1_g"]), ln1b=f(inputs["ln1_b"]), ln2g=f(inputs["ln2_g"]), ln2b=f(inputs["ln2_b"]),
        ln3g=f(inputs["ln3_g"]), ln3b=f(inputs["ln3_b"]),
        convw=f(np.asarray(inputs["conv_w"])[0].reshape(3, 4, 128).transpose(2, 1, 0).reshape(128, 12)),
        **c)
    maps = []
    for b in cores:
        m = dict(shared)
        m["x"] = f(inputs["x"][b])
        maps.append(m)
    return maps


def kernel(**inputs):
    nc, stack, P = build(NG_FULL)
    cores = list(range(8))
    in_maps = make_in_maps(inputs, cores)
    with stack:
        res = run_bass_kernel_spmd(nc, in_maps, core_ids=cores)
    out = np.stack([np.asarray(r["out"], dtype=np.float32) for r in res.results], axis=0)
    return out
```

```python
import types
import numpy as np
from contextlib import ExitStack
import concourse.bass as bass
import concourse.mybir as mybir
from concourse.bass_utils import run_bass_kernel_spmd

F32 = mybir.dt.float32
BF16 = mybir.dt.bfloat16
F8 = mybir.dt.float8e5
ALU = mybir.AluOpType
AF = mybir.ActivationFunctionType
AX = mybir.AxisListType

T = 114
NTG = 4
G = T * NTG
NG_FULL = 18
L = 8208
NTILE = 72
D = 1024
KC = 8
DFF = 2816
NFC = 22
NMETA = 16
SEQ = 8192
ALPHA = float(2.0 ** 0.25)
EPS = 1e-5
NBIS = 16
NBIS_BLK = 12
NEGM = -28672.0
QO, QIO, KKO, CVO, GAO, GCO, VWO, WPC = 0, 1024, 2048, 2560, 4096, 5120, 6144, 6280
SQ, SK, SV, SQI, SKI, SWI, SU, SGB, SGC, SGA, SGV = 0, 512, 640, 768, 1280, 1344, 1352, 1864, 2376, 2888, 3912
NRING = 3
ND = 8


def _freeze(fn):
    if fn.__closure__ is None:
        return fn
    cells = []
    for c in fn.__closure__:
        try:
            cells.append(types.CellType(c.cell_contents))
        except ValueError:
            cells.append(c)
    return types.FunctionType(fn.__code__, fn.__globals__, fn.__name__, fn.__defaults__, tuple(cells))


class Prog:
    ENGS = ("pe", "act", "dve", "pool", "sp")

    def __init__(self, nc):
        self.nc = nc
        self.ops = []
        self.last_w = {}
        self.readers = {}

    def add(self, eng, fn, R=(), W=(), dma=False):
        i = len(self.ops)
        deps = set()
        for r in R:
            lw = self.last_w.get(r)
            if lw is not None:
                deps.add(lw)
        for w in W:
            lw = self.last_w.get(w)
            if lw is not None:
                deps.add(lw)
            rs = self.readers.get(w)
            if rs:
                deps |= rs
        for r in R:
            self.readers.setdefault(r, set()).add(i)
        for w in W:
            self.last_w[w] = i
            self.readers[w] = set()
        deps.discard(i)
        self.ops.append([eng, _freeze(fn), deps, dma])
        return i

    def transfer(self, src_keys, dst_keys):
        acc = set()
        for k in src_keys:
            lw = self.last_w.get(k)
            if lw is not None:
                acc.add(lw)
            acc |= self.readers.get(k, set())
        for k in dst_keys:
            self.readers.setdefault(k, set()).update(acc)

    def emit(self, stack):
        nc = self.nc
        ops = self.ops
        engobj = {"pe": nc.tensor, "act": nc.scalar, "dve": nc.vector, "pool": nc.gpsimd, "sp": nc.sync}
        n = len(ops)
        needed = [False] * n
        red = []
        for i, (eng, fn, deps, dma) in enumerate(ops):
            best = {}
            dl = []
            for d in deps:
                pe_, _, _, pdma = ops[d]
                if pdma:
                    dl.append(d)
                else:
                    if pe_ == "pe" and eng == "pe" and not dma:
                        continue
                    if best.get(pe_, -1) < d:
                        best[pe_] = d
            dl.extend(best.values())
            red.append(dl)
            for d in dl:
                needed[d] = True
        esem = {e: stack.enter_context(nc.semaphore("s_" + e)) for e in self.ENGS}
        dsem = {e: [stack.enter_context(nc.semaphore("d_%s%d" % (e, k))) for k in range(ND)] for e in self.ENGS}
        cnt = {e: 0 for e in self.ENGS}
        dcnt = {e: 0 for e in self.ENGS}
        dhist = {e: [] for e in self.ENGS}
        sig = [None] * n
        prevdma = [None] * n
        for i, (eng, fn, deps, dma) in enumerate(ops):
            if dma:
                k = dcnt[eng]
                dcnt[eng] += 1
                sig[i] = (dsem[eng][k % ND], 16 * (k // ND + 1))
                if k >= ND:
                    prevdma[i] = dhist[eng][k - ND]
                dhist[eng].append(i)
            elif needed[i]:
                cnt[eng] += 1
                sig[i] = (esem[eng], cnt[eng])
        seen = {e: {} for e in self.ENGS}
        for i, (eng, fn, deps, dma) in enumerate(ops):
            eo = engobj[eng]
            waits = {}
            dl = list(red[i])
            if prevdma[i] is not None:
                dl.append(prevdma[i])
            for d in dl:
                s, v = sig[d]
                key = id(s)
                if key not in waits or waits[key][1] < v:
                    waits[key] = (s, v)
            for key, (s, v) in waits.items():
                if seen[eng].get(key, 0) >= v:
                    continue
                eo.wait_ge(s, v)
                seen[eng][key] = v
            ins = fn()
            if dma:
                ins.then_inc(sig[i][0], 16)
            elif needed[i]:
                ins.then_inc(sig[i][0], 1)
        for e in self.ENGS:
            for i in dhist[e][-ND:]:
                s, v = sig[i]
                if seen["sp"].get(id(s), 0) < v:
                    nc.sync.wait_ge(s, v)
                    seen["sp"][id(s)] = v
        self.stats = dict(n_ops=n, cnt=cnt, dcnt=dcnt)


class _Stop(Exception):
    pass


def build(NG=NG_FULL, debug=None, stop=None):
    nc = bass.Bass("TRN2", target_bir_lowering=False)
    stack = ExitStack()
    P = Prog(nc)

    def chk(name):
        if stop == name:
            raise _Stop()

    def dram_in(name, shape):
        return nc.dram_tensor(name, list(shape), F32, kind="ExternalInput").ap()

    x_d = dram_in("x", [SEQ, D])
    meta_d = dram_in("meta", [NMETA, D])
    wsrc = {}
    for nm, shp in [("f1g", [D, DFF]), ("f1u", [D, DFF]), ("f1d", [DFF, D]),
                    ("f2g", [D, DFF]), ("f2u", [D, DFF]), ("f2d", [DFF, D]),
                    ("win", [D, 4936]), ("wao", [512, D]), ("wco", [512, D]), ("wo", [D, D])]:
        wsrc[nm] = dram_in(nm, shp)
    lnp = {nm: dram_in(nm, [1, D]) for nm in ["ln1g", "ln1b", "ln2g", "ln2b", "ln3g", "ln3b"]}
    convw_d = dram_in("convw", [128, 12])
    cos_d = dram_in("cosT", [128, L])
    sin_d = dram_in("sinT", [128, L])
    ident_d = dram_in("ident", [128, 128])
    trin_d = dram_in("trin", [T, T])
    trip_d = dram_in("trip", [T, T])
    kcnt_d = dram_in("kcnt", [T, NTILE])
    pw2_d = dram_in("pw2", [T, NBIS])
    out_d = nc.dram_tensor("out", [SEQ, D], F32, kind="ExternalOutput").ap()
    dbg_d = {}
    if debug:
        for nm, shp in debug.items():
            dbg_d[nm] = nc.dram_tensor("dbg_" + nm, list(shp), F32, kind="ExternalOutput").ap()

    def scratch(name, shape):
        return nc.dram_tensor(name, list(shape), BF16, kind="Internal").ap()

    wsc = {"f1g": scratch("s_f1g", [D, DFF]), "f1u": scratch("s_f1u", [D, DFF]), "f1d": scratch("s_f1d", [DFF, D]),
           "f2g": scratch("s_f2g", [D, DFF]), "f2u": scratch("s_f2u", [D, DFF]), "f2d": scratch("s_f2d", [DFF, D]),
           "wp": scratch("s_wp", [D, WPC]), "wao": scratch("s_wao", [512, D]), "wco": scratch("s_wco", [512, D]),
           "wo": scratch("s_wo", [D, D])}

    def sb(name, shape, dt=F32):
        return stack.enter_context(nc.sbuf_tensor("sb_" + name, list(shape), dt))

    kT = sb("kT", [128, L], BF16)
    kiT = sb("kiT", [128, L], BF16)
    Vc = sb("Vc", [128, NTILE, 130], BF16)
    S = sb("S", [T, NTG, D], F32)
    aT = sb("aT", [128, KC, G], BF16)
    xbf = [sb("xbf0", [T, D], BF16)] * 2
    hT = sb("hT", [128, NFC, G], BF16)
    qiz = sb("qiz", [128, 8, G], BF16)
    ycv = sb("ycv", [128, 4, G], BF16)
    attT = sb("attT", [64, 8, G], BF16)
    scores = sb("scores", [128, L], F32)
    ring = [sb("ring%d" % i, [128, 4096], BF16) for i in range(NRING)]
    ropec = sb("ropec", [128, G], F32)
    ropes = sb("ropes", [128, G], F32)
    gbq = sb("gbq", [128, 2 * D], F32)
    gb = gbq[0:T, :].rearrange("p (a d) -> p a d", a=2)
    qz = gbq[:].bitcast(BF16)[:, 0:2 * 4 * G].rearrange("p (g r t) -> p g r t", g=2, r=4)
    tmpa = [sb("tmpa%d" % i, [128, G], F32) for i in range(2)]
    sg = tmpa
    tmpb = [sb("tmpb%d" % i, [128, G], F32) for i in range(2)]
    bcs = tmpb[1][0:64, :]
    zt = sb("zt", [128, G + 2], F32)
    halo = sb("halo", [128, 4, 2], F32)
    convw = sb("convw", [128, 12], F32)
    identf = sb("identf", [128, 128], F32)
    identb = sb("identb", [128, 128], BF16)
    I4 = sb("I4", [128, G], F8)
    trin = sb("trin", [T, T], F32)
    trip = sb("trip", [T, T], F32)
    kcnt = sb("kcnt", [T, NTILE], F32)
    pw2 = sb("pw2", [T, NBIS], F32)
    ones32 = sb("ones32", [128, 64], F32)
    neghalf = sb("neghalf", [128, NTG], F32)
    st = sb("st", [T, NTG, 12], F32)
    mv = sb("mv", [T, NTG, 2], F32)
    ve = sb("ve", [T, NTG], F32)
    rstd = sb("rstd", [T, NTG], F32)
    wabs = sb("wabs", [T, NTG, 8], F32)
    dg = sb("dg", [128, 8, T], BF16)
    rr = [sb("rr%d" % i, [128, 2, 512], BF16) for i in range(3)]
    ee = [sb("ee%d" % i, [128, 2, G], BF16) for i in range(2)]
    t114 = sb("t114", [T, T], F32)
    mx = zt[0:T, 0:256]
    m8t = zt[0:T, 256:288]
    m0 = sb("m0", [T, 1], F32)
    m1 = sb("m1", [T, 1], F32)
    lo128 = sb("lo", [128, 1], F32)
    lo = lo128[0:T, :]
    rmax = sb("rmax", [T, 1], F32)
    rng = sb("rng", [T, 1], F32)
    steps = sb("steps", [T, NBIS], F32)
    mid = sb("mid", [T, 1], F32)
    cntc = sb("cntc", [T, 1], F32)
    cnta = sb("cnta", [T, 1], F32)
    negmid = sb("negmid", [T, 1], F32)
    kadj = sb("kadj", [T, 1], F32)
    ctmp = sb("ctmp", [T, 1], F32)
    gec = sb("gec", [T, 1], F32)
    rden = sb("rden", [128, G], F32)
    psall = stack.enter_context(nc.psum_tensor("psall", [128, 8, 512], F32))
    ps = [psall[:, b, :] for b in range(8)]
    ps7b = psall[:, 7, :].bitcast(BF16)
    h8 = hT[:].rearrange("p a b -> p (a b)").bitcast(F8)
    nm8 = [h8[:, 0:L], h8[:, L:2 * L]]

    PSK = [("ps", b) for b in range(8)]
    HTK = [("hT", f) for f in range(NFC)]

    sp_dma = lambda out, in_, R, W: P.add("sp", lambda: nc.sync.dma_start(out=out, in_=in_), R=R, W=W, dma=True)
    st_dma = lambda out, in_, R, W: P.add("pool", lambda: nc.gpsimd.dma_start(out=out, in_=in_), R=R, W=W, dma=True)

    wsc_barrier = []
    SCK = [("sc", c) for c in range(17)]

    def main():
        sp_dma(identf[:], ident_d, [], ["identf"])
        sp_dma(trin[:], trin_d, [], ["trin"])
        sp_dma(trip[:], trip_d, [], ["trip"])
        sp_dma(kcnt[:], kcnt_d, [], ["kcnt"])
        sp_dma(pw2[:], pw2_d, [], ["pw2"])
        sp_dma(convw[:], convw_d, [], ["convw"])
        P.add("dve", lambda: nc.vector.tensor_copy(identb[:], identf[:]), R=["identf"], W=["identb"])
        P.add("dve", lambda: nc.vector.memset(I4[:], 0.0), W=["I4"])
        for r in range(4):
            P.add("dve", lambda r=r: nc.vector.tensor_scalar(I4[0:T, r * T:(r + 1) * T], identf[0:T, 0:T], NEGM, None, ALU.mult, saturate=False), R=["identf"], W=["I4"])
        P.add("pool", lambda: nc.gpsimd.memset(qiz[:], 0.0), W=[("qiT", c) for c in range(4)])
        P.add("pool", lambda: nc.gpsimd.memset(lo128[:], 0.0), W=["lo"])
        P.add("pool", lambda: nc.gpsimd.memset(dg[:], 0.0), W=[("dg", h) for h in range(8)])
        for i in range(3):
            P.add("pool", lambda i=i: nc.gpsimd.memset(rr[i][:], 0.0), W=[("rr", i)])
        for i in range(2):
            P.add("pool", lambda i=i: nc.gpsimd.memset(ee[i][:], 0.0), W=[("ee", i)])
        P.add("pool", lambda: nc.gpsimd.memset(ones32[:], 1.0), W=["ones32"])
        P.add("pool", lambda: nc.gpsimd.memset(neghalf[:], -0.5), W=["neghalf"])
        P.add("pool", lambda: nc.gpsimd.memset(halo[:], 0.0), W=["halo"])
        P.add("pool", lambda: nc.gpsimd.memset(Vc[:], 0.0), W=[("V", t) for t in range(NTILE)])
        P.add("pool", lambda: nc.gpsimd.memset(Vc[0:T], 1.0), W=[("V", t) for t in range(NTILE)])

        chk('const')
        stage32 = [scores[:, 0:4104], scores[:, 4104:8208]]
        hflat = hT[:].rearrange("p a b -> p (a b)")
        stage16 = [hflat[:, 0:4104], hflat[:, 4104:8208]]
        ceng = ["act", "dve", "act", "dve", "pool"]
        cstate = [0]

        def cast(out, in_, R, W):
            e = ceng[cstate[0] % 5]
            cstate[0] += 1
            if e == "act":
                P.add("act", lambda: nc.scalar.copy(out, in_), R=R, W=W)
            elif e == "dve":
                P.add("dve", lambda: nc.vector.tensor_copy(out, in_), R=R, W=W)
            else:
                P.add("pool", lambda: nc.gpsimd.tensor_copy(out, in_), R=R, W=W)

        pj = [0]

        def prep_plain(src, dst, rows, cols):
            for rb in range(rows // 128):
                i = pj[0] % 2
                pj[0] += 1
                s32 = stage32[i][:, 0:cols]
                s16 = stage16[i][:, 0:cols]
                sp_dma(s32, src[rb * 128:(rb + 1) * 128, :], [], [("p32", i)])
                h = cols // 2
                cast(s16[:, 0:h], s32[:, 0:h], [("p32", i)], [("p16", i, 0)])
                cast(s16[:, h:cols], s32[:, h:cols], [("p32", i)], [("p16", i, 1)])
                st_dma(dst[rb * 128:(rb + 1) * 128, :], s16, [("p16", i, 0), ("p16", i, 1)], [])

        for nm in ["f1g", "f1u"]:
            prep_plain(wsrc[nm], wsc[nm], D, DFF)
        prep_plain(wsrc["f1d"], wsc["f1d"], DFF, D)

        s32 = scores[:, 0:4936]
        s16 = hflat[:, 0:WPC]
        for rb in range(8):
            K32 = [("p32", 0), ("p32", 1)]
            K16 = [("p16", 0, 0), ("p16", 0, 1), ("p16", 1, 0), ("p16", 1, 1)]
            sp_dma(s32, wsrc["win"][rb * 128:(rb + 1) * 128, :], [], K32)
            if rb == 0:
                P.add("pool", lambda: nc.gpsimd.memset(ve[:], 0.0), R=[], W=K16 + ["wpbar", "ve"])

            def cp(dst, src, tag):
                cast(dst, src, K32 + ["wpbar"], [("wpw", tag)])
            qsrc = s32[:, SQ:SQ + 512].rearrange("p (j a e) -> p a j e", j=2, a=4, e=64)
            for c2 in range(2):
                dA = s16[:, QO + c2 * 512: QO + c2 * 512 + 256].rearrange("p (a j e) -> p a j e", a=2, j=2, e=64)
                dB = s16[:, QO + c2 * 512 + 256: QO + c2 * 512 + 512].rearrange("p (a j e) -> p a j e", a=2, j=2, e=64)
                for a in range(2):
                    cp(dA[:, a], qsrc[:, 2 * c2 + a], ("qA", c2, a))
                    for hf in range(2):
                        cp(dB[:, a, :, hf * 32:(hf + 1) * 32], qsrc[:, 2 * c2 + a, :, (1 - hf) * 32:(2 - hf) * 32], ("qB", c2, a, hf))
            for c2 in range(2):
                cp(s16[:, QIO + c2 * 512: QIO + c2 * 512 + 256], s32[:, SQI + c2 * 256: SQI + c2 * 256 + 256], ("qiA", c2))
                dB = s16[:, QIO + c2 * 512 + 256: QIO + c2 * 512 + 512].rearrange("p (h f e) -> p h f e", h=4, f=2, e=32)
                sB = s32[:, SQI + c2 * 256: SQI + c2 * 256 + 256].rearrange("p (h f e) -> p h f e", h=4, f=2, e=32)
                for hf in range(2):
                    cp(dB[:, :, hf], sB[:, :, 1 - hf], ("qiB", c2, hf))
            cp(s16[:, KKO:KKO + 128], s32[:, SK:SK + 128], "k")
            dB = s16[:, KKO + 128:KKO + 256].rearrange("p (h f e) -> p h f e", h=2, f=2, e=32)
            sB = s32[:, SK:SK + 128].rearrange("p (h f e) -> p h f e", h=2, f=2, e=32)
            for hf in range(2):
                cp(dB[:, :, hf], sB[:, :, 1 - hf], ("kB", hf))
            for cpy in range(2):
                cp(s16[:, KKO + 256 + cpy * 64: KKO + 256 + cpy * 64 + 64], s32[:, SKI:SKI + 64], ("ki", cpy))
                for hf in range(2):
                    cp(s16[:, KKO + 384 + cpy * 64 + hf * 32: KKO + 384 + cpy * 64 + hf * 32 + 32],
                       s32[:, SKI + (1 - hf) * 32: SKI + (1 - hf) * 32 + 32], ("kiB", cpy, hf))
            dC = s16[:, CVO:CVO + 1536].rearrange("p (c m e) -> p c m e", c=4, m=3, e=128)
            for m, so in enumerate([SU, SGC, SGB]):
                cp(dC[:, :, m], s32[:, so:so + 512].rearrange("p (c e) -> p c e", c=4, e=128), ("cv", m))
            cp(s16[:, GAO:GAO + 1024], s32[:, SGA:SGA + 1024], "ga")
            cp(s16[:, GCO:GCO + 1024], s32[:, SGV:SGV + 1024], "gc")
            cp(s16[:, VWO:VWO + 128], s32[:, SV:SV + 128], "v")
            cp(s16[:, VWO + 128:VWO + 136], s32[:, SWI:SWI + 8], "wi")
            tags = [k for k in P.last_w.keys() if isinstance(k, tuple) and k and k[0] == "wpw"]
            st_dma(wsc["wp"][rb * 128:(rb + 1) * 128, :], s16, tags, K16)

        for nm, rows in [("wao", 512), ("wco", 512), ("wo", D)]:
            prep_plain(wsrc[nm], wsc[nm], rows, D)
        for nm in ["f2g", "f2u"]:
            prep_plain(wsrc[nm], wsc[nm], D, DFF)
        prep_plain(wsrc["f2d"], wsc["f2d"], DFF, D)
        allprep = [("p32", 0), ("p32", 1), ("p16", 0, 0), ("p16", 0, 1), ("p16", 1, 0), ("p16", 1, 1)] + \
            [k for k in P.last_w.keys() if isinstance(k, tuple) and k and k[0] == "wpw"]
        P.transfer(allprep, HTK + SCK + ["nm0", "nm1", "nmA0", "nmA1", "nmB0", "nmB1"])
        prep_stores = [i for i, o in enumerate(P.ops) if o[3]]
        wsc_barrier.extend(prep_stores)


    rstate = [0]
    first_loads = [True]

    def ring_load(parts, nh=2):
        if nh == 2 and rstate[0] % 2 == 1:
            rstate[0] += 1
        idx = rstate[0] % (2 * NRING)
        rstate[0] += nh
        s, h0 = idx // 2, idx % 2
        win = ring[s][:, h0 * 2048:(h0 + nh) * 2048]
        for (vf, src, hk) in parts:
            hks = hk if isinstance(hk, tuple) else (hk,)
            i = sp_dma(vf(win), src, [], [("rg", s, h0 + h_) for h_ in hks])
            P.ops[i][2].update(wsc_barrier)
        if nh == 2:
            return s
        return s, h0, win

    def v3(lo_, n, kc):
        return lambda r: r[:, lo_:lo_ + n * kc].rearrange("p (k f) -> p k f", k=kc)

    def to_featmajor(t):
        xb = xbf[0]
        P.add("act", lambda: nc.scalar.copy(xb[:], S[:, t, :]), R=[("S", t)], W=[("xbf", 0)])
        for kc in range(KC):
            P.add("pe", lambda kc=kc: nc.tensor.transpose(ps7b[:, kc * T:(kc + 1) * T], xb[:, kc * 128:(kc + 1) * 128], identb[0:T, 0:T]),
                  R=[("xbf", 0), "identb"], W=[("ps", 7)])
        P.add("dve", lambda: nc.vector.tensor_copy(aT[:, :, t * T:(t + 1) * T], ps7b[:, 0:KC * T].rearrange("p (k t) -> p k t", k=KC)),
              R=[("ps", 7)], W=[("aT", t)])

    ATK = [("aT", t) for t in range(NTG)]

    def ffn(wg, wu, wd, resid_scale):
        for j in range(11):
            s = ring_load([(v3(0, 256, KC), wg[:, 256 * j:256 * j + 256].rearrange("(k p) f -> p k f", p=128), 0),
                           (v3(2048, 256, KC), wu[:, 256 * j:256 * j + 256].rearrange("(k p) f -> p k f", p=128), 1)])
            gv = v3(0, 256, KC)(ring[s])
            uv = v3(2048, 256, KC)(ring[s])
            for f2 in range(2):
                fc = 2 * j + f2
                bg = fc % 2
                bu = 2 + fc % 2
                for kc in range(KC):
                    P.add("pe", lambda kc=kc, f2=f2, bg=bg: nc.tensor.matmul(ps[bg][:, 0:G], gv[:, kc, f2 * 128:(f2 + 1) * 128], aT[:, kc, :], start=(kc == 0), stop=(kc == KC - 1)),
                          R=[("rg", s, 0)] + ATK, W=[("ps", bg)])
                for kc in range(KC):
                    P.add("pe", lambda kc=kc, f2=f2, bu=bu: nc.tensor.matmul(ps[bu][:, 0:G], uv[:, kc, f2 * 128:(f2 + 1) * 128], aT[:, kc, :], start=(kc == 0), stop=(kc == KC - 1)),
                          R=[("rg", s, 1)] + ATK, W=[("ps", bu)])
                sgt = sg[fc % 2]
                P.add("act", lambda bg=bg, sgt=sgt: nc.scalar.activation(sgt[:], ps[bg][:, 0:G], AF.Silu), R=[("ps", bg)], W=[("tmpa", fc % 2)])
                P.add("dve", lambda bu=bu, sgt=sgt, fc=fc: nc.vector.scalar_tensor_tensor(hT[:, fc, :], sgt[:], resid_scale, ps[bu][:, 0:G], ALU.mult, ALU.mult),
                      R=[("tmpa", fc % 2), ("ps", bu)], W=[("hT", fc)])
        for j in range(11):
            s, h0, win = ring_load([(lambda r: r[:, 0:2048].rearrange("p (c d) -> p c d", c=2),
                                    wd[256 * j:256 * j + 256, :].rearrange("(c p) d -> p c d", p=128), 0)], nh=1)
            wv = win.rearrange("p (c d) -> p c d", c=2)
            for c in range(2):
                fc = 2 * j + c
                for t in range(NTG):
                    for dh in range(2):
                        b = t * 2 + dh
                        P.add("pe", lambda c=c, fc=fc, t=t, dh=dh, b=b, wv=wv: nc.tensor.matmul(ps[b][0:T, :], hT[:, fc, t * T:(t + 1) * T], wv[:, c, dh * 512:(dh + 1) * 512], start=(fc == 0), stop=(fc == NFC - 1)),
                              R=[("rg", s, h0), ("hT", fc)], W=[("ps", b)])
        for t in range(NTG):
            for dh in range(2):
                b = t * 2 + dh
                P.add("dve", lambda t=t, dh=dh, b=b: nc.vector.scalar_tensor_tensor(S[:, t, dh * 512:(dh + 1) * 512], S[:, t, dh * 512:(dh + 1) * 512], ALPHA, ps[b][0:T, :], ALU.mult, ALU.add),
                      R=[("ps", b), ("S", t)], W=[("S", t)])

    def load_gb(gname, bname):
        sp_dma(gb[:, 0, :], lnp[gname].partition_broadcast(T), [], ["gb"])
        sp_dma(gb[:, 1, :], lnp[bname].partition_broadcast(T), [], ["gb2"])

    def layernorm_all(after=None):
        for t in range(NTG):
            for hh in range(2):
                P.add("dve", lambda t=t, hh=hh: nc.vector.bn_stats(st[:, t, hh * 6:(hh + 1) * 6], S[:, t, hh * 512:(hh + 1) * 512]), R=[("S", t)], W=[("st", t, hh)])
            P.add("dve", lambda t=t: nc.vector.bn_aggr(mv[:, t, :], st[:, t, :]), R=[("st", t, 0), ("st", t, 1)], W=[("mv", t)])
        MVK = [("mv", t) for t in range(NTG)]
        P.add("pool", lambda: nc.gpsimd.tensor_scalar(ve[:], mv[:, :, 1], EPS, None, ALU.add), R=MVK, W=["ve"])
        P.add("pool", lambda: nc.gpsimd.tensor_tensor(rstd[:], ve[:], neghalf[0:T, :], ALU.pow), R=["ve", "neghalf"], W=["rstd"])
        for t in range(NTG):
            St = S[:, t, :]
            P.add("dve", lambda t=t, St=St: nc.vector.scalar_tensor_tensor(St, St, mv[:, t, 0:1], gb[:, 0, :], ALU.subtract, ALU.mult), R=[("S", t), ("mv", t), "gb"], W=[("S", t)])
            P.add("dve", lambda t=t, St=St: nc.vector.scalar_tensor_tensor(St, St, rstd[:, t:t + 1], gb[:, 1, :], ALU.mult, ALU.add), R=[("S", t), "rstd", "gb2"], W=[("S", t)])
            if after is not None:
                after(t)

    def wp_piece(col0, ncols):
        return ring_load([(v3(0, ncols, KC), wsc["wp"][:, col0:col0 + ncols].rearrange("(k p) f -> p k f", p=128), (0, 1))])

    def proj_fm(s, off, bank):
        wv = v3(0, 512, KC)(ring[s])
        for kc in range(KC):
            P.add("pe", lambda kc=kc: nc.tensor.matmul(ps[bank][:, 0:G], wv[:, kc, off:off + 128], aT[:, kc, :], start=(kc == 0), stop=(kc == KC - 1)),
                  R=[("rg", s, 0), ("rg", s, 1)] + ATK, W=[("ps", bank)])

    pairctr = [0]

    def rope_chunk(s, offA, offB, dst, dstR, dstW):
        i = pairctr[0] % 2
        pairctr[0] += 1
        bA, bB = 2 * i, 2 * i + 1
        proj_fm(s, offA, bA)
        proj_fm(s, offB, bB)
        ta, tb = tmpa[i], tmpb[i]
        P.add("dve", lambda: nc.vector.tensor_tensor(ta[:], ps[bA][:, 0:G], ropec[:], ALU.mult), R=[("ps", bA), "ropec"], W=[("tmpa", i)])
        P.add("dve", lambda: nc.vector.tensor_tensor(tb[:], ps[bB][:, 0:G], ropes[:], ALU.mult), R=[("ps", bB), "ropes"], W=[("tmpb", i)])
        if isinstance(dst, tuple):
            P.add("dve", lambda: nc.vector.tensor_tensor(dst[0], ta[0:64, :], tb[0:64, :], ALU.add), R=[("tmpa", i), ("tmpb", i)] + dstR, W=dstW)
            P.add("dve", lambda: nc.vector.tensor_tensor(dst[1], ta[64:128, :], tb[64:128, :], ALU.add), R=[("tmpa", i), ("tmpb", i)] + dstR, W=dstW)
        else:
            P.add("dve", lambda: nc.vector.tensor_tensor(dst, ta[:], tb[:], ALU.add), R=[("tmpa", i), ("tmpb", i)] + dstR, W=dstW)

    def group(g):
        p0 = g * G
        for t in range(NTG):
            pos0 = p0 + t * T
            if pos0 == 0:
                sp_dma(S[0:NMETA, t, :], meta_d, [], [("S", t)])
                sp_dma(S[NMETA:T, t, :], x_d[0:T - NMETA, :], [], [("S", t)])
            else:
                sp_dma(S[:, t, :], x_d[pos0 - NMETA:pos0 - NMETA + T, :], [], [("S", t)])
        for t in range(NTG):
            to_featmajor(t)
        chk('load')
        load_gb("ln1g", "ln1b")
        ffn(wsc["f1g"], wsc["f1u"], wsc["f1d"], 0.5)
        chk('ffn1')
        layernorm_all()
        if debug and "h1" in debug and g == 0:
            for t in range(NTG):
                sp_dma(dbg_d["h1"][t * T:(t + 1) * T, :], S[:, t, :], [("S", t)], [])
        for t in range(NTG):
            to_featmajor(t)
        chk('ln1')
        QTK = [("qT", r) for r in range(4)]
        P.transfer(["gb", "gb2"], QTK)
        P.add("pool", lambda: nc.gpsimd.memset(qz[64:128, 0], 0.0), W=QTK)
        P.add("pool", lambda: nc.gpsimd.memset(qz[0:64, 1], 0.0), W=QTK)
        sp_dma(ropec[:], cos_d[:, p0:p0 + G], [], ["ropec"])
        sp_dma(ropes[:], sin_d[:, p0:p0 + G], [], ["ropes"])
        for c2 in range(2):
            s = wp_piece(QIO + c2 * 512, 512)
            for a in range(2):
                c = 2 * c2 + a
                rope_chunk(s, a * 128, 256 + a * 128, (qiz[0:64, 2 * c, :], qiz[64:128, 2 * c + 1, :]), [], [("qiT", c)])
        chk('projq')
        s = wp_piece(KKO, 512)
        rope_chunk(s, 0, 128, kT[:, p0:p0 + G], [], [("kT", g)])
        rope_chunk(s, 256, 384, kiT[:, p0:p0 + G], [], [("kiT", g)])
        chk('projc')
        s, vh0, vwin = ring_load([(v3(0, 136, KC), wsc["wp"][:, VWO:VWO + 136].rearrange("(k p) f -> p k f", p=128), 0)], nh=1)
        wv = v3(0, 136, KC)(vwin)
        for t in range(NTG):
            qt = g * NTG + t
            for kc in range(KC):
                P.add("pe", lambda kc=kc, t=t: nc.tensor.matmul(ps[7][0:T, 0:136], aT[:, kc, t * T:(t + 1) * T], wv[:, kc, :], start=(kc == 0), stop=(kc == KC - 1)),
                      R=[("rg", s, vh0), ("aT", t)], W=[("ps", 7)])
            chk('pv1')
            P.add("act", lambda t=t, qt=qt: nc.scalar.copy(Vc[0:T, qt, :].rearrange("p (g e) -> p g e", g=2)[:, :, 0:64], ps[7][0:T, 0:128].rearrange("p (g e) -> p g e", g=2)),
                  R=[("ps", 7)], W=[("V", qt)])
            chk('pv2')
            P.add("act", lambda t=t: nc.scalar.activation(wabs[:, t, :], ps[7][0:T, 128:136], AF.Copy, scale=float(512 ** -0.5)), R=[("ps", 7)], W=[("wabs", t)])
        def late_q(c2):
            s = wp_piece(QO + c2 * 512, 512)
            for a in range(2):
                r = 2 * c2 + a
                rope_chunk(s, a * 128, 256 + a * 128, (qz[0:64, 0, r, :], qz[64:128, 1, r, :]), [], [("qT", r)])

        def late_conv(sl, c):
            offs = [(3 * c + m) * 128 for m in range(3)]
            for m in range(3):
                proj_fm(sl[offs[m] // 512], offs[m] % 512, 4 + m)
            P.add("act", lambda: nc.scalar.copy(tmpa[0][:], ps[4][:, 0:G]), R=[("ps", 4)], W=[("tmpa", 0)])
            P.add("pool", lambda c=c: nc.gpsimd.tensor_copy(zt[:, 0:2], halo[:, c, :]), R=["halo"], W=["zt0"])
            P.add("dve", lambda: nc.vector.tensor_tensor(zt[:, 2:G + 2], tmpa[0][:], ps[5][:, 0:G], ALU.mult), R=[("tmpa", 0), ("ps", 5)], W=["zt"])
            P.add("pool", lambda c=c: nc.gpsimd.tensor_copy(halo[:, c, :], zt[:, G:G + 2]), R=["zt", "zt0"], W=["halo"])
            P.add("dve", lambda c=c: nc.vector.tensor_scalar(tmpb[0][:], zt[:, 2:G + 2], convw[:, c * 3 + 2:c * 3 + 3], None, ALU.mult), R=["zt", "convw"], W=[("tmpb", 0)])
            P.add("dve", lambda c=c: nc.vector.scalar_tensor_tensor(tmpb[0][:], zt[:, 1:G + 1], convw[:, c * 3 + 1:c * 3 + 2], tmpb[0][:], ALU.mult, ALU.add), R=["zt", "zt0", "convw", ("tmpb", 0)], W=[("tmpb", 0)])
            P.add("dve", lambda c=c: nc.vector.scalar_tensor_tensor(tmpb[0][:], zt[:, 0:G], convw[:, c * 3:c * 3 + 1], tmpb[0][:], ALU.mult, ALU.add), R=["zt", "zt0", "convw", ("tmpb", 0)], W=[("tmpb", 0)])
            P.add("dve", lambda c=c: nc.vector.tensor_tensor(ycv[:, c, :], tmpb[0][:], ps[6][:, 0:G], ALU.mult), R=[("tmpb", 0), ("ps", 6)], W=[("ycv", c)])

        chk('proj')
        P.transfer(HTK, ["nm0", "nm1", "nmA0", "nmA1", "nmB0", "nmB1"])
        ctx = [att_idx(g, 0)]
        ga = att_bis(ctx[0], 0.5)
        slh = []

        def conv_job(c):
            if not slh:
                slh.extend([wp_piece(CVO + i * 512, 512) for i in range(3)])
            late_conv(slh, c)
        work = [lambda: late_q(0), lambda: late_q(1)] + [lambda c=c: conv_job(c) for c in range(4)]
        for i in range(ctx[0]["nbis"]):
            next(ga)
            if i % 2 == 1 and work:
                work.pop(0)()
        for _ in ga:
            pass
        while work:
            work.pop(0)()
        for t in range(1, NTG):
            ctx.append(att_idx(g, t))
            ga = att_bis(ctx[t], 0.38)
            gb_ = att_att(ctx[t - 1])
            nb = ctx[t - 1]["qt"] + 1
            done = 0
            head = min(nb, max(1, nb // 8))
            while done < head:
                next(gb_)
                done += 1
            nbi = ctx[t]["nbis"]
            for i in range(nbi):
                next(ga)
                tgt = head + ((i + 1) * (nb - head)) // nbi
                while done < tgt:
                    next(gb_)
                    done += 1
            for _ in ga:
                pass
            for _ in gb_:
                pass
        for _ in att_att(ctx[NTG - 1]):
            pass
        P.transfer(["nm0", "nm1", "nmA0", "nmA1", "nmB0", "nmB1"], HTK)
        P.transfer([("qT", r) for r in range(4)], ["gb", "gb2"])
        chk('att')
        load_gb("ln2g", "ln2b")
        ATT = [("attT", h) for h in range(8)]
        for qd in range(4):
            c0q = qd * 256
            sX = ring_load([(lambda r: r[0:64, 0:2048].rearrange("p (h d) -> p h d", h=8), wsc["wao"][:, c0q:c0q + 256].rearrange("(h p) d -> p h d", p=64), 0),
                            (lambda r: r[:, 2048:3072].rearrange("p (c d) -> p c d", c=4), wsc["wco"][:, c0q:c0q + 256].rearrange("(c p) d -> p c d", p=128), 1)])
            sY = ring_load([(v3(0, 256, KC), wsc["wp"][:, GAO + c0q:GAO + c0q + 256].rearrange("(k p) f -> p k f", p=128), 0),
                            (v3(2048, 256, KC), wsc["wp"][:, GCO + c0q:GCO + c0q + 256].rearrange("(k p) f -> p k f", p=128), 1)])
            wa = ring[sX][0:64, 0:2048].rearrange("p (h d) -> p h d", h=8)
            wc = ring[sX][:, 2048:3072].rearrange("p (c d) -> p c d", c=4)
            wga = v3(0, 256, KC)(ring[sY])
            wgc = v3(2048, 256, KC)(ring[sY])
            for d2 in range(2):
                dc = qd * 2 + d2
                bo = 4 * (dc % 2)
                cs = slice(d2 * 128, (d2 + 1) * 128)
                for h in range(8):
                    P.add("pe", lambda: nc.tensor.matmul(ps[bo][:, 0:G], wa[:, h, cs], attT[:, h, :], start=(h == 0), stop=(h == 7)),
                          R=[("rg", sX, 0)] + ATT, W=[("ps", bo)])
                for c in range(4):
                    P.add("pe", lambda: nc.tensor.matmul(ps[bo + 1][:, 0:G], wc[:, c, cs], ycv[:, c, :], start=(c == 0), stop=(c == 3)),
                          R=[("rg", sX, 1)] + [("ycv", cc) for cc in range(4)], W=[("ps", bo + 1)])
                for kc in range(KC):
                    P.add("pe", lambda: nc.tensor.matmul(ps[bo + 2][:, 0:G], wga[:, kc, cs], aT[:, kc, :], start=(kc == 0), stop=(kc == KC - 1)),
                          R=[("rg", sY, 0)] + ATK, W=[("ps", bo + 2)])
                for kc in range(KC):
                    P.add("pe", lambda: nc.tensor.matmul(ps[bo + 3][:, 0:G], wgc[:, kc, cs], aT[:, kc, :], start=(kc == 0), stop=(kc == KC - 1)),
                          R=[("rg", sY, 1)] + ATK, W=[("ps", bo + 3)])
                i = dc % 2
                P.add("act", lambda: nc.scalar.activation(tmpa[i][:], ps[bo + 2][:, 0:G], AF.Sigmoid), R=[("ps", bo + 2)], W=[("tmpa", i)])
                P.add("act", lambda: nc.scalar.activation(tmpb[i][:], ps[bo + 3][:, 0:G], AF.Sigmoid), R=[("ps", bo + 3)], W=[("tmpb", i)])
                P.add("dve", lambda: nc.vector.tensor_tensor(tmpa[i][:], tmpa[i][:], ps[bo][:, 0:G], ALU.mult), R=[("tmpa", i), ("ps", bo)], W=[("tmpa", i)])
                P.add("dve", lambda: nc.vector.tensor_tensor(tmpb[i][:], tmpb[i][:], ps[bo + 1][:, 0:G], ALU.mult), R=[("tmpb", i), ("ps", bo + 1)], W=[("tmpb", i)])
                P.add("dve", lambda: nc.vector.tensor_tensor(hT[:, dc, :], tmpa[i][:], tmpb[i][:], ALU.add), R=[("tmpa", i), ("tmpb", i)], W=[("hT", dc)])
        for dh in range(2):
            s = ring_load([(v3(0, 512, KC), wsc["wo"][:, dh * 512:(dh + 1) * 512].rearrange("(k p) f -> p k f", p=128), (0, 1))])
            wv = v3(0, 512, KC)(ring[s])
            for t in range(NTG):
                b = (dh * NTG + t) % 8
                for kc in range(KC):
                    P.add("pe", lambda kc=kc, t=t, b=b, wv=wv: nc.tensor.matmul(ps[b][0:T, :], hT[:, kc, t * T:(t + 1) * T], wv[:, kc, :], start=(kc == 0), stop=(kc == KC - 1)),
                          R=[("rg", s, 0), ("rg", s, 1)] + [("hT", k) for k in range(8)], W=[("ps", b)])
                P.add("dve", lambda t=t, dh=dh, b=b: nc.vector.scalar_tensor_tensor(S[:, t, dh * 512:(dh + 1) * 512], S[:, t, dh * 512:(dh + 1) * 512], ALPHA, ps[b][0:T, :], ALU.mult, ALU.add),
                      R=[("ps", b), ("S", t)], W=[("S", t)])
        chk('merge')
        layernorm_all()
        if debug and "h2" in debug and g == 0:
            for t in range(NTG):
                sp_dma(dbg_d["h2"][t * T:(t + 1) * T, :], S[:, t, :], [("S", t)], [])
        for t in range(NTG):
            to_featmajor(t)
        chk('ln2')
        load_gb("ln3g", "ln3b")
        ffn(wsc["f2g"], wsc["f2u"], wsc["f2d"], 0.5)

        def store(t):
            pos0 = p0 + t * T
            if pos0 == 0:
                st_dma(out_d[0:T - NMETA, :], S[NMETA:T, t, :], [("S", t)], [])
            else:
                st_dma(out_d[pos0 - NMETA:pos0 - NMETA + T, :], S[:, t, :], [("S", t)], [])
        layernorm_all(after=store)

    def att_idx(g, t):
        qt = g * NTG + t
        q0 = qt * T
        nk = q0 + T
        tc = slice(t * T, (t + 1) * T)
        for h in range(8):
            P.add("pool", lambda h=h: nc.gpsimd.tensor_scalar(dg[0:T, h, :], identb[0:T, 0:T], wabs[:, t, h:h + 1], None, ALU.mult),
                  R=["identb", ("wabs", t)], W=[("dg", h)])
        nch = (nk + 511) // 512
        items = [(ci, hp) for ci in range(nch) for hp in range(4)]

        def geom(ci):
            c0 = ci * 512
            c1 = min(nk, c0 + 512)
            return c0, c1, c1 - c0

        def X(j):
            ci, hp = items[j]
            c0, c1, w = geom(ci)
            b0 = [0, 4, 6][j % 3]
            ri = j % 3
            KIK = [("kiT", gg) for gg in range(c0 // G, (c1 - 1) // G + 1)]
            for hh in range(2):
                h = 2 * hp + hh
                P.add("pe", lambda: nc.tensor.matmul(ps[b0 + hh][0:T, 0:w], qiz[:, h, tc], kiT[:, c0:c1], start=True, stop=True),
                      R=[("qiT", hp)] + KIK, W=[("ps", b0 + hh)])
            if j % 8 in (1, 3, 4, 6):
                P.add("dve", lambda: nc.vector.tensor_scalar(rr[ri][0:T, :, 0:w], psall[0:T, b0:b0 + 2, 0:w], 0.0, None, ALU.max),
                      R=[("ps", b0), ("ps", b0 + 1)], W=[("rr", ri)])
            else:
                P.add("act", lambda: nc.scalar.activation(rr[ri][0:T, :, 0:w], psall[0:T, b0:b0 + 2, 0:w], AF.Relu),
                      R=[("ps", b0), ("ps", b0 + 1)], W=[("rr", ri)])

        def A(j):
            ci, hp = items[j]
            c0, c1, w = geom(ci)
            pacc = 2 + ci % 2
            ri = j % 3
            for hh in range(2):
                h = 2 * hp + hh
                P.add("pe", lambda: nc.tensor.matmul(ps[pacc][0:T, 0:w], dg[:, h, :], rr[ri][:, hh, 0:w], start=(h == 0), stop=(h == 7)),
                      R=[("dg", h), ("rr", ri)], W=[("ps", pacc)])
            if hp == 3:
                if ci % 2 == 0:
                    P.add("act", lambda: nc.scalar.copy(scores[0:T, c0:c1], ps[pacc][0:T, 0:w]), R=[("ps", pacc)], W=[("sc", ci)])
                else:
                    P.add("dve", lambda: nc.vector.tensor_copy(scores[0:T, c0:c1], ps[pacc][0:T, 0:w]), R=[("ps", pacc)], W=[("sc", ci)])
        LA = 2
        for j in range(min(LA, len(items))):
            X(j)
        for j in range(len(items)):
            if j + LA < len(items):
                X(j + LA)
            A(j)
        SCR = [("sc", ci) for ci in range(nch)]
        dci = sorted(set([q0 // 512, (nk - 1) // 512]))
        DCK = [("sc", ci) for ci in dci]
        blk = q0 >= 1024
        if not blk:
            P.add("dve", lambda: nc.vector.tensor_tensor(t114[:], scores[0:T, q0:nk], trip[:], ALU.add), R=DCK + ["trip"], W=["t114"])
            P.add("dve", lambda: nc.vector.tensor_reduce(m1[:], t114[:], AX.X, ALU.min), R=["t114"], W=["m1"])
            if q0 > 0:
                P.add("dve", lambda: nc.vector.tensor_reduce(m0[:], scores[0:T, 0:q0], AX.X, ALU.min), R=SCR, W=["m0"])
                P.add("dve", lambda: nc.vector.tensor_tensor(lo[:], m0[:], m1[:], ALU.min), R=["m0", "m1"], W=["lo"])
            else:
                P.add("dve", lambda: nc.vector.tensor_copy(lo[:], m1[:]), R=["m1"], W=["lo"])
            P.add("dve", lambda: nc.vector.tensor_tensor(scores[0:T, q0:nk], scores[0:T, q0:nk], trin[:], ALU.add), R=DCK + ["trin", "t114"], W=DCK)
            P.add("dve", lambda: nc.vector.tensor_reduce(rmax[:], scores[0:T, 0:nk], AX.X, ALU.max), R=SCR, W=["rmax"])
        else:
            bs = q0 // 32
            P.add("dve", lambda: nc.vector.tensor_tensor(scores[0:T, q0:nk], scores[0:T, q0:nk], trin[:], ALU.add), R=DCK + ["trin"], W=DCK)
            for bk in range(32):
                P.add("dve", lambda bk=bk: nc.vector.max(out=mx[:, bk * 8:(bk + 1) * 8], in_=scores[0:T, bk * bs:(bk + 1) * bs]), R=SCR, W=[("mx", bk), "zt", "zt0"] if bk == 0 else [("mx", bk)])
            MXK = [("mx", bk) for bk in range(32)]
            P.add("dve", lambda: nc.vector.tensor_reduce(m8t, mx.rearrange("p (b e) -> p b e", e=8), AX.X, ALU.min), R=MXK + ["zt", "zt0"], W=["m8t"])
            P.add("dve", lambda: nc.vector.tensor_reduce(lo[:], m8t, AX.X, ALU.min), R=["m8t", "zt", "zt0"], W=["lo"])
            P.add("dve", lambda: nc.vector.tensor_reduce(m0[:], m8t, AX.X, ALU.max), R=["m8t", "zt", "zt0"], W=["m0"])
            P.add("dve", lambda: nc.vector.tensor_reduce(m1[:], scores[0:T, 32 * bs:nk], AX.X, ALU.max), R=SCR, W=["m1"])
            P.add("dve", lambda: nc.vector.tensor_tensor(rmax[:], m0[:], m1[:], ALU.max), R=["m0", "m1"], W=["rmax"])
        P.add("dve", lambda: nc.vector.tensor_tensor(rng[:], rmax[:], lo[:], ALU.subtract), R=["rmax", "lo"], W=["rng"])
        P.add("dve", lambda: nc.vector.tensor_scalar(steps[:], pw2[:], rng[:], None, ALU.mult), R=["pw2", "rng"], W=["steps"])
        return dict(g=g, t=t, qt=qt, q0=q0, nk=nk, tc=tc, SCR=SCR, nbis=(NBIS_BLK if blk else NBIS), MXK=(MXK if blk else []))


    def att_bis(c, act_share):
        qt, nk, SCR = c["qt"], c["nk"], c["SCR"]
        bi = qt % 2
        nmb = nm8[bi]
        kA, kB, kM = "nmA%d" % bi, "nmB%d" % bi, "nm%d" % bi
        split = nk >= 1400
        ca = (int(nk * (1.0 - act_share)) // 2) * 2 if split else nk
        if split:
            nact = nk - ca
            P.add("dve", lambda: nc.vector.tensor_scalar(kadj[:], kcnt[:, qt:qt + 1], float(-0.5 - 0.5 * nact), None, ALU.add), R=["kcnt"], W=["kadj"])
        for it in range(c["nbis"]):
            P.add("dve", lambda it=it: nc.vector.tensor_tensor(mid[:], lo[:], steps[:, it:it + 1], ALU.add), R=["lo", "steps"], W=["mid"])
            if split:
                P.add("dve", lambda it=it: nc.vector.tensor_scalar(negmid[:], lo[:], steps[:, it:it + 1], -1.0, ALU.add, ALU.mult), R=["lo", "steps"], W=["negmid"])
                P.add("act", lambda: nc.scalar.activation(nmb[0:T, ca:nk], scores[0:T, ca:nk], AF.Sign, bias=negmid[:], scale=1.0, accum_out=cnta[:], saturate=False),
                      R=SCR + ["negmid"], W=[kB, "cnta"])
            P.add("dve", lambda: nc.vector.tensor_scalar(nmb[0:T, 0:ca], scores[0:T, 0:ca], mid[:], None, ALU.is_ge, ALU.add, accum_out=cntc[:], saturate=False),
                  R=SCR + ["mid"], W=[kA, "cntc"])
            if split:
                P.add("dve", lambda: nc.vector.scalar_tensor_tensor(ctmp[:], cnta[:], 0.5, cntc[:], ALU.mult, ALU.add), R=["cnta", "cntc"], W=["ctmp"])
                P.add("dve", lambda: nc.vector.tensor_tensor(gec[:], ctmp[:], kadj[:], ALU.is_ge), R=["ctmp", "kadj"], W=["gec"])
            else:
                P.add("dve", lambda: nc.vector.tensor_tensor(gec[:], cntc[:], kcnt[:, qt:qt + 1], ALU.is_ge), R=["cntc", "kcnt"], W=["gec"])
            P.add("dve", lambda it=it: nc.vector.scalar_tensor_tensor(lo[:], gec[:], steps[:, it:it + 1], lo[:], ALU.mult, ALU.add), R=["gec", "steps", "lo"], W=["lo"])
            yield
        P.add("dve", lambda: nc.vector.tensor_scalar(nmb[:, 0:nk], scores[:, 0:nk], lo128[:], None, ALU.is_lt, saturate=False), R=SCR + ["lo"], W=[kM, kA, kB])
        if debug and "sc" in debug and qt == debug.get("_qt", 0):
            sp_dma(dbg_d["sc"][:, 0:nk], scores[0:T, 0:nk], SCR, [])
            sp_dma(dbg_d["lo"][:, :], lo[:], ["lo"], [])

    def att_att(c):
        qt, t, tc = c["qt"], c["t"], c["tc"]
        bi = qt % 2
        nmb = nm8[bi]
        kM = "nm%d" % bi
        QK = [("qT", r) for r in range(4)]

        def QKm(kt):
            k0 = kt * T
            sset = kt % 2
            b0 = [0, 4][sset]
            for gq in range(2):
                b = b0 + gq
                P.add("pe", lambda: nc.tensor.matmul(ps[b][0:T, 0:G], kT[:, k0:k0 + T], qz[:, gq, :, tc], start=True, stop=False),
                      R=[("kT", k0 // G)] + QK, W=[("ps", b)])
                P.add("pe", lambda: nc.tensor.matmul(ps[b][0:T, 0:G], nmb[:, k0:k0 + T], I4[:], start=False, stop=True),
                      R=[kM, "I4"], W=[("ps", b)])
            P.add("act", lambda: nc.scalar.activation(ee[sset][0:T, :, :], psall[0:T, b0:b0 + 2, 0:G], AF.Exp, scale=0.125),
                  R=[("ps", b0), ("ps", b0 + 1)], W=[("ee", sset)])

        def PVm(kt):
            sset = kt % 2
            for gq in range(2):
                P.add("pe", lambda: nc.tensor.matmul(ps[6 + gq][0:65, 0:G], Vc[:, kt, gq * 65:(gq + 1) * 65], ee[sset][:, gq, :], start=(kt == 0), stop=(kt == qt)),
                      R=[("V", kt), ("ee", sset)], W=[("ps", 6 + gq)])
        QKm(0)
        for kt in range(qt + 1):
            if kt + 1 <= qt:
                QKm(kt + 1)
            PVm(kt)
            yield
        for gq in range(2):
            P.add("dve", lambda: nc.vector.reciprocal(rden[64:65, :], ps[6 + gq][64:65, 0:G]), R=[("ps", 6 + gq)], W=["rden"])
            P.add("pe", lambda: nc.tensor.matmul(ps[2][0:64, 0:G], ones32[64:65, 0:64], rden[64:65, :], start=True, stop=True),
                  R=["rden", "ones32"], W=[("ps", 2)])
            P.add("act", lambda: nc.scalar.copy(bcs[:], ps[2][0:64, 0:G]), R=[("ps", 2)], W=[("tmpb", 1)])
            P.add("dve", lambda: nc.vector.tensor_tensor(attT[:, gq * 4:(gq + 1) * 4, tc], ps[6 + gq][0:64, 0:G].rearrange("p (r t) -> p r t", r=4), bcs[:].rearrange("p (r t) -> p r t", r=4), ALU.mult),
                  R=[("ps", 6 + gq), ("tmpb", 1)], W=[("attT", gq * 4 + r) for r in range(4)])

    try:
        main()
        for g in range(NG):
            group(g)
    except _Stop:
        pass
    P.emit(stack)
    return nc, stack, P


def host_consts():
    half = 32
    inv_freq = (np.float32(10000.0) ** (-np.arange(half, dtype=np.float32) / np.float32(half))).astype(np.float32)
    pos = np.arange(L, dtype=np.float32)
    ang = (pos[:, None] * inv_freq[None, :]).astype(np.float32).astype(np.float64)
    cos = np.cos(ang).astype(np.float32)
    sin = np.sin(ang).astype(np.float32)
    cosT = np.zeros((128, L), np.float32)
    sinT = np.zeros((128, L), np.float32)
    for p in range(128):
        d = p % 64
        cosT[p] = cos[:, d % 32]
        sinT[p] = (-sin[:, d % 32]) if d < 32 else sin[:, d % 32]
    r = np.arange(T)
    trin = np.where(r[None, :] <= r[:, None], 0.0, -1e30).astype(np.float32)
    trip = np.where(r[None, :] <= r[:, None], 0.0, 1e30).astype(np.float32)
    posq = (np.arange(NTILE)[None, :] * T + r[:, None])
    kcnt = np.minimum(256, posq + 1).astype(np.float32)
    pw2 = np.tile((2.0 ** -(np.arange(NBIS) + 1.0)).astype(np.float32)[None, :], (T, 1))
    return dict(cosT=cosT, sinT=sinT, ident=np.eye(128, dtype=np.float32), trin=trin, trip=trip,
                kcnt=np.ascontiguousarray(kcnt), pw2=np.ascontiguousarray(pw2))


def make_in_maps(inputs, cores):
    c = host_consts()
    f = lambda a: np.ascontiguousarray(np.asarray(a, dtype=np.float32))
    shared = dict(
        meta=f(inputs["meta_tokens"]),
        f1g=f(inputs["ffn1_w_gate"][0]), f1u=f(inputs["ffn1_w_up"][0]), f1d=f(inputs["ffn1_w_down"][0]),
        f2g=f(inputs["ffn2_w_gate"][0]), f2u=f(inputs["ffn2_w_up"][0]), f2d=f(inputs["ffn2_w_down"][0]),
        win=f(inputs["w_in"][0]), wao=f(inputs["w_att_out"][0]), wco=f(inputs["w_conv_out"][0]), wo=f(inputs["w_o"][0]),
        ln1g=f(inputs["ln1_g"]), ln1b=f(inputs["ln1_b"]), ln2g=f(inputs["ln2_g"]), ln2b=f(inputs["ln2_b"]),
        ln3g=f(inputs["ln3_g"]), ln3b=f(inputs["ln3_b"]),
        convw=f(np.asarray(inputs["conv_w"])[0].reshape(3, 4, 128).transpose(2, 1, 0).reshape(128, 12)),
        **c)
    maps = []
    for b in cores:
        m = dict(shared)
        m["x"] = f(inputs["x"][b])
        maps.append(m)
    return maps


def kernel(**inputs):
    nc, stack, P = build(NG_FULL)
    cores = list(range(8))
    in_maps = make_in_maps(inputs, cores)
    with stack:
        res = run_bass_kernel_spmd(nc, in_maps, core_ids=cores)
    out = np.stack([np.asarray(r["out"], dtype=np.float32) for r in res.results], axis=0)
    return out
```
